# Optimizing a Trainium2 kernel written in Bass

```python
import math
import jax, jax.numpy as jnp
from jax import lax
import numpy as np

D_MODEL = 1024
BATCH = 4
SEQ = 4096
DEPTH = 2

HEAD_DIM = 64
A_HEADS = 6
A_WIDTH = A_HEADS * HEAD_DIM
GMLP_CHUNK = 128
POOL_WINDOWS = (2, 4, 8, 16)
B_GROUPS = len(POOL_WINDOWS)
B_GROUP_DIM = 64
B_WIDTH = B_GROUPS * B_GROUP_DIM
DILATED_CONFIGS = ((128, 1), (512, 4), (2048, 16))
C_HEADS_PER_GROUP = 2
C_GROUPS = len(DILATED_CONFIGS)
C_HEADS = C_GROUPS * C_HEADS_PER_GROUP
C_WIDTH = C_HEADS * HEAD_DIM
C_OUT_WIDTH = C_HEADS_PER_GROUP * HEAD_DIM
MIX_WIDTH = A_WIDTH + B_WIDTH + C_WIDTH
IN_WIDTH = 2 * A_WIDTH + B_WIDTH + 3 * C_WIDTH
OUT_WIDTH = A_WIDTH + B_WIDTH + C_OUT_WIDTH
N_BUCKETS = 32
MAX_DISTANCE = 1024
MEM_LEN = 256
X_HEADS = 4
X_HEAD_DIM = D_MODEL // X_HEADS
D_FF = 128 * ((8 * D_MODEL // 3 + 127) // 128)
CONV_WIDTH = 3
EPS = 1e-6
NEG_INF = -1e30

kernel_name = "hybrid_gmlp_pool_dilated_encoder"


def _rmsnorm(x, g):
    x32 = x.astype(jnp.float32)
    y = x32 * lax.rsqrt(jnp.mean(x32 * x32, axis=-1, keepdims=True) + EPS)
    return (y * g.astype(jnp.float32)).astype(x.dtype)


def _t5_bucket(rel):
    nb = N_BUCKETS // 2
    ret = (rel > 0).astype(np.int32) * nb
    n = np.abs(rel)
    max_exact = nb // 2
    large = max_exact + (np.log(np.maximum(n, 1) / max_exact)
                         / math.log(MAX_DISTANCE / max_exact) * (nb - max_exact)).astype(np.int32)
    large = np.minimum(large, nb - 1)
    return ret + np.where(n < max_exact, n, large)


def _spatial_gating(z_uv, v_gain, w_s, b_s):
    B, S, _ = z_uv.shape
    z = jax.nn.gelu(z_uv)
    u, v = jnp.split(z, 2, axis=-1)
    v = _rmsnorm(v.reshape(B, S, A_HEADS, HEAD_DIM), v_gain)
    vc = v.reshape(B, S // GMLP_CHUNK, GMLP_CHUNK, A_HEADS, HEAD_DIM)
    s = jnp.einsum('hpq,bnqhe->bnphe', w_s, vc) + b_s.T[None, None, :, :, None]
    return (u.reshape(B, S, A_HEADS, HEAD_DIM) * s.reshape(B, S, A_HEADS, HEAD_DIM)).reshape(B, S, A_WIDTH)


def _multiscale_pool(z, w_pool, b_pool, pool_scale):
    B, S, _ = z.shape
    cs = jnp.pad(jnp.cumsum(z.astype(jnp.float32), axis=1), ((0, 0), (1, 0), (0, 0)))
    pos = np.arange(S)
    outs = []
    for g, w in enumerate(POOL_WINDOWS):
        lo = np.clip(pos - w // 2, 0, S)
        hi = np.clip(pos + w // 2, 0, S)
        sl = slice(g * B_GROUP_DIM, (g + 1) * B_GROUP_DIM)
        cnt = (hi - lo).astype(np.float32)[None, :, None]
        outs.append((cs[:, hi, sl] - cs[:, lo, sl]) / cnt)
    pooled = jnp.concatenate(outs, axis=-1).astype(z.dtype) - z
    pooled = pooled.reshape(B, S, B_GROUPS, B_GROUP_DIM)
    y = jnp.einsum('bsge,gef->bsgf', pooled, w_pool) + b_pool
    return y.reshape(B, S, B_WIDTH) * pool_scale


def _dilated_window_attention(q, k, v, rel_table_g, dilation, radius):
    B, S, H, E = q.shape
    L = S // dilation
    C = radius
    n_blk = -(-L // C)
    Lp = n_blk * C

    def to_sub(t):
        t = t.reshape(B, L, dilation, H, E).transpose(0, 2, 1, 3, 4)
        return jnp.pad(t, ((0, 0), (0, 0), (0, Lp - L), (0, 0), (0, 0)))

    def band(t):
        t = jnp.pad(to_sub(t), ((0, 0), (0, 0), (C, C), (0, 0), (0, 0))).reshape(B, dilation, n_blk + 2, C, H, E)
        return jnp.concatenate([t[:, :, :-2], t[:, :, 1:-1], t[:, :, 2:]], axis=3)

    qs = to_sub(q).reshape(B, dilation, n_blk, C, H, E)
    kb, vb = band(k), band(v)
    delta = (np.arange(3 * C)[None, :] - C) - np.arange(C)[:, None]
    bias = rel_table_g[_t5_bucket(delta * dilation)].transpose(2, 0, 1)
    key_sub = np.arange(n_blk)[:, None] * C + np.arange(3 * C)[None, :] - C
    valid = ((np.abs(delta) <= radius)[None]
             & ((key_sub >= 0) & (key_sub < L))[:, None, :])
    logits = jnp.einsum('bdnqhe,bdnkhe->bdnhqk', qs, kb).astype(jnp.float32) * (E ** -0.5)
    logits = logits + bias[None, None, None].astype(jnp.float32)
    logits = jnp.where(valid[None, None, :, None], logits, NEG_INF)
    lse = jax.nn.logsumexp(logits, axis=-1)
    p = jnp.exp(logits - lse[..., None]).astype(v.dtype)
    out = jnp.einsum('bdnhqk,bdnkhe->bdnqhe', p, vb).reshape(B, dilation, Lp, H, E)[:, :, :L]
    out = out.transpose(0, 2, 1, 3, 4).reshape(B, S, H, E)
    lse = lse.transpose(0, 1, 2, 4, 3).reshape(B, dilation, Lp, H)[:, :, :L]
    lse = lse.transpose(0, 2, 1, 3).reshape(B, S, H)
    return out, lse


def _hybrid_mixer(h, rel_table, w_in, b_in, v_gain, w_s, b_s, w_pool, b_pool, pool_scale, w_out, b_out):
    B, S, _ = h.shape
    z = h @ w_in + b_in
    za = z[..., :2 * A_WIDTH]
    zb = z[..., 2 * A_WIDTH:2 * A_WIDTH + B_WIDTH]
    zc = z[..., 2 * A_WIDTH + B_WIDTH:]
    ya = _spatial_gating(za, v_gain, w_s, b_s)
    yb = _multiscale_pool(zb, w_pool, b_pool, pool_scale)
    q, k, v = [t.reshape(B, S, C_GROUPS, C_HEADS_PER_GROUP, HEAD_DIM) for t in jnp.split(zc, 3, axis=-1)]
    outs, lses = [], []
    for g, (window, dil) in enumerate(DILATED_CONFIGS):
        o, l = _dilated_window_attention(q[:, :, g], k[:, :, g], v[:, :, g],
                                         rel_table[:, g * C_HEADS_PER_GROUP:(g + 1) * C_HEADS_PER_GROUP],
                                         dil, window // (2 * dil))
        outs.append(o)
        lses.append(l)
    weights = jax.nn.softmax(jnp.stack(lses, axis=0), axis=0)
    yc = jnp.einsum('gbsh,gbshe->bshe', weights, jnp.stack(outs, axis=0).astype(jnp.float32))
    yc = yc.astype(h.dtype).reshape(B, S, C_OUT_WIDTH)
    y = jnp.concatenate([ya, yb, yc], axis=-1)
    return y @ w_out + b_out


def _memory_cross_attention(h, mem_n, w_q, w_kv, w_o, b_o):
    B, S, _ = h.shape
    M = mem_n.shape[1]
    q = (h @ w_q).reshape(B, S, X_HEADS, X_HEAD_DIM)
    k, v = [t.reshape(B, M, X_HEADS, X_HEAD_DIM) for t in jnp.split(mem_n @ w_kv, 2, axis=-1)]
    logits = jnp.einsum('bshe,bmhe->bhsm', q, k).astype(jnp.float32) * (X_HEAD_DIM ** -0.5)
    p = jax.nn.softmax(logits, axis=-1).astype(v.dtype)
    o = jnp.einsum('bhsm,bmhe->bshe', p, v).reshape(B, S, D_MODEL)
    return o @ w_o + b_o


def _conv_ffn(h, w_up, b_up, conv_w, conv_b, w_down, b_down):
    S = h.shape[1]
    u = h @ w_up + b_up
    up = jnp.pad(u, ((0, 0), (1, 1), (0, 0)))
    u = conv_w[0] * up[:, :S] + conv_w[1] * up[:, 1:S + 1] + conv_w[2] * up[:, 2:] + conv_b
    gate, val = jnp.split(u, 2, axis=-1)
    return (jax.nn.silu(gate) * val) @ w_down + b_down


def setup_inputs(seed: int = 0) -> dict:
    key = jax.random.key(seed)
    ks = iter(jax.random.split(key, 40))
    f32 = jnp.float32

    def nrm(shape, scale):
        return jax.random.normal(next(ks), shape, f32) * scale

    L = DEPTH
    return {
        "x": nrm((BATCH, SEQ, D_MODEL), 1.0),
        "mem": nrm((BATCH, MEM_LEN, D_MODEL), 1.0),
        "rel_table": nrm((N_BUCKETS, C_HEADS), 0.5),
        "mem_norm_g": 1.0 + nrm((D_MODEL,), 0.02),
        "norm_mix_g": 1.0 + nrm((L, D_MODEL), 0.02),
        "w_in": nrm((L, D_MODEL, IN_WIDTH), D_MODEL ** -0.5),
        "b_in": nrm((L, IN_WIDTH), 0.02),
        "gmlp_v_g": 1.0 + nrm((L, A_HEADS, HEAD_DIM), 0.02),
        "gmlp_w_s": nrm((L, A_HEADS, GMLP_CHUNK, GMLP_CHUNK), GMLP_CHUNK ** -0.5),
        "gmlp_b_s": 1.0 + nrm((L, A_HEADS, GMLP_CHUNK), 0.02),
        "pool_w": nrm((L, B_GROUPS, B_GROUP_DIM, B_GROUP_DIM), B_GROUP_DIM ** -0.5),
        "pool_b": nrm((L, B_GROUPS, B_GROUP_DIM), 0.02),
        "pool_scale": 1.0 + nrm((L, B_WIDTH), 0.1),
        "w_out": nrm((L, OUT_WIDTH, D_MODEL), OUT_WIDTH ** -0.5),
        "b_out": nrm((L, D_MODEL), 0.02),
        "norm_mem_g": 1.0 + nrm((L, D_MODEL), 0.02),
        "xattn_w_q": nrm((L, D_MODEL, D_MODEL), D_MODEL ** -0.5),
        "xattn_w_kv": nrm((L, D_MODEL, 2 * D_MODEL), D_MODEL ** -0.5),
        "xattn_w_o": nrm((L, D_MODEL, D_MODEL), D_MODEL ** -0.5),
        "xattn_b_o": nrm((L, D_MODEL), 0.02),
        "norm_ffn_g": 1.0 + nrm((L, D_MODEL), 0.02),
        "ffn_w_up": nrm((L, D_MODEL, 2 * D_FF), D_MODEL ** -0.5),
        "ffn_b_up": nrm((L, 2 * D_FF), 0.02),
        "ffn_conv_w": jnp.array([0.25, 0.5, 0.25], f32)[None, :, None] + nrm((L, CONV_WIDTH, 2 * D_FF), 0.1),
        "ffn_conv_b": nrm((L, 2 * D_FF), 0.02),
        "ffn_w_down": nrm((L, D_FF, D_MODEL), D_FF ** -0.5),
        "ffn_b_down": nrm((L, D_MODEL), 0.02),
        "final_norm_g": 1.0 + nrm((D_MODEL,), 0.02),
    }


def reference(x, mem, rel_table, mem_norm_g, norm_mix_g, w_in, b_in, gmlp_v_g, gmlp_w_s, gmlp_b_s,
              pool_w, pool_b, pool_scale, w_out, b_out, norm_mem_g, xattn_w_q, xattn_w_kv, xattn_w_o,
              xattn_b_o, norm_ffn_g, ffn_w_up, ffn_b_up, ffn_conv_w, ffn_conv_b, ffn_w_down, ffn_b_down,
              final_norm_g):
    mem_n = _rmsnorm(mem, mem_norm_g)
    for l in range(DEPTH):
        h = _rmsnorm(x, norm_mix_g[l])
        x = x + _hybrid_mixer(h, rel_table, w_in[l], b_in[l], gmlp_v_g[l], gmlp_w_s[l], gmlp_b_s[l],
                              pool_w[l], pool_b[l], pool_scale[l], w_out[l], b_out[l])
        h = _rmsnorm(x, norm_mem_g[l])
        x = x + _memory_cross_attention(h, mem_n, xattn_w_q[l], xattn_w_kv[l], xattn_w_o[l], xattn_b_o[l])
        h = _rmsnorm(x, norm_ffn_g[l])
        x = x + _conv_ffn(h, ffn_w_up[l], ffn_b_up[l], ffn_conv_w[l], ffn_conv_b[l], ffn_w_down[l], ffn_b_down[l])
    return _rmsnorm(x, final_norm_g)
```

```python
import contextlib
import math
import numpy as np
import concourse.bass as bass
import concourse.mybir as mybir
from concourse.bass_utils import run_bass_kernel_spmd

F32 = mybir.dt.float32
BF16 = mybir.dt.bfloat16
AF = mybir.ActivationFunctionType
ALU = mybir.AluOpType

D = 1024
SEQ = 4096
BATCH = 4
T = 2048
NBLK = 4
BLK = 512
KC = 8
DFF = 2816
NFF = 22
EPS = 1e-6
IN_W = 2176
NEG = -1e30


_KEY = [0]


def new_key():
    _KEY[0] += 1
    return _KEY[0]


class EngQ:
    def __init__(self, cx, eng, name):
        self.key = new_key()
        self.name = name
        self.e = eng
        self.sem = cx.sem("q_" + name)
        self.n = 0
        self.seen = {}

    def wait(self, *evs):
        for ev in evs:
            if ev is None:
                continue
            if isinstance(ev, list):
                self.wait(*ev)
                continue
            sem, val, key = ev
            if self.seen.get(key, 0) >= val:
                continue
            self.seen[key] = val
            self.e.wait_ge(sem, val)

    def tick(self, ins):
        self.n += 1
        ins.then_inc(self.sem, 1)
        return (self.sem, self.n, self.key)


class DSlot:
    def __init__(self, cx, name):
        self.key = new_key()
        self.sem = cx.sem("d_" + name)
        self.n = 0

    def start(self, q, out, in_):
        assert (q.name == "pool") == bool(getattr(self, "sw", False)), "gpsimd DMAs need sw=True semaphores (and only them)"
        q.e.dma_start(out=out, in_=in_).then_inc(self.sem, 16)
        self.n += 16
        return (self.sem, self.n, self.key)


class Cx:
    def __init__(self, nc):
        self.nc = nc
        self.root = contextlib.ExitStack()
        self.st = self.root
        self._nsem = 0
        self._nsb = 0
        self.free_slots = []
        self.sw_slots = []
        self.scope_slots = [[]]
        self.scope_sw = [[]]
        self.banks8 = [self.root.enter_context(nc.psum_tensor("bank%d" % i, [128, 512], F32)) for i in range(8)]
        self.pe = EngQ(self, nc.tensor, "pe")
        self.dve = EngQ(self, nc.vector, "dve")
        self.act = EngQ(self, nc.scalar, "act")
        self.pool = EngQ(self, nc.gpsimd, "pool")
        self.sp = EngQ(self, nc.sync, "sp")

    def sem(self, name):
        self._nsem += 1
        return self.root.enter_context(self.nc.semaphore("%s_%d" % (name, self._nsem)))

    def dslot(self, name, sw=False):
        if sw:
            d = DSlot(self, name)
            d.sw = True
            self.sw_slots.append(d)
            self.scope_sw[-1].append(d)
            return d
        if self.free_slots:
            d = self.free_slots.pop()
        else:
            d = DSlot(self, name)
        d.sw = False
        self.scope_slots[-1].append(d)
        return d

    def bank(self, i):
        return self.banks8[i]

    def sb(self, name, shape, dt):
        self._nsb += 1
        return self.st.enter_context(self.nc.sbuf_tensor("s%d_%s" % (self._nsb, name), shape, dt))

    def ps(self, name, shape=(128, 512), dt=F32):
        return self.st.enter_context(self.nc.psum_tensor("p_" + name, list(shape), dt))

    def dram_in(self, name, shape, dt=F32):
        return self.nc.dram_tensor(name, list(shape), dt, kind="ExternalInput")

    def dram_out(self, name, shape, dt=F32):
        return self.nc.dram_tensor(name, list(shape), dt, kind="ExternalOutput")

    def close(self):
        self.root.close()


class Ring:
    def __init__(self, bufs):
        self.bufs = bufs
        self.rel = [None] * len(bufs)
        self.i = 0

    def acquire(self, q):
        k = self.i % len(self.bufs)
        self.i += 1
        q.wait(self.rel[k])
        self.rel[k] = None
        return k, self.bufs[k]

    def release(self, k, ev):
        if self.rel[k] is None:
            self.rel[k] = [ev]
        else:
            self.rel[k].append(ev)


def small_loads(cx, pairs):
    ds = cx.dslot("small%d" % cx._nsem)
    ev = None
    for o, i in pairs:
        ev = ds.start(cx.sp, o, i)
    return ev


class Normer:
    def __init__(self, cx, ps_bank, tag, nmax=BLK):
        self.cx = cx
        nc = cx.nc
        self.ones = cx.sb("ones_" + tag, [128, 128], BF16)
        self.sq = Ring([cx.sb("sq%d_%s" % (i, tag), [128, KC, nmax], BF16) for i in range(2)])
        self.rt = Ring([cx.sb("rt%d_%s" % (i, tag), [128, nmax], F32) for i in range(2)])
        self.bank = ps_bank
        self.bank_rel = None
        self.ones_ev = cx.pool.tick(nc.gpsimd.memset(self.ones[:], 1.0))

    def run(self, xs, g, outs, n, x_ev=None, out_wait=None):
        cx = self.cx
        nc = cx.nc
        ks, sq = self.sq.acquire(cx.act)
        cx.act.wait(x_ev)
        ev = None
        for c in range(KC):
            ev = nc.scalar.activation(out=sq[:, c, 0:n], in_=xs(c), func=AF.Square)
        sq_ev = cx.act.tick(ev)
        cx.pe.wait(sq_ev, self.ones_ev, self.bank_rel)
        for c in range(KC):
            mm = nc.tensor.matmul(self.bank[:, 0:n], self.ones[:], sq[:, c, 0:n], start=(c == 0), stop=(c == KC - 1))
        ss_ev = cx.pe.tick(mm)
        self.sq.release(ks, ss_ev)
        kr, rt = self.rt.acquire(cx.act)
        cx.act.wait(ss_ev)
        rt_ev = cx.act.tick(nc.scalar.activation(out=rt[:, 0:n], in_=self.bank[:, 0:n], func=AF.Sqrt,
                                                 bias=self.epsb[:, 0:1], scale=1.0 / D))
        self.bank_rel = rt_ev
        cx.dve.wait(rt_ev, x_ev, out_wait)
        r_ev = cx.dve.tick(nc.vector.reciprocal(out=rt[:, 0:n], in_=rt[:, 0:n]))
        cx.dve.wait(r_ev)
        for c in range(KC):
            o = nc.vector.scalar_tensor_tensor(out=outs(c), in0=xs(c), scalar=g[:, c:c + 1], in1=rt[:, 0:n],
                                               op0=ALU.mult, op1=ALU.mult)
        o_ev = cx.dve.tick(o)
        self.rt.release(kr, o_ev)
        return o_ev


def make_epsb(cx, normer):
    normer.epsb = cx.sb("epsb_%d" % cx._nsem, [128, 1], F32)
    ev = cx.pool.tick(cx.nc.gpsimd.memset(normer.epsb[:], EPS))
    cx.act.wait(ev)


FF_PARTS = [(0, 6), (6, 12), (12, 17), (17, 22)]


def _emit_F(cx, io, final, xT_res=None, hbuf=None):
    nc = cx.nc
    xT_d = io["xT"]
    fl_d = io["fl"]
    g3_d = io["g3"]
    gf_d = io["gf"]
    wup_d = io["wup"]
    bup_d = io["bup"]
    cw_d = io["cw"]
    cb_d = io["cb"]
    wd_d = io["wd"]
    bd_d = io["bd"]
    out_d = io["out"]

    xT = xT_res if xT_res is not None else cx.sb("xT", [128, KC, T], F32)
    xh = cx.sb("xh", [128, KC, 2], F32)
    hT = hbuf if hbuf is not None else cx.sb("hT", [128, KC, T], BF16)
    hTh = cx.sb("hTh", [128, KC, 2], BF16)
    fl = cx.sb("fl", [128, 2], F32)
    g3 = cx.sb("g3", [128, KC], F32)
    gf = cx.sb("gf", [128, KC], F32)
    bup = cx.sb("bup", [128, 2, NFF], F32)
    cw = cx.sb("cw", [128, 3, 2, NFF], F32)
    cb = cx.sb("cb", [128, 2, NFF], F32)
    bd = cx.sb("bd", [128, KC], F32)
    G = cx.sb("G", [128, 6, T], BF16)
    ubuf = [cx.sb("ubuf%d" % s, [128, T + 2], F32) for s in range(2)]
    cA = [Ring([cx.sb("cA%d_%d" % (s, i), [128, BLK], F32) for i in range(4)]) for s in range(2)]
    sg = Ring([cx.sb("sg%d" % i, [128, BLK], F32) for i in range(2)])
    wu = Ring([cx.sb("wu%d" % i, [128, KC, 2, 128], BF16) for i in range(2)])
    wdb = Ring([cx.sb("wdb%d" % i, [128, 6, 128], BF16) for i in range(2)])
    wu_d = [cx.dslot("wu%d" % i, sw=True) for i in range(2)]
    wd_s = [cx.dslot("wd%d" % i, sw=True) for i in range(2)]
    banks = Ring([cx.bank(i) for i in range(6)])
    dbanks = banks
    pnorm = cx.bank(6)
    phalo = cx.bank(7)

    xT_dv = xT_d.ap().rearrange("(c p) t -> p c t", p=128)
    xs = cx.dslot("x")
    x_evs = []
    for b in range(NBLK):
        if xT_res is not None:
            x_evs.append(None)
        else:
            x_evs.append(xs.start(cx.sp, xT[:, :, b * BLK:(b + 1) * BLK], xT_dv[:, :, b * BLK:(b + 1) * BLK]))
    with nc.allow_non_contiguous_dma(reason="halo columns"):
        xhs = cx.dslot("xh")
        xhs.start(cx.sp, xh[:, :, 0:1], io["xhl"].ap())
        xh_ev = xhs.start(cx.sp, xh[:, :, 1:2], io["xhr"].ap())
    c_ev = small_loads(cx, [(fl[:], fl_d.ap()), (g3[:], g3_d.ap()), (gf[:], gf_d.ap()), (bup[:], bup_d.ap()),
                            (cw[:], cw_d.ap()), (cb[:], cb_d.ap()), (bd[:], bd_d.ap())])
    for q in (cx.act, cx.dve, cx.pool):
        q.wait(c_ev)

    nm = Normer(cx, pnorm, "f")
    make_epsb(cx, nm)
    h_evs = []
    for b in range(NBLK):
        sl = slice(b * BLK, (b + 1) * BLK)
        h_evs.append(nm.run(lambda c: xT[:, c, sl], g3, lambda c: hT[:, c, sl], BLK, x_ev=x_evs[b]))
    cx.act.wait(xh_ev)
    cx.dve.wait(xh_ev)
    hh_ev = nm.run(lambda c: xh[:, c, :], g3, lambda c: hTh[:, c, :], 2, x_ev=xh_ev)

    ubuf_rel = [None, None]
    phalo_rel = None
    G_rel = None
    x_upd = [[None] * NBLK for _ in range(KC)]
    for (k0, k1) in FF_PARTS:
        for j in range(k0, k1):
            kw, wub = wu.acquire(cx.pool)
            w_ev = wu_d[kw].start(cx.pool, wub[:], wup_d.ap()[j])
            cx.pe.wait(w_ev, h_evs, hh_ev)
            conv_out = [None, None]
            for s in range(2):
                cx.pe.wait(phalo_rel)
                for c in range(KC):
                    mm = nc.tensor.matmul(phalo[:, 2 * s:2 * s + 2], wub[:, c, s, :], hTh[:, c, :], start=(c == 0), stop=(c == KC - 1))
                ph_ev = cx.pe.tick(mm)
                cx.act.wait(ubuf_rel[s])
                cx.dve.wait(ubuf_rel[s], ph_ev)
                nc.vector.tensor_scalar(out=ubuf[s][:, 0:1], in0=phalo[:, 2 * s:2 * s + 1], scalar1=bup[:, s, j:j + 1],
                                        scalar2=fl[:, 0:1], op0=ALU.add, op1=ALU.mult)
                hv = nc.vector.tensor_scalar(out=ubuf[s][:, T + 1:T + 2], in0=phalo[:, 2 * s + 1:2 * s + 2], scalar1=bup[:, s, j:j + 1],
                                             scalar2=fl[:, 1:2], op0=ALU.add, op1=ALU.mult)
                halo_ev = cx.dve.tick(hv)
                phalo_rel = halo_ev
                ev_blocks = []
                for b in range(NBLK):
                    kb, bank = banks.acquire(cx.pe)
                    for c in range(KC):
                        mm = nc.tensor.matmul(bank[:, :], wub[:, c, s, :], hT[:, c, b * BLK:(b + 1) * BLK], start=(c == 0), stop=(c == KC - 1))
                    mm_ev = cx.pe.tick(mm)
                    cx.act.wait(mm_ev)
                    e = cx.act.tick(nc.scalar.activation(out=ubuf[s][:, 1 + b * BLK:1 + (b + 1) * BLK], in_=bank[:, :], func=AF.Identity,
                                                         bias=bup[:, s, j:j + 1], scale=1.0))
                    banks.release(kb, e)
                    ev_blocks.append(e)
                if s == 1:
                    wu.release(kw, mm_ev)
                slots = []
                t0s = []
                for b in range(NBLK):
                    ka, ca = cA[s].acquire(cx.act)
                    cx.act.wait(halo_ev, ev_blocks[b])
                    t0s.append(cx.act.tick(nc.scalar.activation(out=ca[:], in_=ubuf[s][:, b * BLK:b * BLK + BLK], func=AF.Copy,
                                                                scale=cw[:, 0, s, j:j + 1])))
                    slots.append((ka, ca))
                t1s = []
                for b in range(NBLK):
                    ka, ca = slots[b]
                    cx.dve.wait(t0s[b], halo_ev)
                    t1s.append(cx.dve.tick(nc.vector.scalar_tensor_tensor(out=ca[:], in0=ubuf[s][:, 1 + b * BLK:1 + b * BLK + BLK],
                                                                          scalar=cw[:, 1, s, j:j + 1], in1=ca[:], op0=ALU.mult, op1=ALU.add)))
                outs = []
                for b in range(NBLK):
                    ka, ca = slots[b]
                    cx.dve.wait(t1s[b], ev_blocks[min(b + 1, NBLK - 1)])
                    t2 = cx.dve.tick(nc.vector.scalar_tensor_tensor(out=ca[:], in0=ubuf[s][:, 2 + b * BLK:2 + b * BLK + BLK],
                                                                    scalar=cw[:, 2, s, j:j + 1], in1=ca[:], op0=ALU.mult, op1=ALU.add))
                    outs.append((ka, ca, t2))
                ubuf_rel[s] = outs[-1][2]
                conv_out[s] = outs
            for b in range(NBLK):
                kag, cag, tg = conv_out[0][b]
                kav, cav, tv = conv_out[1][b]
                ks, sgb = sg.acquire(cx.act)
                cx.act.wait(tg)
                s_ev = cx.act.tick(nc.scalar.activation(out=sgb[:], in_=cag[:], func=AF.Silu, bias=cb[:, 0, j:j + 1], scale=1.0))
                cA[0].release(kag, s_ev)
                cx.dve.wait(s_ev, tv, G_rel)
                g_ev = cx.dve.tick(nc.vector.scalar_tensor_tensor(out=G[:, j - k0, b * BLK:(b + 1) * BLK], in0=cav[:], scalar=cb[:, 1, j:j + 1],
                                                                  in1=sgb[:], op0=ALU.add, op1=ALU.mult))
                cA[1].release(kav, g_ev)
                sg.release(ks, g_ev)
            G_ev = g_ev
        nk = k1 - k0
        for dc in range(KC):
            kd, wb = wdb.acquire(cx.pool)
            w_ev = wd_s[kd].start(cx.pool, wb[:, 0:nk, :], wd_d.ap()[dc, :, k0:k1, :])
            cx.pe.wait(w_ev, G_ev)
            for b in range(NBLK):
                kb, bank = dbanks.acquire(cx.pe)
                for k in range(nk):
                    mm = nc.tensor.matmul(bank[:, :], wb[:, k, :], G[:, k, b * BLK:(b + 1) * BLK], start=(k == 0), stop=(k == nk - 1))
                mm_ev = cx.pe.tick(mm)
                cx.dve.wait(mm_ev, x_upd[dc][b])
                xs_ = xT[:, dc, b * BLK:(b + 1) * BLK]
                if k0 == 0:
                    ins = nc.vector.scalar_tensor_tensor(out=xs_, in0=bank[:, :], scalar=bd[:, dc:dc + 1], in1=xs_, op0=ALU.add, op1=ALU.add)
                else:
                    ins = nc.vector.tensor_tensor(out=xs_, in0=bank[:, :], in1=xs_, op=ALU.add)
                e = cx.dve.tick(ins)
                x_upd[dc][b] = e
                dbanks.release(kb, e)
            wdb.release(kd, mm_ev)
        G_rel = mm_ev

    os_ = cx.dslot("o")
    out_dv = out_d.ap().rearrange("(c p) t -> p c t", p=128)
    o_ev = None
    if final:
        for b in range(NBLK):
            sl = slice(b * BLK, (b + 1) * BLK)
            e = nm.run(lambda c: xT[:, c, sl], gf, lambda c: xT[:, c, sl], BLK, x_ev=[x_upd[c][b] for c in range(KC)])
            cx.sp.wait(e)
            o_ev = os_.start(cx.sp, out_dv[:, :, sl], xT[:, :, sl])
    else:
        for b in range(NBLK):
            sl = slice(b * BLK, (b + 1) * BLK)
            cx.sp.wait([x_upd[c][b] for c in range(KC)])
            o_ev = os_.start(cx.sp, out_dv[:, :, sl], xT[:, :, sl])
    cx.sp.wait(o_ev)
    return


def colvec(v):
    v = np.asarray(v, np.float32)
    return np.ascontiguousarray(v.reshape(-1, 128).T)


def wchunks(w):
    K, N = w.shape
    return np.ascontiguousarray(w.reshape(K // 128, 128, N // 128, 128).transpose(2, 1, 0, 3))


def core_flags():
    out = []
    for cid in range(8):
        s = cid % 2
        f = np.zeros((128, 2), np.float32)
        f[:, 0] = 1.0 if s == 1 else 0.0
        f[:, 1] = 1.0 if s == 0 else 0.0
        out.append(f)
    return out


def prep_F(inp, l):
    wup = inp["ffn_w_up"][l]
    wu = wup.reshape(KC, 128, 2, NFF, 128).transpose(3, 1, 0, 2, 4)
    bup = inp["ffn_b_up"][l].reshape(2, NFF, 128).transpose(2, 0, 1)
    cw = inp["ffn_conv_w"][l].reshape(3, 2, NFF, 128).transpose(3, 0, 1, 2)
    cb = inp["ffn_conv_b"][l].reshape(2, NFF, 128).transpose(2, 0, 1)
    wd = inp["ffn_w_down"][l].reshape(NFF, 128, KC, 128).transpose(2, 1, 0, 3)
    return {
        "g3": colvec(inp["norm_ffn_g"][l]), "gf": colvec(inp["final_norm_g"]),
        "wup": np.ascontiguousarray(wu), "bup": np.ascontiguousarray(bup), "cw": np.ascontiguousarray(cw),
        "cb": np.ascontiguousarray(cb), "wd": np.ascontiguousarray(wd), "bd": colvec(inp["ffn_b_down"][l]),
    }


def proj_fm(cx, w_src, nj, kcin, rhs, evac, wring, wslots, banks, rhs_ev, nblk=NBLK, wshape=None):
    nc = cx.nc
    mm_ev = None
    for j in range(nj):
        kw, wb = wring.acquire(cx.pool)
        w_ev = wslots[kw].start(cx.pool, wb[:, 0:kcin, :], w_src(j))
        cx.pe.wait(w_ev, rhs_ev)
        for b in range(nblk):
            kb, bank = banks.acquire(cx.pe)
            for c in range(kcin):
                mm = nc.tensor.matmul(bank[:, :], wb[:, c, :], rhs(c, b), start=(c == 0), stop=(c == kcin - 1))
            mm_ev = cx.pe.tick(mm)
            banks.release(kb, evac(j, b, bank, mm_ev))
        wring.release(kw, mm_ev)
    return mm_ev


MEM = 256


def _emit_M2(cx, io, xT_res=None, store=True, load=True, hbuf=None):
    nc = cx.nc
    xT_d = io["xT"]
    memT_d = io["memT"]
    gm_d = io["gm"]
    g2_d = io["g2"]
    wq_d = io["wq"]
    wk_d = io["wk"]
    wv_d = io["wv"]
    wo_d = io["wo"]
    bo_d = io["bo"]
    out_d = io["out"]

    xT = xT_res if xT_res is not None else cx.sb("xT", [128, KC, T], F32)
    memT = cx.sb("memT", [128, KC, MEM], F32)
    mnT = cx.sb("mnT", [128, KC, MEM], BF16)
    gm = cx.sb("gm", [128, KC], F32)
    g2 = cx.sb("g2", [128, KC], F32)
    bo = cx.sb("bo", [128, KC], F32)
    hT = hbuf if hbuf is not None else cx.sb("hT", [128, KC, T], BF16)
    QT = cx.sb("QT", [128, KC, T], BF16)
    KxT = cx.sb("KxT", [128, KC, MEM], BF16)
    Vx = cx.sb("Vx", [128, 2, D], BF16)
    wv = cx.sb("wv", [128, KC, D], BF16)
    PT = Ring([cx.sb("PT%d" % i, [128, 2, BLK], BF16) for i in range(2)])
    rc = Ring([cx.sb("rc%d" % i, [128, BLK], F32) for i in range(2)])
    wring = Ring([cx.sb("w%d" % i, [128, KC, 128], BF16) for i in range(2)])
    wslots = [cx.dslot("w%d" % i, sw=True) for i in range(2)]
    banks = Ring([cx.bank(i) for i in range(4)])
    pden = Ring([cx.bank(4)])
    pov = Ring([cx.bank(5), cx.bank(6)])
    pnorm = cx.bank(7)

    xT_dv = xT_d.ap().rearrange("(c p) t -> p c t", p=128)
    xs = cx.dslot("x")
    m_ev = cx.dslot("mem").start(cx.sp, memT[:], memT_d.ap().rearrange("(c p) t -> p c t", p=128))
    x_evs = [xs.start(cx.sp, xT[:, :, b * BLK:(b + 1) * BLK], xT_dv[:, :, b * BLK:(b + 1) * BLK]) if load else None for b in range(NBLK)]
    c_ev = small_loads(cx, [(gm[:], gm_d.ap()), (g2[:], g2_d.ap()), (bo[:], bo_d.ap())])
    wv_ev = cx.dslot("wv", sw=True).start(cx.pool, wv[:], wv_d.ap())
    for q in (cx.act, cx.dve):
        q.wait(c_ev)
    nm = Normer(cx, pnorm, "m")
    make_epsb(cx, nm)
    ones = nm.ones

    mn_ev = nm.run(lambda c: memT[:, c, :], gm, lambda c: mnT[:, c, :], MEM, x_ev=m_ev)

    def evac_k(j, b, bank, mm_ev):
        cx.act.wait(mm_ev)
        return cx.act.tick(nc.scalar.copy(out=KxT[:, j, :], in_=bank[:, 0:MEM]))

    nck = None
    for j in range(KC):
        kw, wb = wring.acquire(cx.pool)
        w_ev = wslots[kw].start(cx.pool, wb[:], wk_d.ap()[j])
        cx.pe.wait(w_ev, mn_ev)
        kb, bank = banks.acquire(cx.pe)
        for c in range(KC):
            mm = nc.tensor.matmul(bank[:, 0:MEM], wb[:, c, :], mnT[:, c, :], start=(c == 0), stop=(c == KC - 1))
        mm_ev = cx.pe.tick(mm)
        k_ev = evac_k(j, 0, bank, mm_ev)
        banks.release(kb, k_ev)
        wring.release(kw, mm_ev)
    cx.pe.wait(wv_ev)
    for mt in range(2):
        for nh in range(2):
            kb, bank = banks.acquire(cx.pe)
            for c in range(KC):
                mm = nc.tensor.matmul(bank[:, :], mnT[:, c, mt * 128:(mt + 1) * 128], wv[:, c, nh * 512:(nh + 1) * 512],
                                      start=(c == 0), stop=(c == KC - 1))
            mm_ev = cx.pe.tick(mm)
            cx.act.wait(mm_ev)
            v_ev = cx.act.tick(nc.scalar.copy(out=Vx[:, mt, nh * 512:(nh + 1) * 512], in_=bank[:, :]))
            banks.release(kb, v_ev)

    h_evs = []
    for b in range(NBLK):
        sl = slice(b * BLK, (b + 1) * BLK)
        h_evs.append(nm.run(lambda c: xT[:, c, sl], g2, lambda c: hT[:, c, sl], BLK, x_ev=x_evs[b]))

    def evac_q(j, b, bank, mm_ev):
        cx.act.wait(mm_ev)
        return cx.act.tick(nc.scalar.copy(out=QT[:, j, b * BLK:(b + 1) * BLK], in_=bank[:, :]))

    q_pe = proj_fm(cx, lambda j: wq_d.ap()[j], KC, KC, lambda c, b: hT[:, c, b * BLK:(b + 1) * BLK], evac_q, wring, wslots, banks, h_evs)
    q_ev = (cx.act.sem, cx.act.n, cx.act.key)
    oT = hT
    o_evs = []
    for h in range(4):
        for b in range(NBLK):
            sl = slice(b * BLK, (b + 1) * BLK)
            kp, pt = PT.acquire(cx.act)
            for mt in range(2):
                kb, bank = banks.acquire(cx.pe)
                cx.pe.wait(q_ev, k_ev)
                for hf in range(2):
                    mm = nc.tensor.matmul(bank[:, :], KxT[:, 2 * h + hf, mt * 128:(mt + 1) * 128], QT[:, 2 * h + hf, sl],
                                          start=(hf == 0), stop=(hf == 1))
                mm_ev = cx.pe.tick(mm)
                cx.act.wait(mm_ev)
                p_ev = cx.act.tick(nc.scalar.activation(out=pt[:, mt, :], in_=bank[:, :], func=AF.Exp, scale=1.0 / 16.0))
                banks.release(kb, p_ev)
            cx.pe.wait(p_ev, v_ev)
            kd, dbank = pden.acquire(cx.pe)
            for mt in range(2):
                mm = nc.tensor.matmul(dbank[:, :], ones[:], pt[:, mt, :], start=(mt == 0), stop=(mt == 1))
            d_ev = cx.pe.tick(mm)
            kr, rcb = rc.acquire(cx.dve)
            cx.dve.wait(d_ev)
            r_ev = cx.dve.tick(nc.vector.reciprocal(out=rcb[:], in_=dbank[:, :]))
            pden.release(kd, r_ev)
            for hf in range(2):
                ko, obank = pov.acquire(cx.pe)
                if h == 0 and b == 0 and hf == 0:
                    cx.pe.wait(q_pe)
                for mt in range(2):
                    mm = nc.tensor.matmul(obank[:, :], Vx[:, mt, (2 * h + hf) * 128:(2 * h + hf + 1) * 128], pt[:, mt, :],
                                          start=(mt == 0), stop=(mt == 1))
                o_mm = cx.pe.tick(mm)
                cx.dve.wait(o_mm, r_ev)
                o_ev = cx.dve.tick(nc.vector.tensor_tensor(out=oT[:, 2 * h + hf, sl], in0=obank[:, :], in1=rcb[:], op=ALU.mult))
                pov.release(ko, o_ev)
            PT.release(kp, o_mm)
            rc.release(kr, o_ev)
            o_evs.append(o_ev)

    os_ = cx.dslot("o")
    out_dv = out_d.ap().rearrange("(c p) t -> p c t", p=128)
    st_evs = []

    def evac_o(j, b, bank, mm_ev):
        cx.dve.wait(mm_ev)
        xs_ = xT[:, j, b * BLK:(b + 1) * BLK]
        e = cx.dve.tick(nc.vector.scalar_tensor_tensor(out=xs_, in0=bank[:, :], scalar=bo[:, j:j + 1], in1=xs_, op0=ALU.add, op1=ALU.add))
        if store:
            cx.sp.wait(e)
            st_evs.append(os_.start(cx.sp, out_dv[:, j, b * BLK:(b + 1) * BLK], xs_))
        return e

    proj_fm(cx, lambda j: wo_d.ap()[j], KC, KC, lambda c, b: oT[:, c, b * BLK:(b + 1) * BLK], evac_o, wring, wslots, banks, o_evs)
    if store:
        cx.sp.wait(st_evs[-1])
    return


def prep_M2(inp, l):
    wkv = inp["xattn_w_kv"][l]
    return {
        "gm": colvec(inp["mem_norm_g"]), "g2": colvec(inp["norm_mem_g"][l]),
        "wq": wchunks(inp["xattn_w_q"][l]), "wk": wchunks(wkv[:, :D]),
        "wv": np.ascontiguousarray(wkv[:, D:].reshape(KC, 128, D).transpose(1, 0, 2)),
        "wo": wchunks(inp["xattn_w_o"][l]), "bo": colvec(inp["xattn_b_o"][l]),
    }


AX = mybir.AxisListType
GD = [1, 4, 16]
GOFF = [0, 64, 320]
GTOFF = [0, 1, 5]
NH = 1344
MASKV = -1.0e4


def sts(start, n, step):
    return slice(start, start + (n - 1) * step + 1, step)


def barrier(cx):
    qs = (cx.pe, cx.act, cx.dve, cx.pool)
    evs = [(q.sem, q.n, q.key) for q in qs if q.n > 0]
    devs = [(d.sem, d.n, d.key) for d in cx.scope_slots[-1] + cx.scope_sw[-1] if d.n > 0]
    for q in (cx.pe, cx.act, cx.dve, cx.pool, cx.sp):
        q.wait([e for e in evs if e[2] != q.key] + devs)


@contextlib.contextmanager
def scope(cx):
    old = cx.st
    cx.st = contextlib.ExitStack()
    cx.scope_slots.append([])
    cx.scope_sw.append([])
    try:
        yield
    finally:
        barrier(cx)
        cx.st.close()
        cx.st = old
        cx.free_slots.extend(cx.scope_slots.pop())
        cx.scope_sw.pop()


class View:
    def __init__(self, ap):
        self._ap = ap

    def ap(self):
        return self._ap


def _emit_M1(cx, io, p_only, dbg_stop=None, dbg_att=None, hbuf=None, xres=False):
    nc = cx.nc
    xT_d = io["xT"]
    g1_d = io["g1"]
    win_d = io["win"]
    wvC_d = io["wvC"]
    bin_d = io["bin"]
    bvC_d = io["bvC"]
    if p_only:
        KTb_d = io["KTb"]
        Vb_d = io["Vb"]
        zbb_d = io["zbb"]
    else:
        wvA_d = io["wvA"]
        bvA_d = io["bvA"]
        vgain_d = io["vgain"]
        wsT_d = io["wsT"]
        bsT_d = io["bsT"]
        wpbd_d = io["wpbd"]
        pb_d = io["pb"]
        psc_d = io["psc"]
        icnt_d = io["icnt"]
        wout_d = io["wout"]
        bout_d = io["bout"]
        bAB_d = io["bAB"]
        bAL_d = io["bAL"]
        bBR_d = io["bBR"]
        fl_d = io["fl"]
        out_d = io["out"]
    hT = hbuf if hbuf is not None else cx.sb("hT", [128, KC, T], BF16)
    inner = None
    if xres:
        inner = scope(cx)
        inner.__enter__()
    zbT = cx.sb("zbT", [128, 2, T + 16], F32)
    KT = [cx.sb("KT%d" % g, [128, GD[g], T // GD[g] + 128], BF16) for g in range(3)]
    V = [cx.sb("V%d" % g, [128, 16, 128], BF16) for g in range(3)]
    g1 = cx.sb("g1", [128, KC], F32)
    binc = cx.sb("binc", [128, 17], F32)
    bvC = cx.sb("bvC", [128, 384], F32)
    loads = [(g1[:], g1_d.ap()), (binc[:], bin_d.ap()), (bvC[:], bvC_d.ap().partition_broadcast(128))]
    if not p_only:
        uT = cx.sb("uT", [128, 3, T], BF16)
        vn = cx.sb("vn", [128, 16, 384], BF16)
        QT = [cx.sb("QT%d" % g, [128, GD[g], T // GD[g]], BF16) for g in range(3)]
        bvA = cx.sb("bvA", [128, 384], F32)
        vgain = cx.sb("vgain", [128, 384], F32)
        bsT = cx.sb("bsT", [128, 3, 128], F32)
        pbc = cx.sb("pbc", [128, 2], F32)
        psc = cx.sb("psc", [128, 2], F32)
        icnt = cx.sb("icnt", [128, 2, 16], F32)
        bout = cx.sb("bout", [128, KC], F32)
        flm = cx.sb("flm", [128, 2], F32)
        wsT = cx.sb("wsT", [128, 6, 128], BF16)
        wpbd = cx.sb("wpbd", [128, 2, 128], BF16)
        loads += [(bvA[:], bvA_d.ap().partition_broadcast(128)), (vgain[:], vgain_d.ap().partition_broadcast(128)),
                  (bsT[:], bsT_d.ap()), (pbc[:], pb_d.ap()), (psc[:], psc_d.ap()), (icnt[:], icnt_d.ap()), (bout[:], bout_d.ap()), (flm[:], fl_d.ap())]
    c_ev = small_loads(cx, loads)
    for q in (cx.act, cx.dve):
        q.wait(c_ev)
    banks = Ring([cx.bank(i) for i in range(6)])
    pnorm = cx.bank(6)
    xT_dv = xT_d.ap().rearrange("(c p) t -> p c t", p=128)
    wsl = cx.dslot("wres", sw=True)
    NB2 = 256

    with scope(cx):
        nm = Normer(cx, pnorm, "n1", nmax=NB2)
        make_epsb(cx, nm)
        arena = cx.sb("arena", [128, 4 * KC * NB2], F32)
        xring = Ring([arena[:, i * KC * NB2:(i + 1) * KC * NB2].rearrange("p (c t) -> p c t", c=KC) for i in range(4)])
        xsl = [cx.dslot("xr%d" % i) for i in range(4)]
        wring = Ring([cx.sb("w%d" % i, [128, KC, 128], BF16) for i in range(2)])
        wslots = [cx.dslot("w%d" % i, sw=True) for i in range(2)]
        wvC = cx.sb("wvC", [128, KC, 384], BF16)
        wvC_ev = wsl.start(cx.pool, wvC[:], wvC_d.ap())
        if not p_only:
            wvA = cx.sb("wvA", [128, KC, 384], BF16)
            wsl.start(cx.pool, wvA[:], wvA_d.ap())
            wsl.start(cx.pool, wsT[:], wsT_d.ap())
            wvA_ev = wsl.start(cx.pool, wpbd[:], wpbd_d.ap())
            wvC_ev = wvA_ev
        h_evs = []
        for bb in range(T // NB2):
            kx, xr = xring.acquire(cx.sp)
            x_ev = xsl[kx].start(cx.sp, xr, xT_dv[:, :, bb * NB2:(bb + 1) * NB2])
            h_ev = nm.run(lambda c: xr[:, c, :], g1, lambda c: hT[:, c, bb * NB2:(bb + 1) * NB2], NB2, x_ev=x_ev)
            xring.release(kx, h_ev)
            h_evs.append(h_ev)

        jls = [[11, 12, 13, 6, 7]] if p_only else [[11, 12, 13, 6, 7], [0, 1, 2, 8, 9, 10]]
        cur = {"jl": jls[0]}
        fm_evs = []

        def evac_in(i, b, bank, mm_ev):
            j = cur["jl"][i]
            sl = slice(b * BLK, (b + 1) * BLK)
            cx.act.wait(mm_ev)
            bias = binc[:, j:j + 1]
            if j < 3:
                ins = nc.scalar.activation(out=uT[:, j, sl], in_=bank[:, :], func=AF.Gelu_apprx_tanh, bias=bias, scale=1.0)
            elif j < 8:
                ins = nc.scalar.activation(out=zbT[:, j - 6, 8 + b * BLK:8 + (b + 1) * BLK], in_=bank[:, :], func=AF.Identity, bias=bias, scale=1.0)
            elif j < 11:
                g = j - 8
                mb = BLK // GD[g]
                ins = nc.scalar.activation(out=QT[g][:, :, b * mb:(b + 1) * mb].rearrange("p r m -> p m r"),
                                           in_=bank[:, :].rearrange("p (m r) -> p m r", r=GD[g]), func=AF.Identity, bias=bias, scale=1.0)
            else:
                g = j - 11
                mb = BLK // GD[g]
                ins = nc.scalar.activation(out=KT[g][:, :, 64 + b * mb:64 + (b + 1) * mb].rearrange("p r m -> p m r"),
                                           in_=bank[:, :].rearrange("p (m r) -> p m r", r=GD[g]), func=AF.Identity, bias=bias, scale=1.0)
            e = cx.act.tick(ins)
            fm_evs.append(e)
            return e

        proj_fm(cx, lambda i: win_d.ap()[cur["jl"][i]], len(cur["jl"]), KC, lambda c, b: hT[:, c, b * BLK:(b + 1) * BLK], evac_in,
                wring, wslots, banks, h_evs)

        v_evs = []
        for g in range(3):
            d = GD[g]
            nt = 16 // d
            tiles = [(r, t) for r in range(d) for t in range(nt)]
            for q4 in range(4):
                kb, bank = banks.acquire(cx.pe)
                cx.pe.wait(wvC_ev, h_evs)
                for i in range(4):
                    r, t = tiles[q4 * 4 + i]
                    s0 = r + d * 128 * t
                    for c in range(KC):
                        mm = nc.tensor.matmul(bank[:, i * 128:(i + 1) * 128], hT[:, c, sts(s0, 128, d)], wvC[:, c, g * 128:(g + 1) * 128],
                                              start=(c == 0), stop=(c == KC - 1))
                mm_ev = cx.pe.tick(mm)
                cx.dve.wait(mm_ev)
                e = cx.dve.tick(nc.vector.tensor_tensor(out=V[g][:, q4 * 4:q4 * 4 + 4, :], in0=bank[:, :].rearrange("p (a n) -> p a n", n=128),
                                                        in1=bvC[:, g * 128:(g + 1) * 128].unsqueeze(1).to_broadcast([128, 4, 128]), op=ALU.add))
                banks.release(kb, e)
                v_evs.append(e)

        cc_evs = None
        if "exp" in io:
            ex = io["exp"]
            es = cx.dslot("exp")
            cx.sp.wait(fm_evs, v_evs)
            with nc.allow_non_contiguous_dma(reason="boundary export"):
                for g in range(3):
                    d = GD[g]
                    nt = 16 // d
                    L = T // d
                    es.start(cx.sp, ex["K"](0, g), KT[g][:, :, 64:128])
                    es.start(cx.sp, ex["K"](1, g), KT[g][:, :, L:L + 64])
                    es.start(cx.sp, ex["V"](0, g), V[g][0:64, 0:16:nt, :])
                    es.start(cx.sp, ex["V"](1, g), V[g][64:128, nt - 1:16:nt, :])
                es.start(cx.sp, ex["zb"](0), zbT[:, :, 8:16])
                e_last = es.start(cx.sp, ex["zb"](1), zbT[:, :, T:T + 8])
        if not p_only:
            cur["jl"] = jls[1]
            proj_fm(cx, lambda i: win_d.ap()[cur["jl"][i]], len(cur["jl"]), KC, lambda c, b: hT[:, c, b * BLK:(b + 1) * BLK], evac_in,
                    wring, wslots, banks, h_evs)
        if "exp" in io:
            cc_evs = ex["run"](cx, e_last)
        if not p_only:
            cx.dve.wait(h_evs)
            cx.act.wait(h_evs)
            vtr = Ring([(arena[:, (2 * i) * 1536:(2 * i + 1) * 1536].rearrange("p (a n) -> p a n", a=4),
                         arena[:, (2 * i + 1) * 1536:(2 * i + 2) * 1536].rearrange("p (a n) -> p a n", a=4),
                         cx.sb("v6%d" % i, [128, 24], F32)) for i in range(2)])
            vn_evs = []
            for q4 in range(4):
                kv, (vt, vg, v6) = vtr.acquire(cx.dve)
                for i in range(4):
                    t = q4 * 4 + i
                    kb, bank = banks.acquire(cx.pe)
                    cx.pe.wait(wvA_ev, h_evs)
                    for c in range(KC):
                        mm = nc.tensor.matmul(bank[:, 0:384], hT[:, c, t * 128:(t + 1) * 128], wvA[:, c, :], start=(c == 0), stop=(c == KC - 1))
                    mm_ev = cx.pe.tick(mm)
                    cx.dve.wait(mm_ev)
                    e1 = cx.dve.tick(nc.vector.tensor_tensor(out=vt[:, i, :], in0=bank[:, 0:384], in1=bvA[:], op=ALU.add))
                    banks.release(kb, e1)
                cx.act.wait(e1)
                e2 = cx.act.tick(nc.scalar.activation(out=vg[:], in_=vt[:], func=AF.Gelu_apprx_tanh))
                cx.act.wait(e2)
                e3 = cx.act.tick(nc.scalar.activation(out=vt[:], in_=vg[:], func=AF.Square))
                cx.dve.wait(e3)
                e4 = cx.dve.tick(nc.vector.tensor_reduce(out=v6[:], in_=vt[:].rearrange("p a (h e) -> p (a h) e", e=64), axis=AX.X, op=ALU.add))
                cx.act.wait(e4)
                e5 = cx.act.tick(nc.scalar.activation(out=v6[:], in_=v6[:], func=AF.Sqrt, bias=nm.epsb[:, 0:1], scale=1.0 / 64))
                cx.dve.wait(e5)
                e6 = cx.dve.tick(nc.vector.reciprocal(out=v6[:], in_=v6[:]))
                cx.dve.wait(e6)
                e7 = cx.dve.tick(nc.vector.tensor_tensor(out=vt[:].rearrange("p a (h e) -> p (a h) e", e=64), in0=vg[:].rearrange("p a (h e) -> p (a h) e", e=64),
                                                         in1=v6[:].unsqueeze(2).to_broadcast([128, 24, 64]), op=ALU.mult))
                cx.dve.wait(e7)
                e8 = cx.dve.tick(nc.vector.tensor_tensor(out=vn[:, q4 * 4:q4 * 4 + 4, :], in0=vt[:], in1=vgain[:].unsqueeze(1).to_broadcast([128, 4, 384]), op=ALU.mult))
                vtr.release(kv, e8)
                vn_evs.append(e8)

        if p_only:
            os_ = cx.dslot("o", sw=True)
            cx.pool.wait(fm_evs, v_evs)
            for g in range(3):
                d = GD[g]
                nt = 16 // d
                L = T // d
                os_.start(cx.pool, KTb_d.ap()[:, 0, GOFF[g]:GOFF[g] + 64 * d].rearrange("p (r m) -> p r m", m=64), KT[g][:, :, 64:128])
                os_.start(cx.pool, KTb_d.ap()[:, 1, GOFF[g]:GOFF[g] + 64 * d].rearrange("p (r m) -> p r m", m=64), KT[g][:, :, L:L + 64])
                os_.start(cx.pool, Vb_d.ap()[0, :, GTOFF[g]:GTOFF[g] + d, :], V[g][0:64, 0:16:nt, :])
                os_.start(cx.pool, Vb_d.ap()[1, :, GTOFF[g]:GTOFF[g] + d, :], V[g][64:128, nt - 1:16:nt, :])
            cx.sp.wait(fm_evs)
            o2 = cx.dslot("o2")
            with nc.allow_non_contiguous_dma(reason="small boundary"):
                o2.start(cx.sp, zbb_d.ap()[:, :, 0:8], zbT[:, :, 8:16])
                e_o2 = o2.start(cx.sp, zbb_d.ap()[:, :, 8:16], zbT[:, :, T:T + 8])
            cx.sp.wait(e_o2)
            cx.pool.wait((os_.sem, os_.n, os_.key))

    if p_only:
        return

    if dbg_stop == 1:
        return
    if cc_evs is not None:
        cx.pool.wait(cc_evs)
        cx.sp.wait(cc_evs)
    hs = cx.dslot("halo", sw=True)
    for g in range(3):
        d = GD[g]
        L = T // d
        hs.start(cx.pool, KT[g][:, :, 0:64], io["KThL"].ap()[:, GOFF[g]:GOFF[g] + 64 * d].rearrange("p (r m) -> p r m", m=64))
        hs.start(cx.pool, KT[g][:, :, 64 + L:128 + L], io["KThR"].ap()[:, GOFF[g]:GOFF[g] + 64 * d].rearrange("p (r m) -> p r m", m=64))
    kh_ev = (hs.sem, hs.n, hs.key)
    h2 = cx.dslot("halo2")
    with nc.allow_non_contiguous_dma(reason="small halo"):
        h2.start(cx.sp, zbT[:, :, 0:8], io["zbhL"].ap())
        zh_ev = h2.start(cx.sp, zbT[:, :, T + 8:T + 16], io["zbhR"].ap())
    yT = hT

    with scope(cx):
        pq = cx.dve
        pv = nc.vector
        VhL = cx.sb("VhL", [64, 21, 128], BF16)
        VhR = cx.sb("VhR", [64, 21, 128], BF16)
        hs2 = cx.dslot("vh", sw=True)
        hs2.start(cx.pool, VhL[:], io["VhL"].ap())
        vh_ev = hs2.start(cx.pool, VhR[:], io["VhR"].ap())
        gt = Ring([cx.sb("gt%d" % i, [128, 512], F32) for i in range(2)])
        for hp in range(3):
            for t4 in range(4):
                kb, bank = banks.acquire(cx.pe)
                cx.pe.wait(vn_evs, wvA_ev)
                for i in range(4):
                    t = t4 * 4 + i
                    for hh in range(2):
                        h = 2 * hp + hh
                        mm = nc.tensor.matmul(bank[hh * 64:(hh + 1) * 64, i * 128:(i + 1) * 128], vn[:, t, h * 64:(h + 1) * 64], wsT[:, h, :],
                                              start=True, stop=True)
                mm_ev = cx.pe.tick(mm)
                kg, gtb = gt.acquire(cx.dve)
                cx.dve.wait(mm_ev)
                e1 = cx.dve.tick(nc.vector.tensor_tensor(out=gtb[:].rearrange("p (a n) -> p a n", n=128), in0=bank[:, :].rearrange("p (a n) -> p a n", n=128),
                                                         in1=bsT[:, hp, :].unsqueeze(1).to_broadcast([128, 4, 128]), op=ALU.add))
                banks.release(kb, e1)
                cx.dve.wait(e1, fm_evs)
                e2 = cx.dve.tick(nc.vector.tensor_tensor(out=yT[:, hp, t4 * 512:(t4 + 1) * 512], in0=gtb[:], in1=uT[:, hp, t4 * 512:(t4 + 1) * 512], op=ALU.mult))
                gt.release(kg, e2)

        if dbg_stop == 2:
            barrier(cx)
            return
        pa = cx.sb("pa", [128, T + 16], F32)
        pbuf = cx.sb("pbuf", [128, T + 16], F32)
        pooled = cx.sb("pooled", [128, 2, T], BF16)
        W = T + 16
        pool_ops = []

        def dv(thunk):
            pool_ops.append(thunk)

        def run_pool_ops(n):
            for _ in range(n):
                if not pool_ops:
                    return
                if not pool_started:
                    pool_started.append(1)
                    pq.wait(zh_ev, fm_evs, c_ev)
                e = pq.tick(pool_ops.pop(0)())
                pq.wait(e)

        pool_started = []
        dv(lambda: pv.tensor_scalar(out=zbT[:, :, 0:8], in0=zbT[:, :, 0:8], scalar1=flm[:, 0:1], scalar2=None, op0=ALU.mult))
        dv(lambda: pv.tensor_scalar(out=zbT[:, :, T + 8:T + 16], in0=zbT[:, :, T + 8:T + 16], scalar1=flm[:, 1:2], scalar2=None, op0=ALU.mult))

        def pool_out(src, p0, ch, w):
            ps_ = slice(p0, p0 + 64)
            o = 8 - w // 2
            dv(lambda: pv.tensor_scalar(out=pa[ps_, 0:T], in0=src[ps_, o:o + T], scalar1=1.0 / w, scalar2=None, op0=ALU.mult))
            dv(lambda: pv.tensor_tensor(out=pooled[ps_, ch, :], in0=pa[ps_, 0:T], in1=zbT[ps_, ch, 8:8 + T], op=ALU.subtract))
            for (c0, k0) in ((0, 0), (T - 8, 8)):
                dv(lambda c0=c0, k0=k0: pv.tensor_tensor(out=pa[ps_, 0:8], in0=src[ps_, o + c0:o + c0 + 8], in1=icnt[ps_, ch, k0:k0 + 8], op=ALU.mult))
                dv(lambda c0=c0, k0=k0: pv.tensor_tensor(out=pooled[ps_, ch, c0:c0 + 8], in0=pa[ps_, 0:8], in1=zbT[ps_, ch, 8 + c0:16 + c0], op=ALU.subtract))

        dv(lambda: pv.tensor_tensor(out=pbuf[:, 0:W - 1], in0=zbT[:, 0, 0:W - 1], in1=zbT[:, 0, 1:W], op=ALU.add))
        dv(lambda: pv.tensor_tensor(out=pa[64:128, 8:W - 3], in0=pbuf[64:128, 8:W - 3], in1=pbuf[64:128, 10:W - 1], op=ALU.add))
        pool_out(pbuf, 0, 0, 2)
        dv(lambda: pv.tensor_tensor(out=pa[64:128, 0:W - 3], in0=pbuf[64:128, 0:W - 3], in1=pbuf[64:128, 2:W - 1], op=ALU.add))
        dv(lambda: pv.tensor_copy(out=pbuf[64:128, 0:W - 3], in_=pa[64:128, 0:W - 3]))
        pool_out(pbuf, 64, 0, 4)
        dv(lambda: pv.tensor_tensor(out=pa[:, 0:W - 1], in0=zbT[:, 1, 0:W - 1], in1=zbT[:, 1, 1:W], op=ALU.add))
        dv(lambda: pv.tensor_tensor(out=pbuf[:, 0:W - 3], in0=pa[:, 0:W - 3], in1=pa[:, 2:W - 1], op=ALU.add))
        dv(lambda: pv.tensor_tensor(out=pa[:, 0:W - 7], in0=pbuf[:, 0:W - 7], in1=pbuf[:, 4:W - 3], op=ALU.add))
        dv(lambda: pv.tensor_tensor(out=pbuf[64:128, 0:W - 15], in0=pa[64:128, 0:W - 15], in1=pa[64:128, 8:W - 7], op=ALU.add))
        dv(lambda: pv.tensor_copy(out=pbuf[0:64, 0:W - 7], in_=pa[0:64, 0:W - 7]))
        pool_out(pbuf, 0, 1, 8)
        pe_pool = pool_out(pbuf, 64, 1, 16)
        acc = cx.sb("acc", [128, 2, T], F32)
        EAB = cx.sb("EAB", [128, 6, 2, 128], F32)
        EAL = cx.sb("EAL", [64, 6, 64], F32)
        EBR = cx.sb("EBR", [64, 6, 64], F32)
        ones = cx.sb("ones_a", [128, 64], BF16)
        pex = Ring([cx.sb("pex%d" % i, [128, 128], F32) for i in range(4)])
        PTr = Ring([cx.sb("PT%d" % i, [128, 2, 128], BF16) for i in range(4)])
        b_ev = small_loads(cx, [(EAB[:], bAB_d.ap()), (EAL[:], bAL_d.ap()), (EBR[:], bBR_d.ap())])
        cx.act.wait(b_ev)
        nc.scalar.activation(out=EAB[:], in_=EAB[:], func=AF.Exp)
        nc.scalar.activation(out=EAL[:], in_=EAL[:], func=AF.Exp)
        eb_ev = cx.act.tick(nc.scalar.activation(out=EBR[:], in_=EBR[:], func=AF.Exp))
        on_ev = cx.dve.tick(nc.vector.memset(ones[:], 1.0))
        cx.dve.wait(eb_ev)
        cx.pe.wait(on_ev, vh_ev, kh_ev, fm_evs, v_evs)
        sbanks = [Ring([banks.bufs[2 + 2 * hh], banks.bufs[3 + 2 * hh]]) for hh in range(2)]
        obank = Ring([banks.bufs[0], banks.bufs[1]])
        acc_last = None
        da = dbg_att or {}
        for g in da.get("groups", range(3)):
            d = GD[g]
            nt = 16 // d
            L = T // d
            for r in range(d):
                for j in range(nt + 1):
                    if j == 0:
                        nq, m0, qq0 = 64, 0, 64
                    elif j == nt:
                        nq, m0, qq0 = 64, L - 64, 0
                    else:
                        nq, m0, qq0 = 128, 128 * j - 64, 0
                    q0 = r + d * m0
                    tl = []
                    for ti in range(2):
                        t = j - 1 + ti
                        if t < 0:
                            tl.append((64, 0, lambda hh: EAL[:, 2 * g + hh, :],
                                       lambda hh: VhL[:, GTOFF[g] + r, hh * 64:(hh + 1) * 64]))
                        elif t >= nt:
                            tl.append((64, 64 + L, lambda hh: EBR[:, 2 * g + hh, :],
                                       lambda hh: VhR[:, GTOFF[g] + r, hh * 64:(hh + 1) * 64]))
                        else:
                            tl.append((128, 64 + 128 * t, (lambda hh, ti=ti: EAB[:, 2 * g + hh, ti, qq0:qq0 + nq]),
                                       (lambda hh, t=t: V[g][:, r * nt + t, hh * 64:(hh + 1) * 64])))
                    ko, ob = obank.acquire(cx.pe)
                    pts = []
                    for hh in range(2):
                        ks, sb_ = sbanks[hh].acquire(cx.pe)
                        hp_ = slice(hh * 64, (hh + 1) * 64)
                        kp, pt = PTr.acquire(cx.dve)
                        for ti, (nk, kc0, ebf, vf) in enumerate(tl):
                            mm_ev = cx.pe.tick(nc.tensor.matmul(sb_[0:nk, ti * 128:ti * 128 + nq], KT[g][hp_, r, kc0:kc0 + nk],
                                                                QT[g][hp_, r, m0:m0 + nq], start=True, stop=True))
                            kx, px = pex.acquire(cx.act)
                            cx.act.wait(mm_ev)
                            x_ev = cx.act.tick(nc.scalar.activation(out=px[0:nk, 0:nq], in_=sb_[0:nk, ti * 128:ti * 128 + nq], func=AF.Exp, scale=0.125))
                            cx.dve.wait(x_ev)
                            p_ev = cx.dve.tick(nc.vector.tensor_tensor(out=pt[0:nk, ti, 0:nq], in0=px[0:nk, 0:nq], in1=ebf(hh), op=ALU.mult))
                            pex.release(kx, p_ev)
                        sbanks[hh].release(ks, x_ev)
                        pts.append((kp, pt, p_ev))
                    if da.get("nopv"):
                        for hh in range(2):
                            PTr.release(pts[hh][0], pts[hh][2])
                        obank.release(ko, pts[1][2])
                        acc_last = pts[1][2]
                        continue
                    for hh in range(2):
                        kp, pt, p_ev = pts[hh]
                        cx.pe.wait(p_ev)
                        for ti, (nk, kc0, ebf, vf) in enumerate(tl):
                            mm = nc.tensor.matmul(ob[hh * 64:(hh + 1) * 64, 0:nq], vf(hh), pt[0:nk, ti, 0:nq], start=(ti == 0), stop=(ti == 1))
                    for hh in range(2):
                        kp, pt, p_ev = pts[hh]
                        for ti, (nk, kc0, ebf, vf) in enumerate(tl):
                            mm = nc.tensor.matmul(ob[hh * 64:(hh + 1) * 64, 128:128 + nq], ones[0:nk, :], pt[0:nk, ti, 0:nq], start=(ti == 0), stop=(ti == 1))
                        o_mm = cx.pe.tick(mm)
                        PTr.release(kp, o_mm)
                    cx.dve.wait(o_mm, acc_last)
                    src = ob[:, 0:256].rearrange("p (a n) -> p a n", n=128)[:, :, 0:nq]
                    dst = acc[:, :, sts(q0, nq, d)]
                    if g == 0:
                        a_ev = cx.dve.tick(nc.vector.tensor_copy(out=dst, in_=src))
                    else:
                        a_ev = cx.dve.tick(nc.vector.tensor_tensor(out=dst, in0=src, in1=dst, op=ALU.add))
                    acc_last = a_ev
                    obank.release(ko, a_ev)
                    run_pool_ops(1)
        cx.dve.wait(acc_last)
        e = cx.dve.tick(nc.vector.reciprocal(out=acc[:, 1, :], in_=acc[:, 1, :]))
        cx.dve.wait(e)
        yc_ev = cx.dve.tick(nc.vector.tensor_tensor(out=yT[:, 5, :], in0=acc[:, 0, :], in1=acc[:, 1, :], op=ALU.mult))
        run_pool_ops(len(pool_ops))
        pool_ev = (pq.sem, pq.n, pq.key)
        cx.pe.wait(acc_last)
        for ch in range(2):
            for b in range(NBLK):
                kb, bank = banks.acquire(cx.pe)
                cx.pe.wait(pool_ev, wvA_ev)
                mm_ev = cx.pe.tick(nc.tensor.matmul(bank[:, :], wpbd[:, ch, :], pooled[:, ch, b * BLK:(b + 1) * BLK], start=True, stop=True))
                cx.dve.wait(mm_ev)
                e = cx.dve.tick(nc.vector.tensor_scalar(out=yT[:, 3 + ch, b * BLK:(b + 1) * BLK], in0=bank[:, :], scalar1=pbc[:, ch:ch + 1],
                                                        scalar2=psc[:, ch:ch + 1], op0=ALU.add, op1=ALU.mult))
                banks.release(kb, e)

    xT_res = None
    if inner is not None:
        inner.__exit__(None, None, None)
        xT_res = cx.sb("xT_res", [128, KC, T], F32)
    with scope(cx):
        if xT_res is not None:
            bout = cx.sb("bout3", [128, KC], F32)
            b3_ev = small_loads(cx, [(bout[:], bout_d.ap())])
            cx.dve.wait(b3_ev)
        wring = Ring([cx.sb("wo%d" % i, [128, 6, 128], BF16) for i in range(2)])
        wslots = [cx.dslot("wo%d" % i, sw=True) for i in range(2)]
        xr = Ring([cx.sb("xo%d" % i, [128, BLK], F32) for i in range(6)])
        xsl = [cx.dslot("xo%d" % i) for i in range(6)]
        os_ = cx.dslot("o")
        out_dv = out_d.ap().rearrange("(c p) t -> p c t", p=128)
        st = []

        def evac_o(j, b, bank, mm_ev):
            sl = slice(b * BLK, (b + 1) * BLK)
            kx, xb = xr.acquire(cx.act)
            x_ev = xsl[kx].start(cx.act, xb[:], xT_dv[:, j, sl])
            cx.dve.wait(mm_ev, x_ev)
            if xT_res is not None:
                e = cx.dve.tick(nc.vector.scalar_tensor_tensor(out=xT_res[:, j, sl], in0=bank[:, :], scalar=bout[:, j:j + 1], in1=xb[:], op0=ALU.add, op1=ALU.add))
                xr.release(kx, e)
                return e
            e = cx.dve.tick(nc.vector.scalar_tensor_tensor(out=xb[:], in0=bank[:, :], scalar=bout[:, j:j + 1], in1=xb[:], op0=ALU.add, op1=ALU.add))
            cx.sp.wait(e)
            o_ev = os_.start(cx.sp, out_dv[:, j, sl], xb[:])
            xr.release(kx, o_ev)
            st.append(o_ev)
            return e

        proj_fm(cx, lambda j: wout_d.ap()[j], KC, 6, lambda c, b: yT[:, c, b * BLK:(b + 1) * BLK], evac_o, wring, wslots, banks, None)
        if st:
            cx.sp.wait(st[-1])
    return xT_res


def _t5_bucket(rel):
    nb = 16
    ret = (rel > 0).astype(np.int32) * nb
    n = np.abs(rel)
    max_exact = nb // 2
    large = max_exact + (np.log(np.maximum(n, 1) / max_exact) / math.log(1024 / max_exact) * (nb - max_exact)).astype(np.int32)
    large = np.minimum(large, nb - 1)
    return ret + np.where(n < max_exact, n, large)


def attn_bias(rel_table):
    kk = np.arange(128)[:, None]
    qq = np.arange(128)[None, :]
    bAB = np.full((128, 6, 2, 128), MASKV, np.float32)
    for g, d in enumerate(GD):
        for ti, delta in enumerate((kk - qq - 64, kk - qq + 64)):
            bk = _t5_bucket(delta * d)
            ok = np.abs(delta) <= 64
            for hh in range(2):
                vals = rel_table[bk, 2 * g + hh]
                bAB[:, 2 * g + hh, ti, :] = np.where(ok, vals, MASKV)
    bAL = np.ascontiguousarray(bAB[64:128, :, 0, 64:128])
    bBR = np.ascontiguousarray(bAB[0:64, :, 1, 0:64])
    return bAB, bAL, bBR


def pool_icnt(s):
    out = np.zeros((128, 2, 16), np.float32)
    a = s * T
    for ch in range(2):
        for gh in range(2):
            w = (2, 4, 8, 16)[2 * ch + gh]
            for k in range(16):
                pos = a + (k if k < 8 else T - 16 + k)
                lo = min(max(pos - w // 2, 0), SEQ)
                hi = min(max(pos + w // 2, 0), SEQ)
                out[gh * 64:(gh + 1) * 64, ch, k] = 1.0 / (hi - lo)
    return out


def prep_M1(inp, l):
    w_in = inp["w_in"][l]
    b_in = inp["b_in"][l]

    def tm(w):
        return np.ascontiguousarray(w.reshape(KC, 128, -1).transpose(1, 0, 2))
    ws = inp["gmlp_w_s"][l]
    bs = inp["gmlp_b_s"][l]
    bsT = np.zeros((128, 3, 128), np.float32)
    for hp in range(3):
        for hh in range(2):
            bsT[hh * 64:(hh + 1) * 64, hp, :] = bs[2 * hp + hh][None, :]
    wpbd = np.zeros((128, 2, 128), np.float32)
    for ch in range(2):
        for gh in range(2):
            wpbd[gh * 64:(gh + 1) * 64, ch, gh * 64:(gh + 1) * 64] = inp["pool_w"][l][2 * ch + gh]
    bAB, bAL, bBR = attn_bias(inp["rel_table"])
    shared = {
        "g1": colvec(inp["norm_mix_g"][l]), "win": wchunks(w_in), "wvC": tm(w_in[:, 1792:2176]), "bin": colvec(b_in),
        "bvC": np.ascontiguousarray(b_in[1792:2176]),
    }
    extra = {
        "wvA": tm(w_in[:, 384:768]), "bvA": np.ascontiguousarray(b_in[384:768]),
        "vgain": np.ascontiguousarray(inp["gmlp_v_g"][l].reshape(384)),
        "wsT": np.ascontiguousarray(ws.transpose(2, 0, 1)), "bsT": bsT, "wpbd": wpbd,
        "pb": colvec(inp["pool_b"][l].reshape(256)), "psc": colvec(inp["pool_scale"][l]),
        "wout": wchunks(inp["w_out"][l]), "bout": colvec(inp["b_out"][l]), "bAB": bAB,
    }
    percore = []
    for cid in range(8):
        s = cid % 2
        percore.append({
            "icnt": pool_icnt(s),
            "bAL": bAL if s == 1 else np.full_like(bAL, MASKV),
            "bBR": bBR if s == 0 else np.full_like(bBR, MASKV),
        })
    return shared, extra, percore


NCORES = 8
F_KEYS = ["g3", "gf", "wup", "bup", "cw", "cb", "wd", "bd"]
M2_KEYS = ["gm", "g2", "wq", "wk", "wv", "wo", "bo"]
M1_KEYS = ["g1", "win", "wvC", "bin", "bvC", "wvA", "bvA", "vgain", "wsT", "bsT", "wpbd", "pb", "psc", "wout", "bout"]
CORE_KEYS = ["icnt", "bAL", "bBR", "fl"]
PAIRS = [[0, 1], [2, 3], [4, 5], [6, 7]]
KVW = 2 * NH + 21 * 128


def emit_F(cx, io, final):
    with scope(cx):
        _emit_F(cx, io, final)


def emit_M2(cx, io):
    with scope(cx):
        _emit_M2(cx, io)


def emit_M1(cx, io, p_only):
    with scope(cx):
        _emit_M1(cx, io, p_only)


def run_cc(cx, pairs, after_ev):
    nc = cx.nc
    cx.pool.wait(after_ev)
    evs = []
    for (snd, rcv) in pairs:
        sem = cx.sem("cc")
        nc.gpsimd.collective_compute("AllGather", ALU.bypass, replica_groups=PAIRS, ins=[snd.ap().opt()],
                                     outs=[rcv.ap().opt()]).then_inc(sem)
        evs.append((sem, 1, new_key()))
    return evs


def build_fused(shapes):
    nc = bass.Bass("TRN2", target_bir_lowering=False)
    cx = Cx(nc)
    dr = {name: cx.dram_in(name, list(shp)) for name, shp in shapes.items()}
    out_d = cx.dram_out("out", [D, T])

    def scratch(name, shape):
        return nc.dram_tensor(name, list(shape), F32)
    XA = scratch("XA", [D, T])
    XB = scratch("XB", [D, T])
    X1 = scratch("X1", [D, T])
    nkv = 128 * KVW // 2 // 16
    for l in range(2):
        KVs = scratch("KVs%d" % l, [16, nkv])
        KVr = scratch("KVr%d" % l, [32, nkv])
        ZBs = scratch("ZBs%d" % l, [16, 256])
        ZBr = scratch("ZBr%d" % l, [32, 256])
        XHs = scratch("XHs%d" % l, [16, 128])
        XHr = scratch("XHr%d" % l, [32, 128])
        KVs_bf = KVs.ap().bitcast(BF16).rearrange("a (b c) -> (a b) c", b=8)
        KVr_bf = KVr.ap().bitcast(BF16).rearrange("(r a) (b c) -> r (a b) c", r=2, b=8)
        ZBs_v = ZBs.ap().rearrange("a (b c) -> (a b) c", b=8).rearrange("p (h k) -> p h k", k=16)
        ZBr_v = ZBr.ap().rearrange("(r a) (b c) -> r (a b) c", r=2, b=8).rearrange("r p (h k) -> r p h k", k=16)
        XHs_v = XHs.ap().rearrange("a (b c) -> (a b) c", b=8).rearrange("p (c k) -> p c k", k=2)
        XHr_v = XHr.ap().rearrange("(r a) (b c) -> r (a b) c", r=2, b=8).rearrange("r p (c k) -> r p c k", k=2)
        Xin = dr["x"] if l == 0 else X1
        Xout = out_d if l == 1 else X1

        def wts(keys):
            return {k: dr["%s_%d" % (k, l)] for k in keys}

        def expK(side, g, KVs_bf=KVs_bf):
            return KVs_bf[:, side * NH + GOFF[g]:side * NH + GOFF[g] + 64 * GD[g]].rearrange("p (r m) -> p r m", m=64)

        def expV(side, g, KVs_bf=KVs_bf):
            c0 = 2 * NH + GTOFF[g] * 128
            return KVs_bf[side * 64:(side + 1) * 64, c0:c0 + GD[g] * 128].rearrange("p (t n) -> p t n", n=128)

        def expZ(side, ZBs_v=ZBs_v):
            return ZBs_v[:, :, side * 8:(side + 1) * 8]

        io = wts(M1_KEYS)
        io.update({k: dr[k] for k in CORE_KEYS})
        io.update({"xT": Xin, "out": XA, "bAB": dr["bAB"],
                   "exp": {"K": expK, "V": expV, "zb": expZ,
                           "run": (lambda cx_, ev, a=KVs, b=KVr, c=ZBs, d=ZBr: run_cc(cx_, [(a, b), (c, d)], ev))},
                   "KThL": View(KVr_bf[0, :, NH:2 * NH]), "KThR": View(KVr_bf[1, :, 0:NH]),
                   "VhL": View(KVr_bf[0, 64:128, 2 * NH:KVW].rearrange("p (t n) -> p t n", n=128)),
                   "VhR": View(KVr_bf[1, 0:64, 2 * NH:KVW].rearrange("p (t n) -> p t n", n=128)),
                   "zbhL": View(ZBr_v[0, :, :, 8:16]), "zbhR": View(ZBr_v[1, :, :, 0:8])})
        io1 = io
        with scope(cx):
            hbuf = cx.sb("hbuf", [128, KC, T], BF16)
            xT_res = _emit_M1(cx, io1, False, hbuf=hbuf, xres=True)
            io = wts(M2_KEYS)
            io.update({"xT": XA, "out": XB, "memT": dr["memT"]})
            with scope(cx):
                _emit_M2(cx, io, xT_res=xT_res, store=False, load=False, hbuf=hbuf)
            with scope(cx):
                ds = cx.dslot("xh", sw=True)
                with nc.allow_non_contiguous_dma(reason="boundary columns"):
                    ds.start(cx.pool, XHs_v[:, :, 0:1], xT_res[:, :, 0:1])
                    e = ds.start(cx.pool, XHs_v[:, :, 1:2], xT_res[:, :, T - 1:T])
                cc = run_cc(cx, [(XHs, XHr)], e)
                for q in (cx.pool, cx.sp):
                    q.wait(cc)
            io = wts(F_KEYS)
            io.update({"xT": XB, "out": Xout, "fl": dr["fl"], "xhl": View(XHr_v[0, :, :, 1:2]), "xhr": View(XHr_v[1, :, :, 0:1])})
            with scope(cx):
                _emit_F(cx, io, l == 1, xT_res=xT_res, hbuf=hbuf)
    print('semaphores used:', cx._nsem)
    cx.close()
    return nc


_FUSED = {}


def kernel(**inp):
    inp = {k: np.asarray(v) for k, v in inp.items()}
    x = inp["x"].astype(np.float32, copy=False)
    shared = {}
    for l in range(2):
        sh, ex, pc = prep_M1(inp, l)
        d = dict(sh)
        d.update(ex)
        d.update(prep_M2(inp, l))
        d.update(prep_F(inp, l))
        bAB = d.pop("bAB")
        for k, v in d.items():
            shared["%s_%d" % (k, l)] = np.ascontiguousarray(v, dtype=np.float32)
    shared["bAB"] = bAB
    fls = core_flags()
    maps = []
    for c in range(NCORES):
        m = dict(shared)
        for k in ("icnt", "bAL", "bBR"):
            m[k] = pc[c][k]
        m["fl"] = fls[c]
        m["x"] = np.ascontiguousarray(x[c // 2, (c % 2) * T:(c % 2 + 1) * T, :].T)
        m["memT"] = np.ascontiguousarray(inp["mem"][c // 2].T)
        maps.append(m)
    if "nc" not in _FUSED:
        _FUSED["nc"] = build_fused({k: v.shape for k, v in maps[0].items()})
    res = run_bass_kernel_spmd(_FUSED["nc"], maps, core_ids=list(range(NCORES)))
    out = np.empty((BATCH, SEQ, D), np.float32)
    for c in range(NCORES):
        out[c // 2, (c % 2) * T:(c % 2 + 1) * T, :] = res.results[c]["out"].T
    return out
```

```python
import contextlib
import math
import numpy as np
import concourse.bass as bass
import concourse.mybir as mybir
from concourse.bass_utils import run_bass_kernel_spmd

F32 = mybir.dt.float32
BF16 = mybir.dt.bfloat16
AF = mybir.ActivationFunctionType
ALU = mybir.AluOpType

D = 1024
SEQ = 4096
BATCH = 4
T = 2048
NBLK = 4
BLK = 512
KC = 8
DFF = 2816
NFF = 22
EPS = 1e-6
IN_W = 2176
NEG = -1e30


_KEY = [0]


def new_key():
    _KEY[0] += 1
    return _KEY[0]


class EngQ:
    def __init__(self, cx, eng, name):
        self.key = new_key()
        self.name = name
        self.e = eng
        self.sem = cx.sem("q_" + name)
        self.n = 0
        self.seen = {}

    def wait(self, *evs):
        for ev in evs:
            if ev is None:
                continue
            if isinstance(ev, list):
                self.wait(*ev)
                continue
            sem, val, key = ev
            if self.seen.get(key, 0) >= val:
                continue
            self.seen[key] = val
            self.e.wait_ge(sem, val)

    def tick(self, ins):
        self.n += 1
        ins.then_inc(self.sem, 1)
        return (self.sem, self.n, self.key)


class DSlot:
    def __init__(self, cx, name):
        self.key = new_key()
        self.sem = cx.sem("d_" + name)
        self.n = 0

    def start(self, q, out, in_):
        assert (q.name == "pool") == bool(getattr(self, "sw", False)), "gpsimd DMAs need sw=True semaphores (and only them)"
        q.e.dma_start(out=out, in_=in_).then_inc(self.sem, 16)
        self.n += 16
        return (self.sem, self.n, self.key)


class Cx:
    def __init__(self, nc):
        self.nc = nc
        self.root = contextlib.ExitStack()
        self.st = self.root
        self._nsem = 0
        self._nsb = 0
        self.free_slots = []
        self.sw_slots = []
        self.scope_slots = [[]]
        self.scope_sw = [[]]
        self.banks8 = [self.root.enter_context(nc.psum_tensor("bank%d" % i, [128, 512], F32)) for i in range(8)]
        self.pe = EngQ(self, nc.tensor, "pe")
        self.dve = EngQ(self, nc.vector, "dve")
        self.act = EngQ(self, nc.scalar, "act")
        self.pool = EngQ(self, nc.gpsimd, "pool")
        self.sp = EngQ(self, nc.sync, "sp")

    def sem(self, name):
        self._nsem += 1
        return self.root.enter_context(self.nc.semaphore("%s_%d" % (name, self._nsem)))

    def dslot(self, name, sw=False):
        if sw:
            d = DSlot(self, name)
            d.sw = True
            self.sw_slots.append(d)
            self.scope_sw[-1].append(d)
            return d
        if self.free_slots:
            d = self.free_slots.pop()
        else:
            d = DSlot(self, name)
        d.sw = False
        self.scope_slots[-1].append(d)
        return d

    def bank(self, i):
        return self.banks8[i]

    def sb(self, name, shape, dt):
        self._nsb += 1
        return self.st.enter_context(self.nc.sbuf_tensor("s%d_%s" % (self._nsb, name), shape, dt))

    def ps(self, name, shape=(128, 512), dt=F32):
        return self.st.enter_context(self.nc.psum_tensor("p_" + name, list(shape), dt))

    def dram_in(self, name, shape, dt=F32):
        return self.nc.dram_tensor(name, list(shape), dt, kind="ExternalInput")

    def dram_out(self, name, shape, dt=F32):
        return self.nc.dram_tensor(name, list(shape), dt, kind="ExternalOutput")

    def close(self):
        self.root.close()


class Ring:
    def __init__(self, bufs):
        self.bufs = bufs
        self.rel = [None] * len(bufs)
        self.i = 0

    def acquire(self, q):
        k = self.i % len(self.bufs)
        self.i += 1
        q.wait(self.rel[k])
        self.rel[k] = None
        return k, self.bufs[k]

    def release(self, k, ev):
        if self.rel[k] is None:
            self.rel[k] = [ev]
        else:
            self.rel[k].append(ev)


def small_loads(cx, pairs):
    ds = cx.dslot("small%d" % cx._nsem)
    ev = None
    for o, i in pairs:
        ev = ds.start(cx.sp, o, i)
    return ev


class Normer:
    def __init__(self, cx, ps_bank, tag, nmax=BLK):
        self.cx = cx
        nc = cx.nc
        self.ones = cx.sb("ones_" + tag, [128, 128], BF16)
        self.sq = Ring([cx.sb("sq%d_%s" % (i, tag), [128, KC, nmax], BF16) for i in range(2)])
        self.rt = Ring([cx.sb("rt%d_%s" % (i, tag), [128, nmax], F32) for i in range(2)])
        self.bank = ps_bank
        self.bank_rel = None
        self.ones_ev = cx.pool.tick(nc.gpsimd.memset(self.ones[:], 1.0))

    def run(self, xs, g, outs, n, x_ev=None, out_wait=None):
        cx = self.cx
        nc = cx.nc
        ks, sq = self.sq.acquire(cx.act)
        cx.act.wait(x_ev)
        ev = None
        for c in range(KC):
            ev = nc.scalar.activation(out=sq[:, c, 0:n], in_=xs(c), func=AF.Square)
        sq_ev = cx.act.tick(ev)
        cx.pe.wait(sq_ev, self.ones_ev, self.bank_rel)
        for c in range(KC):
            mm = nc.tensor.matmul(self.bank[:, 0:n], self.ones[:], sq[:, c, 0:n], start=(c == 0), stop=(c == KC - 1))
        ss_ev = cx.pe.tick(mm)
        self.sq.release(ks, ss_ev)
        kr, rt = self.rt.acquire(cx.act)
        cx.act.wait(ss_ev)
        rt_ev = cx.act.tick(nc.scalar.activation(out=rt[:, 0:n], in_=self.bank[:, 0:n], func=AF.Sqrt,
                                                 bias=self.epsb[:, 0:1], scale=1.0 / D))
        self.bank_rel = rt_ev
        cx.dve.wait(rt_ev, x_ev, out_wait)
        r_ev = cx.dve.tick(nc.vector.reciprocal(out=rt[:, 0:n], in_=rt[:, 0:n]))
        cx.dve.wait(r_ev)
        for c in range(KC):
            o = nc.vector.scalar_tensor_tensor(out=outs(c), in0=xs(c), scalar=g[:, c:c + 1], in1=rt[:, 0:n],
                                               op0=ALU.mult, op1=ALU.mult)
        o_ev = cx.dve.tick(o)
        self.rt.release(kr, o_ev)
        return o_ev


def make_epsb(cx, normer):
    normer.epsb = cx.sb("epsb_%d" % cx._nsem, [128, 1], F32)
    ev = cx.pool.tick(cx.nc.gpsimd.memset(normer.epsb[:], EPS))
    cx.act.wait(ev)


FF_PARTS = [(0, 6), (6, 12), (12, 17), (17, 22)]


def _emit_F(cx, io, final, xT_res=None, hbuf=None):
    nc = cx.nc
    xT_d = io["xT"]
    fl_d = io["fl"]
    g3_d = io["g3"]
    gf_d = io["gf"]
    wup_d = io["wup"]
    bup_d = io["bup"]
    cw_d = io["cw"]
    cb_d = io["cb"]
    wd_d = io["wd"]
    bd_d = io["bd"]
    out_d = io["out"]

    xT = xT_res if xT_res is not None else cx.sb("xT", [128, KC, T], F32)
    xh = cx.sb("xh", [128, KC, 2], F32)
    hT = hbuf if hbuf is not None else cx.sb("hT", [128, KC, T], BF16)
    hTh = cx.sb("hTh", [128, KC, 2], BF16)
    fl = cx.sb("fl", [128, 2], F32)
    g3 = cx.sb("g3", [128, KC], F32)
    gf = cx.sb("gf", [128, KC], F32)
    bup = cx.sb("bup", [128, 2, NFF], F32)
    cw = cx.sb("cw", [128, 3, 2, NFF], F32)
    cb = cx.sb("cb", [128, 2, NFF], F32)
    bd = cx.sb("bd", [128, KC], F32)
    G = cx.sb("G", [128, 6, T], BF16)
    ubuf = [cx.sb("ubuf%d" % s, [128, T + 2], F32) for s in range(2)]
    cA = [Ring([cx.sb("cA%d_%d" % (s, i), [128, BLK], F32) for i in range(4)]) for s in range(2)]
    sg = Ring([cx.sb("sg%d" % i, [128, BLK], F32) for i in range(2)])
    wu = Ring([cx.sb("wu%d" % i, [128, KC, 2, 128], BF16) for i in range(2)])
    wdb = Ring([cx.sb("wdb%d" % i, [128, 6, 128], BF16) for i in range(2)])
    wu_d = [cx.dslot("wu%d" % i, sw=True) for i in range(2)]
    wd_s = [cx.dslot("wd%d" % i, sw=True) for i in range(2)]
    banks = Ring([cx.bank(i) for i in range(6)])
    dbanks = banks
    pnorm = cx.bank(6)
    phalo = cx.bank(7)

    xT_dv = xT_d.ap().rearrange("(c p) t -> p c t", p=128)
    xs = cx.dslot("x")
    x_evs = []
    for b in range(NBLK):
        if xT_res is not None:
            x_evs.append(None)
        else:
            x_evs.append(xs.start(cx.sp, xT[:, :, b * BLK:(b + 1) * BLK], xT_dv[:, :, b * BLK:(b + 1) * BLK]))
    with nc.allow_non_contiguous_dma(reason="halo columns"):
        xhs = cx.dslot("xh")
        xhs.start(cx.sp, xh[:, :, 0:1], io["xhl"].ap())
        xh_ev = xhs.start(cx.sp, xh[:, :, 1:2], io["xhr"].ap())
    c_ev = small_loads(cx, [(fl[:], fl_d.ap()), (g3[:], g3_d.ap()), (gf[:], gf_d.ap()), (bup[:], bup_d.ap()),
                            (cw[:], cw_d.ap()), (cb[:], cb_d.ap()), (bd[:], bd_d.ap())])
    for q in (cx.act, cx.dve, cx.pool):
        q.wait(c_ev)

    nm = Normer(cx, pnorm, "f")
    make_epsb(cx, nm)
    h_evs = []
    for b in range(NBLK):
        sl = slice(b * BLK, (b + 1) * BLK)
        h_evs.append(nm.run(lambda c: xT[:, c, sl], g3, lambda c: hT[:, c, sl], BLK, x_ev=x_evs[b]))
    cx.act.wait(xh_ev)
    cx.dve.wait(xh_ev)
    hh_ev = nm.run(lambda c: xh[:, c, :], g3, lambda c: hTh[:, c, :], 2, x_ev=xh_ev)

    ubuf_rel = [None, None]
    phalo_rel = None
    G_rel = None
    x_upd = [[None] * NBLK for _ in range(KC)]
    for (k0, k1) in FF_PARTS:
        for j in range(k0, k1):
            kw, wub = wu.acquire(cx.pool)
            w_ev = wu_d[kw].start(cx.pool, wub[:], wup_d.ap()[j])
            cx.pe.wait(w_ev, h_evs, hh_ev)
            conv_out = [None, None]
            for s in range(2):
                cx.pe.wait(phalo_rel)
                for c in range(KC):
                    mm = nc.tensor.matmul(phalo[:, 2 * s:2 * s + 2], wub[:, c, s, :], hTh[:, c, :], start=(c == 0), stop=(c == KC - 1))
                ph_ev = cx.pe.tick(mm)
                cx.act.wait(ubuf_rel[s])
                cx.dve.wait(ubuf_rel[s], ph_ev)
                nc.vector.tensor_scalar(out=ubuf[s][:, 0:1], in0=phalo[:, 2 * s:2 * s + 1], scalar1=bup[:, s, j:j + 1],
                                        scalar2=fl[:, 0:1], op0=ALU.add, op1=ALU.mult)
                hv = nc.vector.tensor_scalar(out=ubuf[s][:, T + 1:T + 2], in0=phalo[:, 2 * s + 1:2 * s + 2], scalar1=bup[:, s, j:j + 1],
                                             scalar2=fl[:, 1:2], op0=ALU.add, op1=ALU.mult)
                halo_ev = cx.dve.tick(hv)
                phalo_rel = halo_ev
                ev_blocks = []
                for b in range(NBLK):
                    kb, bank = banks.acquire(cx.pe)
                    for c in range(KC):
                        mm = nc.tensor.matmul(bank[:, :], wub[:, c, s, :], hT[:, c, b * BLK:(b + 1) * BLK], start=(c == 0), stop=(c == KC - 1))
                    mm_ev = cx.pe.tick(mm)
                    cx.act.wait(mm_ev)
                    e = cx.act.tick(nc.scalar.activation(out=ubuf[s][:, 1 + b * BLK:1 + (b + 1) * BLK], in_=bank[:, :], func=AF.Identity,
                                                         bias=bup[:, s, j:j + 1], scale=1.0))
                    banks.release(kb, e)
                    ev_blocks.append(e)
                if s == 1:
                    wu.release(kw, mm_ev)
                slots = []
                t0s = []
                for b in range(NBLK):
                    ka, ca = cA[s].acquire(cx.act)
                    cx.act.wait(halo_ev, ev_blocks[b])
                    t0s.append(cx.act.tick(nc.scalar.activation(out=ca[:], in_=ubuf[s][:, b * BLK:b * BLK + BLK], func=AF.Copy,
                                                                scale=cw[:, 0, s, j:j + 1])))
                    slots.append((ka, ca))
                t1s = []
                for b in range(NBLK):
                    ka, ca = slots[b]
                    cx.dve.wait(t0s[b], halo_ev)
                    t1s.append(cx.dve.tick(nc.vector.scalar_tensor_tensor(out=ca[:], in0=ubuf[s][:, 1 + b * BLK:1 + b * BLK + BLK],
                                                                          scalar=cw[:, 1, s, j:j + 1], in1=ca[:], op0=ALU.mult, op1=ALU.add)))
                outs = []
                for b in range(NBLK):
                    ka, ca = slots[b]
                    cx.dve.wait(t1s[b], ev_blocks[min(b + 1, NBLK - 1)])
                    t2 = cx.dve.tick(nc.vector.scalar_tensor_tensor(out=ca[:], in0=ubuf[s][:, 2 + b * BLK:2 + b * BLK + BLK],
                                                                    scalar=cw[:, 2, s, j:j + 1], in1=ca[:], op0=ALU.mult, op1=ALU.add))
                    outs.append((ka, ca, t2))
                ubuf_rel[s] = outs[-1][2]
                conv_out[s] = outs
            for b in range(NBLK):
                kag, cag, tg = conv_out[0][b]
                kav, cav, tv = conv_out[1][b]
                ks, sgb = sg.acquire(cx.act)
                cx.act.wait(tg)
                s_ev = cx.act.tick(nc.scalar.activation(out=sgb[:], in_=cag[:], func=AF.Silu, bias=cb[:, 0, j:j + 1], scale=1.0))
                cA[0].release(kag, s_ev)
                cx.dve.wait(s_ev, tv, G_rel)
                g_ev = cx.dve.tick(nc.vector.scalar_tensor_tensor(out=G[:, j - k0, b * BLK:(b + 1) * BLK], in0=cav[:], scalar=cb[:, 1, j:j + 1],
                                                                  in1=sgb[:], op0=ALU.add, op1=ALU.mult))
                cA[1].release(kav, g_ev)
                sg.release(ks, g_ev)
            G_ev = g_ev
        nk = k1 - k0
        for dc in range(KC):
            kd, wb = wdb.acquire(cx.pool)
            w_ev = wd_s[kd].start(cx.pool, wb[:, 0:nk, :], wd_d.ap()[dc, :, k0:k1, :])
            cx.pe.wait(w_ev, G_ev)
            for b in range(NBLK):
                kb, bank = dbanks.acquire(cx.pe)
                for k in range(nk):
                    mm = nc.tensor.matmul(bank[:, :], wb[:, k, :], G[:, k, b * BLK:(b + 1) * BLK], start=(k == 0), stop=(k == nk - 1))
                mm_ev = cx.pe.tick(mm)
                cx.dve.wait(mm_ev, x_upd[dc][b])
                xs_ = xT[:, dc, b * BLK:(b + 1) * BLK]
                if k0 == 0:
                    ins = nc.vector.scalar_tensor_tensor(out=xs_, in0=bank[:, :], scalar=bd[:, dc:dc + 1], in1=xs_, op0=ALU.add, op1=ALU.add)
                else:
                    ins = nc.vector.tensor_tensor(out=xs_, in0=bank[:, :], in1=xs_, op=ALU.add)
                e = cx.dve.tick(ins)
                x_upd[dc][b] = e
                dbanks.release(kb, e)
            wdb.release(kd, mm_ev)
        G_rel = mm_ev

    os_ = cx.dslot("o")
    out_dv = out_d.ap().rearrange("(c p) t -> p c t", p=128)
    o_ev = None
    if final:
        for b in range(NBLK):
            sl = slice(b * BLK, (b + 1) * BLK)
            e = nm.run(lambda c: xT[:, c, sl], gf, lambda c: xT[:, c, sl], BLK, x_ev=[x_upd[c][b] for c in range(KC)])
            cx.sp.wait(e)
            o_ev = os_.start(cx.sp, out_dv[:, :, sl], xT[:, :, sl])
    else:
        for b in range(NBLK):
            sl = slice(b * BLK, (b + 1) * BLK)
            cx.sp.wait([x_upd[c][b] for c in range(KC)])
            o_ev = os_.start(cx.sp, out_dv[:, :, sl], xT[:, :, sl])
    cx.sp.wait(o_ev)
    return


def colvec(v):
    v = np.asarray(v, np.float32)
    return np.ascontiguousarray(v.reshape(-1, 128).T)


def wchunks(w):
    K, N = w.shape
    return np.ascontiguousarray(w.reshape(K // 128, 128, N // 128, 128).transpose(2, 1, 0, 3))


def core_flags():
    out = []
    for cid in range(8):
        s = cid % 2
        f = np.zeros((128, 2), np.float32)
        f[:, 0] = 1.0 if s == 1 else 0.0
        f[:, 1] = 1.0 if s == 0 else 0.0
        out.append(f)
    return out


def prep_F(inp, l):
    wup = inp["ffn_w_up"][l]
    wu = wup.reshape(KC, 128, 2, NFF, 128).transpose(3, 1, 0, 2, 4)
    bup = inp["ffn_b_up"][l].reshape(2, NFF, 128).transpose(2, 0, 1)
    cw = inp["ffn_conv_w"][l].reshape(3, 2, NFF, 128).transpose(3, 0, 1, 2)
    cb = inp["ffn_conv_b"][l].reshape(2, NFF, 128).transpose(2, 0, 1)
    wd = inp["ffn_w_down"][l].reshape(NFF, 128, KC, 128).transpose(2, 1, 0, 3)
    return {
        "g3": colvec(inp["norm_ffn_g"][l]), "gf": colvec(inp["final_norm_g"]),
        "wup": np.ascontiguousarray(wu), "bup": np.ascontiguousarray(bup), "cw": np.ascontiguousarray(cw),
        "cb": np.ascontiguousarray(cb), "wd": np.ascontiguousarray(wd), "bd": colvec(inp["ffn_b_down"][l]),
    }


def proj_fm(cx, w_src, nj, kcin, rhs, evac, wring, wslots, banks, rhs_ev, nblk=NBLK, wshape=None):
    nc = cx.nc
    mm_ev = None
    for j in range(nj):
        kw, wb = wring.acquire(cx.pool)
        w_ev = wslots[kw].start(cx.pool, wb[:, 0:kcin, :], w_src(j))
        cx.pe.wait(w_ev, rhs_ev)
        for b in range(nblk):
            kb, bank = banks.acquire(cx.pe)
            for c in range(kcin):
                mm = nc.tensor.matmul(bank[:, :], wb[:, c, :], rhs(c, b), start=(c == 0), stop=(c == kcin - 1))
            mm_ev = cx.pe.tick(mm)
            banks.release(kb, evac(j, b, bank, mm_ev))
        wring.release(kw, mm_ev)
    return mm_ev


MEM = 256


def _emit_M2(cx, io, xT_res=None, store=True, load=True, hbuf=None):
    nc = cx.nc
    xT_d = io["xT"]
    memT_d = io["memT"]
    gm_d = io["gm"]
    g2_d = io["g2"]
    wq_d = io["wq"]
    wk_d = io["wk"]
    wv_d = io["wv"]
    wo_d = io["wo"]
    bo_d = io["bo"]
    out_d = io["out"]

    xT = xT_res if xT_res is not None else cx.sb("xT", [128, KC, T], F32)
    memT = cx.sb("memT", [128, KC, MEM], F32)
    mnT = cx.sb("mnT", [128, KC, MEM], BF16)
    gm = cx.sb("gm", [128, KC], F32)
    g2 = cx.sb("g2", [128, KC], F32)
    bo = cx.sb("bo", [128, KC], F32)
    hT = hbuf if hbuf is not None else cx.sb("hT", [128, KC, T], BF16)
    QT = cx.sb("QT", [128, KC, T], BF16)
    KxT = cx.sb("KxT", [128, KC, MEM], BF16)
    Vx = cx.sb("Vx", [128, 2, D], BF16)
    wv = cx.sb("wv", [128, KC, D], BF16)
    PT = Ring([cx.sb("PT%d" % i, [128, 2, BLK], BF16) for i in range(2)])
    rc = Ring([cx.sb("rc%d" % i, [128, BLK], F32) for i in range(2)])
    wring = Ring([cx.sb("w%d" % i, [128, KC, 128], BF16) for i in range(2)])
    wslots = [cx.dslot("w%d" % i, sw=True) for i in range(2)]
    banks = Ring([cx.bank(i) for i in range(4)])
    pden = Ring([cx.bank(4)])
    pov = Ring([cx.bank(5), cx.bank(6)])
    pnorm = cx.bank(7)

    xT_dv = xT_d.ap().rearrange("(c p) t -> p c t", p=128)
    xs = cx.dslot("x")
    m_ev = cx.dslot("mem").start(cx.sp, memT[:], memT_d.ap().rearrange("(c p) t -> p c t", p=128))
    x_evs = [xs.start(cx.sp, xT[:, :, b * BLK:(b + 1) * BLK], xT_dv[:, :, b * BLK:(b + 1) * BLK]) if load else None for b in range(NBLK)]
    c_ev = small_loads(cx, [(gm[:], gm_d.ap()), (g2[:], g2_d.ap()), (bo[:], bo_d.ap())])
    wv_ev = cx.dslot("wv", sw=True).start(cx.pool, wv[:], wv_d.ap())
    for q in (cx.act, cx.dve):
        q.wait(c_ev)
    nm = Normer(cx, pnorm, "m")
    make_epsb(cx, nm)
    ones = nm.ones

    mn_ev = nm.run(lambda c: memT[:, c, :], gm, lambda c: mnT[:, c, :], MEM, x_ev=m_ev)

    def evac_k(j, b, bank, mm_ev):
        cx.act.wait(mm_ev)
        return cx.act.tick(nc.scalar.copy(out=KxT[:, j, :], in_=bank[:, 0:MEM]))

    nck = None
    for j in range(KC):
        kw, wb = wring.acquire(cx.pool)
        w_ev = wslots[kw].start(cx.pool, wb[:], wk_d.ap()[j])
        cx.pe.wait(w_ev, mn_ev)
        kb, bank = banks.acquire(cx.pe)
        for c in range(KC):
            mm = nc.tensor.matmul(bank[:, 0:MEM], wb[:, c, :], mnT[:, c, :], start=(c == 0), stop=(c == KC - 1))
        mm_ev = cx.pe.tick(mm)
        k_ev = evac_k(j, 0, bank, mm_ev)
        banks.release(kb, k_ev)
        wring.release(kw, mm_ev)
    cx.pe.wait(wv_ev)
    for mt in range(2):
        for nh in range(2):
            kb, bank = banks.acquire(cx.pe)
            for c in range(KC):
                mm = nc.tensor.matmul(bank[:, :], mnT[:, c, mt * 128:(mt + 1) * 128], wv[:, c, nh * 512:(nh + 1) * 512],
                                      start=(c == 0), stop=(c == KC - 1))
            mm_ev = cx.pe.tick(mm)
            cx.act.wait(mm_ev)
            v_ev = cx.act.tick(nc.scalar.copy(out=Vx[:, mt, nh * 512:(nh + 1) * 512], in_=bank[:, :]))
            banks.release(kb, v_ev)

    h_evs = []
    for b in range(NBLK):
        sl = slice(b * BLK, (b + 1) * BLK)
        h_evs.append(nm.run(lambda c: xT[:, c, sl], g2, lambda c: hT[:, c, sl], BLK, x_ev=x_evs[b]))

    def evac_q(j, b, bank, mm_ev):
        cx.act.wait(mm_ev)
        return cx.act.tick(nc.scalar.copy(out=QT[:, j, b * BLK:(b + 1) * BLK], in_=bank[:, :]))

    q_pe = proj_fm(cx, lambda j: wq_d.ap()[j], KC, KC, lambda c, b: hT[:, c, b * BLK:(b + 1) * BLK], evac_q, wring, wslots, banks, h_evs)
    q_ev = (cx.act.sem, cx.act.n, cx.act.key)
    oT = hT
    o_evs = []
    for h in range(4):
        for b in range(NBLK):
            sl = slice(b * BLK, (b + 1) * BLK)
            kp, pt = PT.acquire(cx.act)
            for mt in range(2):
                kb, bank = banks.acquire(cx.pe)
                cx.pe.wait(q_ev, k_ev)
                for hf in range(2):
                    mm = nc.tensor.matmul(bank[:, :], KxT[:, 2 * h + hf, mt * 128:(mt + 1) * 128], QT[:, 2 * h + hf, sl],
                                          start=(hf == 0), stop=(hf == 1))
                mm_ev = cx.pe.tick(mm)
                cx.act.wait(mm_ev)
                p_ev = cx.act.tick(nc.scalar.activation(out=pt[:, mt, :], in_=bank[:, :], func=AF.Exp, scale=1.0 / 16.0))
                banks.release(kb, p_ev)
            cx.pe.wait(p_ev, v_ev)
            kd, dbank = pden.acquire(cx.pe)
            for mt in range(2):
                mm = nc.tensor.matmul(dbank[:, :], ones[:], pt[:, mt, :], start=(mt == 0), stop=(mt == 1))
            d_ev = cx.pe.tick(mm)
            kr, rcb = rc.acquire(cx.dve)
            cx.dve.wait(d_ev)
            r_ev = cx.dve.tick(nc.vector.reciprocal(out=rcb[:], in_=dbank[:, :]))
            pden.release(kd, r_ev)
            for hf in range(2):
                ko, obank = pov.acquire(cx.pe)
                if h == 0 and b == 0 and hf == 0:
                    cx.pe.wait(q_pe)
                for mt in range(2):
                    mm = nc.tensor.matmul(obank[:, :], Vx[:, mt, (2 * h + hf) * 128:(2 * h + hf + 1) * 128], pt[:, mt, :],
                                          start=(mt == 0), stop=(mt == 1))
                o_mm = cx.pe.tick(mm)
                cx.dve.wait(o_mm, r_ev)
                o_ev = cx.dve.tick(nc.vector.tensor_tensor(out=oT[:, 2 * h + hf, sl], in0=obank[:, :], in1=rcb[:], op=ALU.mult))
                pov.release(ko, o_ev)
            PT.release(kp, o_mm)
            rc.release(kr, o_ev)
            o_evs.append(o_ev)

    os_ = cx.dslot("o")
    out_dv = out_d.ap().rearrange("(c p) t -> p c t", p=128)
    st_evs = []

    def evac_o(j, b, bank, mm_ev):
        cx.dve.wait(mm_ev)
        xs_ = xT[:, j, b * BLK:(b + 1) * BLK]
        e = cx.dve.tick(nc.vector.scalar_tensor_tensor(out=xs_, in0=bank[:, :], scalar=bo[:, j:j + 1], in1=xs_, op0=ALU.add, op1=ALU.add))
        if store:
            cx.sp.wait(e)
            st_evs.append(os_.start(cx.sp, out_dv[:, j, b * BLK:(b + 1) * BLK], xs_))
        return e

    proj_fm(cx, lambda j: wo_d.ap()[j], KC, KC, lambda c, b: oT[:, c, b * BLK:(b + 1) * BLK], evac_o, wring, wslots, banks, o_evs)
    if store:
        cx.sp.wait(st_evs[-1])
    return


def prep_M2(inp, l):
    wkv = inp["xattn_w_kv"][l]
    return {
        "gm": colvec(inp["mem_norm_g"]), "g2": colvec(inp["norm_mem_g"][l]),
        "wq": wchunks(inp["xattn_w_q"][l]), "wk": wchunks(wkv[:, :D]),
        "wv": np.ascontiguousarray(wkv[:, D:].reshape(KC, 128, D).transpose(1, 0, 2)),
        "wo": wchunks(inp["xattn_w_o"][l]), "bo": colvec(inp["xattn_b_o"][l]),
    }


AX = mybir.AxisListType
GD = [1, 4, 16]
GOFF = [0, 64, 320]
GTOFF = [0, 1, 5]
NH = 1344
MASKV = -1.0e4


def sts(start, n, step):
    return slice(start, start + (n - 1) * step + 1, step)


def barrier(cx):
    qs = (cx.pe, cx.act, cx.dve, cx.pool)
    evs = [(q.sem, q.n, q.key) for q in qs if q.n > 0]
    devs = [(d.sem, d.n, d.key) for d in cx.scope_slots[-1] + cx.scope_sw[-1] if d.n > 0]
    for q in (cx.pe, cx.act, cx.dve, cx.pool, cx.sp):
        q.wait([e for e in evs if e[2] != q.key] + devs)


@contextlib.contextmanager
def scope(cx):
    old = cx.st
    cx.st = contextlib.ExitStack()
    cx.scope_slots.append([])
    cx.scope_sw.append([])
    try:
        yield
    finally:
        barrier(cx)
        cx.st.close()
        cx.st = old
        cx.free_slots.extend(cx.scope_slots.pop())
        cx.scope_sw.pop()


class View:
    def __init__(self, ap):
        self._ap = ap

    def ap(self):
        return self._ap


def _emit_M1(cx, io, p_only, dbg_stop=None, dbg_att=None, hbuf=None, xres=False):
    nc = cx.nc
    xT_d = io["xT"]
    g1_d = io["g1"]
    win_d = io["win"]
    wvC_d = io["wvC"]
    bin_d = io["bin"]
    bvC_d = io["bvC"]
    if p_only:
        KTb_d = io["KTb"]
        Vb_d = io["Vb"]
        zbb_d = io["zbb"]
    else:
        wvA_d = io["wvA"]
        bvA_d = io["bvA"]
        vgain_d = io["vgain"]
        wsT_d = io["wsT"]
        bsT_d = io["bsT"]
        wpbd_d = io["wpbd"]
        pb_d = io["pb"]
        psc_d = io["psc"]
        icnt_d = io["icnt"]
        wout_d = io["wout"]
        bout_d = io["bout"]
        bAB_d = io["bAB"]
        bAL_d = io["bAL"]
        bBR_d = io["bBR"]
        fl_d = io["fl"]
        out_d = io["out"]
    hT = hbuf if hbuf is not None else cx.sb("hT", [128, KC, T], BF16)
    inner = None
    if xres:
        inner = scope(cx)
        inner.__enter__()
    zbT = cx.sb("zbT", [128, 2, T + 16], F32)
    KT = [cx.sb("KT%d" % g, [128, GD[g], T // GD[g] + 128], BF16) for g in range(3)]
    V = [cx.sb("V%d" % g, [128, 16, 128], BF16) for g in range(3)]
    g1 = cx.sb("g1", [128, KC], F32)
    binc = cx.sb("binc", [128, 17], F32)
    bvC = cx.sb("bvC", [128, 384], F32)
    loads = [(g1[:], g1_d.ap()), (binc[:], bin_d.ap()), (bvC[:], bvC_d.ap().partition_broadcast(128))]
    if not p_only:
        uT = cx.sb("uT", [128, 3, T], BF16)
        vn = cx.sb("vn", [128, 16, 384], BF16)
        QT = [cx.sb("QT%d" % g, [128, GD[g], T // GD[g]], BF16) for g in range(3)]
        bvA = cx.sb("bvA", [128, 384], F32)
        vgain = cx.sb("vgain", [128, 384], F32)
        bsT = cx.sb("bsT", [128, 3, 128], F32)
        pbc = cx.sb("pbc", [128, 2], F32)
        psc = cx.sb("psc", [128, 2], F32)
        icnt = cx.sb("icnt", [128, 2, 16], F32)
        bout = cx.sb("bout", [128, KC], F32)
        flm = cx.sb("flm", [128, 2], F32)
        wsT = cx.sb("wsT", [128, 6, 128], BF16)
        wpbd = cx.sb("wpbd", [128, 2, 128], BF16)
        loads += [(bvA[:], bvA_d.ap().partition_broadcast(128)), (vgain[:], vgain_d.ap().partition_broadcast(128)),
                  (bsT[:], bsT_d.ap()), (pbc[:], pb_d.ap()), (psc[:], psc_d.ap()), (icnt[:], icnt_d.ap()), (bout[:], bout_d.ap()), (flm[:], fl_d.ap())]
    c_ev = small_loads(cx, loads)
    for q in (cx.act, cx.dve):
        q.wait(c_ev)
    banks = Ring([cx.bank(i) for i in range(6)])
    pnorm = cx.bank(6)
    xT_dv = xT_d.ap().rearrange("(c p) t -> p c t", p=128)
    wsl = cx.dslot("wres", sw=True)
    NB2 = 256

    with scope(cx):
        nm = Normer(cx, pnorm, "n1", nmax=NB2)
        make_epsb(cx, nm)
        arena = cx.sb("arena", [128, 4 * KC * NB2], F32)
        xring = Ring([arena[:, i * KC * NB2:(i + 1) * KC * NB2].rearrange("p (c t) -> p c t", c=KC) for i in range(4)])
        xsl = [cx.dslot("xr%d" % i) for i in range(4)]
        wring = Ring([cx.sb("w%d" % i, [128, KC, 128], BF16) for i in range(2)])
        wslots = [cx.dslot("w%d" % i, sw=True) for i in range(2)]
        wvC = cx.sb("wvC", [128, KC, 384], BF16)
        wvC_ev = wsl.start(cx.pool, wvC[:], wvC_d.ap())
        if not p_only:
            wvA = cx.sb("wvA", [128, KC, 384], BF16)
            wsl.start(cx.pool, wvA[:], wvA_d.ap())
            wsl.start(cx.pool, wsT[:], wsT_d.ap())
            wvA_ev = wsl.start(cx.pool, wpbd[:], wpbd_d.ap())
            wvC_ev = wvA_ev
        h_evs = []
        for bb in range(T // NB2):
            kx, xr = xring.acquire(cx.sp)
            x_ev = xsl[kx].start(cx.sp, xr, xT_dv[:, :, bb * NB2:(bb + 1) * NB2])
            h_ev = nm.run(lambda c: xr[:, c, :], g1, lambda c: hT[:, c, bb * NB2:(bb + 1) * NB2], NB2, x_ev=x_ev)
            xring.release(kx, h_ev)
            h_evs.append(h_ev)

        jls = [[11, 12, 13, 6, 7]] if p_only else [[11, 12, 13, 6, 7], [0, 1, 2, 8, 9, 10]]
        cur = {"jl": jls[0]}
        fm_evs = []

        def evac_in(i, b, bank, mm_ev):
            j = cur["jl"][i]
            sl = slice(b * BLK, (b + 1) * BLK)
            cx.act.wait(mm_ev)
            bias = binc[:, j:j + 1]
            if j < 3:
                ins = nc.scalar.activation(out=uT[:, j, sl], in_=bank[:, :], func=AF.Gelu_apprx_tanh, bias=bias, scale=1.0)
            elif j < 8:
                ins = nc.scalar.activation(out=zbT[:, j - 6, 8 + b * BLK:8 + (b + 1) * BLK], in_=bank[:, :], func=AF.Identity, bias=bias, scale=1.0)
            elif j < 11:
                g = j - 8
                mb = BLK // GD[g]
                ins = nc.scalar.activation(out=QT[g][:, :, b * mb:(b + 1) * mb].rearrange("p r m -> p m r"),
                                           in_=bank[:, :].rearrange("p (m r) -> p m r", r=GD[g]), func=AF.Identity, bias=bias, scale=1.0)
            else:
                g = j - 11
                mb = BLK // GD[g]
                ins = nc.scalar.activation(out=KT[g][:, :, 64 + b * mb:64 + (b + 1) * mb].rearrange("p r m -> p m r"),
                                           in_=bank[:, :].rearrange("p (m r) -> p m r", r=GD[g]), func=AF.Identity, bias=bias, scale=1.0)
            e = cx.act.tick(ins)
            fm_evs.append(e)
            return e

        proj_fm(cx, lambda i: win_d.ap()[cur["jl"][i]], len(cur["jl"]), KC, lambda c, b: hT[:, c, b * BLK:(b + 1) * BLK], evac_in,
                wring, wslots, banks, h_evs)

        v_evs = []
        for g in range(3):
            d = GD[g]
            nt = 16 // d
            tiles = [(r, t) for r in range(d) for t in range(nt)]
            for q4 in range(4):
                kb, bank = banks.acquire(cx.pe)
                cx.pe.wait(wvC_ev, h_evs)
                for i in range(4):
                    r, t = tiles[q4 * 4 + i]
                    s0 = r + d * 128 * t
                    for c in range(KC):
                        mm = nc.tensor.matmul(bank[:, i * 128:(i + 1) * 128], hT[:, c, sts(s0, 128, d)], wvC[:, c, g * 128:(g + 1) * 128],
                                              start=(c == 0), stop=(c == KC - 1))
                mm_ev = cx.pe.tick(mm)
                cx.dve.wait(mm_ev)
                e = cx.dve.tick(nc.vector.tensor_tensor(out=V[g][:, q4 * 4:q4 * 4 + 4, :], in0=bank[:, :].rearrange("p (a n) -> p a n", n=128),
                                                        in1=bvC[:, g * 128:(g + 1) * 128].unsqueeze(1).to_broadcast([128, 4, 128]), op=ALU.add))
                banks.release(kb, e)
                v_evs.append(e)

        cc_evs = None
        if "exp" in io:
            ex = io["exp"]
            es = cx.dslot("exp")
            cx.sp.wait(fm_evs, v_evs)
            with nc.allow_non_contiguous_dma(reason="boundary export"):
                for g in range(3):
                    d = GD[g]
                    nt = 16 // d
                    L = T // d
                    es.start(cx.sp, ex["K"](0, g), KT[g][:, :, 64:128])
                    es.start(cx.sp, ex["K"](1, g), KT[g][:, :, L:L + 64])
                    es.start(cx.sp, ex["V"](0, g), V[g][0:64, 0:16:nt, :])
                    es.start(cx.sp, ex["V"](1, g), V[g][64:128, nt - 1:16:nt, :])
                es.start(cx.sp, ex["zb"](0), zbT[:, :, 8:16])
                e_last = es.start(cx.sp, ex["zb"](1), zbT[:, :, T:T + 8])
        if not p_only:
            cur["jl"] = jls[1]
            proj_fm(cx, lambda i: win_d.ap()[cur["jl"][i]], len(cur["jl"]), KC, lambda c, b: hT[:, c, b * BLK:(b + 1) * BLK], evac_in,
                    wring, wslots, banks, h_evs)
        if "exp" in io:
            cc_evs = ex["run"](cx, e_last)
        if not p_only:
            cx.dve.wait(h_evs)
            cx.act.wait(h_evs)
            vtr = Ring([(arena[:, (2 * i) * 1536:(2 * i + 1) * 1536].rearrange("p (a n) -> p a n", a=4),
                         arena[:, (2 * i + 1) * 1536:(2 * i + 2) * 1536].rearrange("p (a n) -> p a n", a=4),
                         cx.sb("v6%d" % i, [128, 24], F32)) for i in range(2)])
            vn_evs = []
            for q4 in range(4):
                kv, (vt, vg, v6) = vtr.acquire(cx.dve)
                for i in range(4):
                    t = q4 * 4 + i
                    kb, bank = banks.acquire(cx.pe)
                    cx.pe.wait(wvA_ev, h_evs)
                    for c in range(KC):
                        mm = nc.tensor.matmul(bank[:, 0:384], hT[:, c, t * 128:(t + 1) * 128], wvA[:, c, :], start=(c == 0), stop=(c == KC - 1))
                    mm_ev = cx.pe.tick(mm)
                    cx.dve.wait(mm_ev)
                    e1 = cx.dve.tick(nc.vector.tensor_tensor(out=vt[:, i, :], in0=bank[:, 0:384], in1=bvA[:], op=ALU.add))
                    banks.release(kb, e1)
                cx.act.wait(e1)
                e2 = cx.act.tick(nc.scalar.activation(out=vg[:], in_=vt[:], func=AF.Gelu_apprx_tanh))
                cx.act.wait(e2)
                e3 = cx.act.tick(nc.scalar.activation(out=vt[:], in_=vg[:], func=AF.Square))
                cx.dve.wait(e3)
                e4 = cx.dve.tick(nc.vector.tensor_reduce(out=v6[:], in_=vt[:].rearrange("p a (h e) -> p (a h) e", e=64), axis=AX.X, op=ALU.add))
                cx.act.wait(e4)
                e5 = cx.act.tick(nc.scalar.activation(out=v6[:], in_=v6[:], func=AF.Sqrt, bias=nm.epsb[:, 0:1], scale=1.0 / 64))
                cx.dve.wait(e5)
                e6 = cx.dve.tick(nc.vector.reciprocal(out=v6[:], in_=v6[:]))
                cx.dve.wait(e6)
                e7 = cx.dve.tick(nc.vector.tensor_tensor(out=vt[:].rearrange("p a (h e) -> p (a h) e", e=64), in0=vg[:].rearrange("p a (h e) -> p (a h) e", e=64),
                                                         in1=v6[:].unsqueeze(2).to_broadcast([128, 24, 64]), op=ALU.mult))
                cx.dve.wait(e7)
                e8 = cx.dve.tick(nc.vector.tensor_tensor(out=vn[:, q4 * 4:q4 * 4 + 4, :], in0=vt[:], in1=vgain[:].unsqueeze(1).to_broadcast([128, 4, 384]), op=ALU.mult))
                vtr.release(kv, e8)
                vn_evs.append(e8)

        if p_only:
            os_ = cx.dslot("o", sw=True)
            cx.pool.wait(fm_evs, v_evs)
            for g in range(3):
                d = GD[g]
                nt = 16 // d
                L = T // d
                os_.start(cx.pool, KTb_d.ap()[:, 0, GOFF[g]:GOFF[g] + 64 * d].rearrange("p (r m) -> p r m", m=64), KT[g][:, :, 64:128])
                os_.start(cx.pool, KTb_d.ap()[:, 1, GOFF[g]:GOFF[g] + 64 * d].rearrange("p (r m) -> p r m", m=64), KT[g][:, :, L:L + 64])
                os_.start(cx.pool, Vb_d.ap()[0, :, GTOFF[g]:GTOFF[g] + d, :], V[g][0:64, 0:16:nt, :])
                os_.start(cx.pool, Vb_d.ap()[1, :, GTOFF[g]:GTOFF[g] + d, :], V[g][64:128, nt - 1:16:nt, :])
            cx.sp.wait(fm_evs)
            o2 = cx.dslot("o2")
            with nc.allow_non_contiguous_dma(reason="small boundary"):
                o2.start(cx.sp, zbb_d.ap()[:, :, 0:8], zbT[:, :, 8:16])
                e_o2 = o2.start(cx.sp, zbb_d.ap()[:, :, 8:16], zbT[:, :, T:T + 8])
            cx.sp.wait(e_o2)
            cx.pool.wait((os_.sem, os_.n, os_.key))

    if p_only:
        return

    if dbg_stop == 1:
        return
    if cc_evs is not None:
        cx.pool.wait(cc_evs)
        cx.sp.wait(cc_evs)
    hs = cx.dslot("halo", sw=True)
    for g in range(3):
        d = GD[g]
        L = T // d
        hs.start(cx.pool, KT[g][:, :, 0:64], io["KThL"].ap()[:, GOFF[g]:GOFF[g] + 64 * d].rearrange("p (r m) -> p r m", m=64))
        hs.start(cx.pool, KT[g][:, :, 64 + L:128 + L], io["KThR"].ap()[:, GOFF[g]:GOFF[g] + 64 * d].rearrange("p (r m) -> p r m", m=64))
    kh_ev = (hs.sem, hs.n, hs.key)
    h2 = cx.dslot("halo2")
    with nc.allow_non_contiguous_dma(reason="small halo"):
        h2.start(cx.sp, zbT[:, :, 0:8], io["zbhL"].ap())
        zh_ev = h2.start(cx.sp, zbT[:, :, T + 8:T + 16], io["zbhR"].ap())
    yT = hT

    with scope(cx):
        pq = cx.dve
        pv = nc.vector
        VhL = cx.sb("VhL", [64, 21, 128], BF16)
        VhR = cx.sb("VhR", [64, 21, 128], BF16)
        hs2 = cx.dslot("vh", sw=True)
        hs2.start(cx.pool, VhL[:], io["VhL"].ap())
        vh_ev = hs2.start(cx.pool, VhR[:], io["VhR"].ap())
        gt = Ring([cx.sb("gt%d" % i, [128, 512], F32) for i in range(2)])
        for hp in range(3):
            for t4 in range(4):
                kb, bank = banks.acquire(cx.pe)
                cx.pe.wait(vn_evs, wvA_ev)
                for i in range(4):
                    t = t4 * 4 + i
                    for hh in range(2):
                        h = 2 * hp + hh
                        mm = nc.tensor.matmul(bank[hh * 64:(hh + 1) * 64, i * 128:(i + 1) * 128], vn[:, t, h * 64:(h + 1) * 64], wsT[:, h, :],
                                              start=True, stop=True)
                mm_ev = cx.pe.tick(mm)
                kg, gtb = gt.acquire(cx.dve)
                cx.dve.wait(mm_ev)
                e1 = cx.dve.tick(nc.vector.tensor_tensor(out=gtb[:].rearrange("p (a n) -> p a n", n=128), in0=bank[:, :].rearrange("p (a n) -> p a n", n=128),
                                                         in1=bsT[:, hp, :].unsqueeze(1).to_broadcast([128, 4, 128]), op=ALU.add))
                banks.release(kb, e1)
                cx.dve.wait(e1, fm_evs)
                e2 = cx.dve.tick(nc.vector.tensor_tensor(out=yT[:, hp, t4 * 512:(t4 + 1) * 512], in0=gtb[:], in1=uT[:, hp, t4 * 512:(t4 + 1) * 512], op=ALU.mult))
                gt.release(kg, e2)

        if dbg_stop == 2:
            barrier(cx)
            return
        pa = cx.sb("pa", [128, T + 16], F32)
        pbuf = cx.sb("pbuf", [128, T + 16], F32)
        pooled = cx.sb("pooled", [128, 2, T], BF16)
        W = T + 16
        pool_ops = []

        def dv(thunk):
            pool_ops.append(thunk)

        def run_pool_ops(n):
            for _ in range(n):
                if not pool_ops:
                    return
                if not pool_started:
                    pool_started.append(1)
                    pq.wait(zh_ev, fm_evs, c_ev)
                e = pq.tick(pool_ops.pop(0)())
                pq.wait(e)

        pool_started = []
        dv(lambda: pv.tensor_scalar(out=zbT[:, :, 0:8], in0=zbT[:, :, 0:8], scalar1=flm[:, 0:1], scalar2=None, op0=ALU.mult))
        dv(lambda: pv.tensor_scalar(out=zbT[:, :, T + 8:T + 16], in0=zbT[:, :, T + 8:T + 16], scalar1=flm[:, 1:2], scalar2=None, op0=ALU.mult))

        def pool_out(src, p0, ch, w):
            ps_ = slice(p0, p0 + 64)
            o = 8 - w // 2
            dv(lambda: pv.tensor_scalar(out=pa[ps_, 0:T], in0=src[ps_, o:o + T], scalar1=1.0 / w, scalar2=None, op0=ALU.mult))
            dv(lambda: pv.tensor_tensor(out=pooled[ps_, ch, :], in0=pa[ps_, 0:T], in1=zbT[ps_, ch, 8:8 + T], op=ALU.subtract))
            for (c0, k0) in ((0, 0), (T - 8, 8)):
                dv(lambda c0=c0, k0=k0: pv.tensor_tensor(out=pa[ps_, 0:8], in0=src[ps_, o + c0:o + c0 + 8], in1=icnt[ps_, ch, k0:k0 + 8], op=ALU.mult))
                dv(lambda c0=c0, k0=k0: pv.tensor_tensor(out=pooled[ps_, ch, c0:c0 + 8], in0=pa[ps_, 0:8], in1=zbT[ps_, ch, 8 + c0:16 + c0], op=ALU.subtract))

        dv(lambda: pv.tensor_tensor(out=pbuf[:, 0:W - 1], in0=zbT[:, 0, 0:W - 1], in1=zbT[:, 0, 1:W], op=ALU.add))
        dv(lambda: pv.tensor_tensor(out=pa[64:128, 8:W - 3], in0=pbuf[64:128, 8:W - 3], in1=pbuf[64:128, 10:W - 1], op=ALU.add))
        pool_out(pbuf, 0, 0, 2)
        dv(lambda: pv.tensor_tensor(out=pa[64:128, 0:W - 3], in0=pbuf[64:128, 0:W - 3], in1=pbuf[64:128, 2:W - 1], op=ALU.add))
        dv(lambda: pv.tensor_copy(out=pbuf[64:128, 0:W - 3], in_=pa[64:128, 0:W - 3]))
        pool_out(pbuf, 64, 0, 4)
        dv(lambda: pv.tensor_tensor(out=pa[:, 0:W - 1], in0=zbT[:, 1, 0:W - 1], in1=zbT[:, 1, 1:W], op=ALU.add))
        dv(lambda: pv.tensor_tensor(out=pbuf[:, 0:W - 3], in0=pa[:, 0:W - 3], in1=pa[:, 2:W - 1], op=ALU.add))
        dv(lambda: pv.tensor_tensor(out=pa[:, 0:W - 7], in0=pbuf[:, 0:W - 7], in1=pbuf[:, 4:W - 3], op=ALU.add))
        dv(lambda: pv.tensor_tensor(out=pbuf[64:128, 0:W - 15], in0=pa[64:128, 0:W - 15], in1=pa[64:128, 8:W - 7], op=ALU.add))
        dv(lambda: pv.tensor_copy(out=pbuf[0:64, 0:W - 7], in_=pa[0:64, 0:W - 7]))
        pool_out(pbuf, 0, 1, 8)
        pe_pool = pool_out(pbuf, 64, 1, 16)
        acc = cx.sb("acc", [128, 2, T], F32)
        EAB = cx.sb("EAB", [128, 6, 2, 128], F32)
        EAL = cx.sb("EAL", [64, 6, 64], F32)
        EBR = cx.sb("EBR", [64, 6, 64], F32)
        ones = cx.sb("ones_a", [128, 64], BF16)
        pex = Ring([cx.sb("pex%d" % i, [128, 128], F32) for i in range(4)])
        PTr = Ring([cx.sb("PT%d" % i, [128, 2, 128], BF16) for i in range(4)])
        b_ev = small_loads(cx, [(EAB[:], bAB_d.ap()), (EAL[:], bAL_d.ap()), (EBR[:], bBR_d.ap())])
        cx.act.wait(b_ev)
        nc.scalar.activation(out=EAB[:], in_=EAB[:], func=AF.Exp)
        nc.scalar.activation(out=EAL[:], in_=EAL[:], func=AF.Exp)
        eb_ev = cx.act.tick(nc.scalar.activation(out=EBR[:], in_=EBR[:], func=AF.Exp))
        on_ev = cx.dve.tick(nc.vector.memset(ones[:], 1.0))
        cx.dve.wait(eb_ev)
        cx.pe.wait(on_ev, vh_ev, kh_ev, fm_evs, v_evs)
        sbanks = [Ring([banks.bufs[2], banks.bufs[3], banks.bufs[4]]), Ring([banks.bufs[5], cx.bank(6), cx.bank(7)])]
        obank = Ring([banks.bufs[0], banks.bufs[1]])
        acc_last = None
        da = dbg_att or {}
        for g in da.get("groups", range(3)):
            d = GD[g]
            nt = 16 // d
            L = T // d
            for r in range(d):
                for j in range(nt + 1):
                    if j == 0:
                        nq, m0, qq0 = 64, 0, 64
                    elif j == nt:
                        nq, m0, qq0 = 64, L - 64, 0
                    else:
                        nq, m0, qq0 = 128, 128 * j - 64, 0
                    q0 = r + d * m0
                    tl = []
                    for ti in range(2):
                        t = j - 1 + ti
                        if t < 0:
                            tl.append((64, 0, lambda hh: EAL[:, 2 * g + hh, :],
                                       lambda hh: VhL[:, GTOFF[g] + r, hh * 64:(hh + 1) * 64]))
                        elif t >= nt:
                            tl.append((64, 64 + L, lambda hh: EBR[:, 2 * g + hh, :],
                                       lambda hh: VhR[:, GTOFF[g] + r, hh * 64:(hh + 1) * 64]))
                        else:
                            tl.append((128, 64 + 128 * t, (lambda hh, ti=ti: EAB[:, 2 * g + hh, ti, qq0:qq0 + nq]),
                                       (lambda hh, t=t: V[g][:, r * nt + t, hh * 64:(hh + 1) * 64])))
                    ko, ob = obank.acquire(cx.pe)
                    pts = []
                    for hh in range(2):
                        hp_ = slice(hh * 64, (hh + 1) * 64)
                        kp, pt = PTr.acquire(cx.dve)
                        for ti, (nk, kc0, ebf, vf) in enumerate(tl):
                            ks, sb_ = sbanks[hh].acquire(cx.pe)
                            mm_ev = cx.pe.tick(nc.tensor.matmul(sb_[0:nk, 0:nq], KT[g][hp_, r, kc0:kc0 + nk],
                                                                QT[g][hp_, r, m0:m0 + nq], start=True, stop=True))
                            kx, px = pex.acquire(cx.act)
                            cx.act.wait(mm_ev)
                            x_ev = cx.act.tick(nc.scalar.activation(out=px[0:nk, 0:nq], in_=sb_[0:nk, 0:nq], func=AF.Exp, scale=0.125))
                            sbanks[hh].release(ks, x_ev)
                            cx.dve.wait(x_ev)
                            p_ev = cx.dve.tick(nc.vector.tensor_tensor(out=pt[0:nk, ti, 0:nq], in0=px[0:nk, 0:nq], in1=ebf(hh), op=ALU.mult))
                            pex.release(kx, p_ev)
                        pts.append((kp, pt, p_ev))
                    if da.get("nopv"):
                        for hh in range(2):
                            PTr.release(pts[hh][0], pts[hh][2])
                        obank.release(ko, pts[1][2])
                        acc_last = pts[1][2]
                        continue
                    for hh in range(2):
                        kp, pt, p_ev = pts[hh]
                        cx.pe.wait(p_ev)
                        for ti, (nk, kc0, ebf, vf) in enumerate(tl):
                            mm = nc.tensor.matmul(ob[hh * 64:(hh + 1) * 64, 0:nq], vf(hh), pt[0:nk, ti, 0:nq], start=(ti == 0), stop=(ti == 1))
                    for hh in range(2):
                        kp, pt, p_ev = pts[hh]
                        for ti, (nk, kc0, ebf, vf) in enumerate(tl):
                            mm = nc.tensor.matmul(ob[hh * 64:(hh + 1) * 64, 128:128 + nq], ones[0:nk, :], pt[0:nk, ti, 0:nq], start=(ti == 0), stop=(ti == 1))
                        o_mm = cx.pe.tick(mm)
                        PTr.release(kp, o_mm)
                    cx.dve.wait(o_mm, acc_last)
                    src = ob[:, 0:256].rearrange("p (a n) -> p a n", n=128)[:, :, 0:nq]
                    dst = acc[:, :, sts(q0, nq, d)]
                    if g == 0:
                        a_ev = cx.dve.tick(nc.vector.tensor_copy(out=dst, in_=src))
                    else:
                        a_ev = cx.dve.tick(nc.vector.tensor_tensor(out=dst, in0=src, in1=dst, op=ALU.add))
                    acc_last = a_ev
                    obank.release(ko, a_ev)
                    run_pool_ops(1)
        cx.dve.wait(acc_last)
        e = cx.dve.tick(nc.vector.reciprocal(out=acc[:, 1, :], in_=acc[:, 1, :]))
        cx.dve.wait(e)
        yc_ev = cx.dve.tick(nc.vector.tensor_tensor(out=yT[:, 5, :], in0=acc[:, 0, :], in1=acc[:, 1, :], op=ALU.mult))
        run_pool_ops(len(pool_ops))
        pool_ev = (pq.sem, pq.n, pq.key)
        cx.pe.wait(acc_last)
        for ch in range(2):
            for b in range(NBLK):
                kb, bank = banks.acquire(cx.pe)
                cx.pe.wait(pool_ev, wvA_ev)
                mm_ev = cx.pe.tick(nc.tensor.matmul(bank[:, :], wpbd[:, ch, :], pooled[:, ch, b * BLK:(b + 1) * BLK], start=True, stop=True))
                cx.dve.wait(mm_ev)
                e = cx.dve.tick(nc.vector.tensor_scalar(out=yT[:, 3 + ch, b * BLK:(b + 1) * BLK], in0=bank[:, :], scalar1=pbc[:, ch:ch + 1],
                                                        scalar2=psc[:, ch:ch + 1], op0=ALU.add, op1=ALU.mult))
                banks.release(kb, e)

    xT_res = None
    if inner is not None:
        inner.__exit__(None, None, None)
        xT_res = cx.sb("xT_res", [128, KC, T], F32)
    with scope(cx):
        if xT_res is not None:
            bout = cx.sb("bout3", [128, KC], F32)
            b3_ev = small_loads(cx, [(bout[:], bout_d.ap())])
            cx.dve.wait(b3_ev)
        wring = Ring([cx.sb("wo%d" % i, [128, 6, 128], BF16) for i in range(2)])
        wslots = [cx.dslot("wo%d" % i, sw=True) for i in range(2)]
        xr = Ring([cx.sb("xo%d" % i, [128, BLK], F32) for i in range(6)])
        xsl = [cx.dslot("xo%d" % i) for i in range(6)]
        os_ = cx.dslot("o")
        out_dv = out_d.ap().rearrange("(c p) t -> p c t", p=128)
        st = []

        def evac_o(j, b, bank, mm_ev):
            sl = slice(b * BLK, (b + 1) * BLK)
            kx, xb = xr.acquire(cx.act)
            x_ev = xsl[kx].start(cx.act, xb[:], xT_dv[:, j, sl])
            cx.dve.wait(mm_ev, x_ev)
            if xT_res is not None:
                e = cx.dve.tick(nc.vector.scalar_tensor_tensor(out=xT_res[:, j, sl], in0=bank[:, :], scalar=bout[:, j:j + 1], in1=xb[:], op0=ALU.add, op1=ALU.add))
                xr.release(kx, e)
                return e
            e = cx.dve.tick(nc.vector.scalar_tensor_tensor(out=xb[:], in0=bank[:, :], scalar=bout[:, j:j + 1], in1=xb[:], op0=ALU.add, op1=ALU.add))
            cx.sp.wait(e)
            o_ev = os_.start(cx.sp, out_dv[:, j, sl], xb[:])
            xr.release(kx, o_ev)
            st.append(o_ev)
            return e

        proj_fm(cx, lambda j: wout_d.ap()[j], KC, 6, lambda c, b: yT[:, c, b * BLK:(b + 1) * BLK], evac_o, wring, wslots, banks, None)
        if st:
            cx.sp.wait(st[-1])
    return xT_res


def _t5_bucket(rel):
    nb = 16
    ret = (rel > 0).astype(np.int32) * nb
    n = np.abs(rel)
    max_exact = nb // 2
    large = max_exact + (np.log(np.maximum(n, 1) / max_exact) / math.log(1024 / max_exact) * (nb - max_exact)).astype(np.int32)
    large = np.minimum(large, nb - 1)
    return ret + np.where(n < max_exact, n, large)


def attn_bias(rel_table):
    kk = np.arange(128)[:, None]
    qq = np.arange(128)[None, :]
    bAB = np.full((128, 6, 2, 128), MASKV, np.float32)
    for g, d in enumerate(GD):
        for ti, delta in enumerate((kk - qq - 64, kk - qq + 64)):
            bk = _t5_bucket(delta * d)
            ok = np.abs(delta) <= 64
            for hh in range(2):
                vals = rel_table[bk, 2 * g + hh]
                bAB[:, 2 * g + hh, ti, :] = np.where(ok, vals, MASKV)
    bAL = np.ascontiguousarray(bAB[64:128, :, 0, 64:128])
    bBR = np.ascontiguousarray(bAB[0:64, :, 1, 0:64])
    return bAB, bAL, bBR


def pool_icnt(s):
    out = np.zeros((128, 2, 16), np.float32)
    a = s * T
    for ch in range(2):
        for gh in range(2):
            w = (2, 4, 8, 16)[2 * ch + gh]
            for k in range(16):
                pos = a + (k if k < 8 else T - 16 + k)
                lo = min(max(pos - w // 2, 0), SEQ)
                hi = min(max(pos + w // 2, 0), SEQ)
                out[gh * 64:(gh + 1) * 64, ch, k] = 1.0 / (hi - lo)
    return out


def prep_M1(inp, l):
    w_in = inp["w_in"][l]
    b_in = inp["b_in"][l]

    def tm(w):
        return np.ascontiguousarray(w.reshape(KC, 128, -1).transpose(1, 0, 2))
    ws = inp["gmlp_w_s"][l]
    bs = inp["gmlp_b_s"][l]
    bsT = np.zeros((128, 3, 128), np.float32)
    for hp in range(3):
        for hh in range(2):
            bsT[hh * 64:(hh + 1) * 64, hp, :] = bs[2 * hp + hh][None, :]
    wpbd = np.zeros((128, 2, 128), np.float32)
    for ch in range(2):
        for gh in range(2):
            wpbd[gh * 64:(gh + 1) * 64, ch, gh * 64:(gh + 1) * 64] = inp["pool_w"][l][2 * ch + gh]
    bAB, bAL, bBR = attn_bias(inp["rel_table"])
    shared = {
        "g1": colvec(inp["norm_mix_g"][l]), "win": wchunks(w_in), "wvC": tm(w_in[:, 1792:2176]), "bin": colvec(b_in),
        "bvC": np.ascontiguousarray(b_in[1792:2176]),
    }
    extra = {
        "wvA": tm(w_in[:, 384:768]), "bvA": np.ascontiguousarray(b_in[384:768]),
        "vgain": np.ascontiguousarray(inp["gmlp_v_g"][l].reshape(384)),
        "wsT": np.ascontiguousarray(ws.transpose(2, 0, 1)), "bsT": bsT, "wpbd": wpbd,
        "pb": colvec(inp["pool_b"][l].reshape(256)), "psc": colvec(inp["pool_scale"][l]),
        "wout": wchunks(inp["w_out"][l]), "bout": colvec(inp["b_out"][l]), "bAB": bAB,
    }
    percore = []
    for cid in range(8):
        s = cid % 2
        percore.append({
            "icnt": pool_icnt(s),
            "bAL": bAL if s == 1 else np.full_like(bAL, MASKV),
            "bBR": bBR if s == 0 else np.full_like(bBR, MASKV),
        })
    return shared, extra, percore


NCORES = 8
F_KEYS = ["g3", "gf", "wup", "bup", "cw", "cb", "wd", "bd"]
M2_KEYS = ["gm", "g2", "wq", "wk", "wv", "wo", "bo"]
M1_KEYS = ["g1", "win", "wvC", "bin", "bvC", "wvA", "bvA", "vgain", "wsT", "bsT", "wpbd", "pb", "psc", "wout", "bout"]
CORE_KEYS = ["icnt", "bAL", "bBR", "fl"]
PAIRS = [[0, 1], [2, 3], [4, 5], [6, 7]]
KVW = 2 * NH + 21 * 128


def emit_F(cx, io, final):
    with scope(cx):
        _emit_F(cx, io, final)


def emit_M2(cx, io):
    with scope(cx):
        _emit_M2(cx, io)


def emit_M1(cx, io, p_only):
    with scope(cx):
        _emit_M1(cx, io, p_only)


def run_cc(cx, pairs, after_ev):
    nc = cx.nc
    cx.pool.wait(after_ev)
    evs = []
    for (snd, rcv) in pairs:
        sem = cx.sem("cc")
        nc.gpsimd.collective_compute("AllGather", ALU.bypass, replica_groups=PAIRS, ins=[snd.ap().opt()],
                                     outs=[rcv.ap().opt()]).then_inc(sem)
        evs.append((sem, 1, new_key()))
    return evs


def build_fused(shapes):
    nc = bass.Bass("TRN2", target_bir_lowering=False)
    cx = Cx(nc)
    dr = {name: cx.dram_in(name, list(shp)) for name, shp in shapes.items()}
    out_d = cx.dram_out("out", [D, T])

    def scratch(name, shape):
        return nc.dram_tensor(name, list(shape), F32)
    XA = scratch("XA", [D, T])
    XB = scratch("XB", [D, T])
    X1 = scratch("X1", [D, T])
    nkv = 128 * KVW // 2 // 16
    for l in range(2):
        KVs = scratch("KVs%d" % l, [16, nkv])
        KVr = scratch("KVr%d" % l, [32, nkv])
        ZBs = scratch("ZBs%d" % l, [16, 256])
        ZBr = scratch("ZBr%d" % l, [32, 256])
        XHs = scratch("XHs%d" % l, [16, 128])
        XHr = scratch("XHr%d" % l, [32, 128])
        KVs_bf = KVs.ap().bitcast(BF16).rearrange("a (b c) -> (a b) c", b=8)
        KVr_bf = KVr.ap().bitcast(BF16).rearrange("(r a) (b c) -> r (a b) c", r=2, b=8)
        ZBs_v = ZBs.ap().rearrange("a (b c) -> (a b) c", b=8).rearrange("p (h k) -> p h k", k=16)
        ZBr_v = ZBr.ap().rearrange("(r a) (b c) -> r (a b) c", r=2, b=8).rearrange("r p (h k) -> r p h k", k=16)
        XHs_v = XHs.ap().rearrange("a (b c) -> (a b) c", b=8).rearrange("p (c k) -> p c k", k=2)
        XHr_v = XHr.ap().rearrange("(r a) (b c) -> r (a b) c", r=2, b=8).rearrange("r p (c k) -> r p c k", k=2)
        Xin = dr["x"] if l == 0 else X1
        Xout = out_d if l == 1 else X1

        def wts(keys):
            return {k: dr["%s_%d" % (k, l)] for k in keys}

        def expK(side, g, KVs_bf=KVs_bf):
            return KVs_bf[:, side * NH + GOFF[g]:side * NH + GOFF[g] + 64 * GD[g]].rearrange("p (r m) -> p r m", m=64)

        def expV(side, g, KVs_bf=KVs_bf):
            c0 = 2 * NH + GTOFF[g] * 128
            return KVs_bf[side * 64:(side + 1) * 64, c0:c0 + GD[g] * 128].rearrange("p (t n) -> p t n", n=128)

        def expZ(side, ZBs_v=ZBs_v):
            return ZBs_v[:, :, side * 8:(side + 1) * 8]

        io = wts(M1_KEYS)
        io.update({k: dr[k] for k in CORE_KEYS})
        io.update({"xT": Xin, "out": XA, "bAB": dr["bAB"],
                   "exp": {"K": expK, "V": expV, "zb": expZ,
                           "run": (lambda cx_, ev, a=KVs, b=KVr, c=ZBs, d=ZBr: run_cc(cx_, [(a, b), (c, d)], ev))},
                   "KThL": View(KVr_bf[0, :, NH:2 * NH]), "KThR": View(KVr_bf[1, :, 0:NH]),
                   "VhL": View(KVr_bf[0, 64:128, 2 * NH:KVW].rearrange("p (t n) -> p t n", n=128)),
                   "VhR": View(KVr_bf[1, 0:64, 2 * NH:KVW].rearrange("p (t n) -> p t n", n=128)),
                   "zbhL": View(ZBr_v[0, :, :, 8:16]), "zbhR": View(ZBr_v[1, :, :, 0:8])})
        io1 = io
        with scope(cx):
            hbuf = cx.sb("hbuf", [128, KC, T], BF16)
            xT_res = _emit_M1(cx, io1, False, hbuf=hbuf, xres=True)
            io = wts(M2_KEYS)
            io.update({"xT": XA, "out": XB, "memT": dr["memT"]})
            with scope(cx):
                _emit_M2(cx, io, xT_res=xT_res, store=False, load=False, hbuf=hbuf)
            with scope(cx):
                ds = cx.dslot("xh", sw=True)
                with nc.allow_non_contiguous_dma(reason="boundary columns"):
                    ds.start(cx.pool, XHs_v[:, :, 0:1], xT_res[:, :, 0:1])
                    e = ds.start(cx.pool, XHs_v[:, :, 1:2], xT_res[:, :, T - 1:T])
                cc = run_cc(cx, [(XHs, XHr)], e)
                for q in (cx.pool, cx.sp):
                    q.wait(cc)
            io = wts(F_KEYS)
            io.update({"xT": XB, "out": Xout, "fl": dr["fl"], "xhl": View(XHr_v[0, :, :, 1:2]), "xhr": View(XHr_v[1, :, :, 0:1])})
            with scope(cx):
                _emit_F(cx, io, l == 1, xT_res=xT_res, hbuf=hbuf)
    print('semaphores used:', cx._nsem)
    cx.close()
    return nc


_FUSED = {}


def kernel(**inp):
    inp = {k: np.asarray(v) for k, v in inp.items()}
    x = inp["x"].astype(np.float32, copy=False)
    shared = {}
    for l in range(2):
        sh, ex, pc = prep_M1(inp, l)
        d = dict(sh)
        d.update(ex)
        d.update(prep_M2(inp, l))
        d.update(prep_F(inp, l))
        bAB = d.pop("bAB")
        for k, v in d.items():
            shared["%s_%d" % (k, l)] = np.ascontiguousarray(v, dtype=np.float32)
    shared["bAB"] = bAB
    fls = core_flags()
    maps = []
    for c in range(NCORES):
        m = dict(shared)
        for k in ("icnt", "bAL", "bBR"):
            m[k] = pc[c][k]
        m["fl"] = fls[c]
        m["x"] = np.ascontiguousarray(x[c // 2, (c % 2) * T:(c % 2 + 1) * T, :].T)
        m["memT"] = np.ascontiguousarray(inp["mem"][c // 2].T)
        maps.append(m)
    if "nc" not in _FUSED:
        _FUSED["nc"] = build_fused({k: v.shape for k, v in maps[0].items()})
    res = run_bass_kernel_spmd(_FUSED["nc"], maps, core_ids=list(range(NCORES)))
    out = np.empty((BATCH, SEQ, D), np.float32)
    for c in range(NCORES):
        out[c // 2, (c % 2) * T:(c % 2 + 1) * T, :] = res.results[c]["out"].T
    return out
```

```python
import contextlib
import math
import numpy as np
import concourse.bass as bass
import concourse.mybir as mybir
from concourse.bass_utils import run_bass_kernel_spmd

F32 = mybir.dt.float32
BF16 = mybir.dt.bfloat16
AF = mybir.ActivationFunctionType
ALU = mybir.AluOpType

D = 1024
SEQ = 4096
BATCH = 4
T = 2048
NBLK = 4
BLK = 512
KC = 8
DFF = 2816
NFF = 22
EPS = 1e-6
IN_W = 2176
NEG = -1e30


_KEY = [0]


def new_key():
    _KEY[0] += 1
    return _KEY[0]


class EngQ:
    def __init__(self, cx, eng, name):
        self.key = new_key()
        self.name = name
        self.e = eng
        self.sem = cx.sem("q_" + name)
        self.n = 0
        self.seen = {}

    def wait(self, *evs):
        for ev in evs:
            if ev is None:
                continue
            if isinstance(ev, list):
                self.wait(*ev)
                continue
            sem, val, key = ev
            if self.seen.get(key, 0) >= val:
                continue
            self.seen[key] = val
            self.e.wait_ge(sem, val)

    def tick(self, ins):
        self.n += 1
        ins.then_inc(self.sem, 1)
        return (self.sem, self.n, self.key)


class DSlot:
    def __init__(self, cx, name):
        self.key = new_key()
        self.sem = cx.sem("d_" + name)
        self.n = 0

    def start(self, q, out, in_):
        assert (q.name == "pool") == bool(getattr(self, "sw", False)), "gpsimd DMAs need sw=True semaphores (and only them)"
        q.e.dma_start(out=out, in_=in_).then_inc(self.sem, 16)
        self.n += 16
        return (self.sem, self.n, self.key)


class Cx:
    def __init__(self, nc):
        self.nc = nc
        self.root = contextlib.ExitStack()
        self.st = self.root
        self._nsem = 0
        self._nsb = 0
        self.free_slots = []
        self.sw_slots = []
        self.scope_slots = [[]]
        self.scope_sw = [[]]
        self.banks8 = [self.root.enter_context(nc.psum_tensor("bank%d" % i, [128, 512], F32)) for i in range(8)]
        self.pe = EngQ(self, nc.tensor, "pe")
        self.dve = EngQ(self, nc.vector, "dve")
        self.act = EngQ(self, nc.scalar, "act")
        self.pool = EngQ(self, nc.gpsimd, "pool")
        self.sp = EngQ(self, nc.sync, "sp")

    def sem(self, name):
        self._nsem += 1
        return self.root.enter_context(self.nc.semaphore("%s_%d" % (name, self._nsem)))

    def dslot(self, name, sw=False):
        if sw:
            d = DSlot(self, name)
            d.sw = True
            self.sw_slots.append(d)
            self.scope_sw[-1].append(d)
            return d
        if self.free_slots:
            d = self.free_slots.pop()
        else:
            d = DSlot(self, name)
        d.sw = False
        self.scope_slots[-1].append(d)
        return d

    def bank(self, i):
        return self.banks8[i]

    def sb(self, name, shape, dt):
        self._nsb += 1
        return self.st.enter_context(self.nc.sbuf_tensor("s%d_%s" % (self._nsb, name), shape, dt))

    def ps(self, name, shape=(128, 512), dt=F32):
        return self.st.enter_context(self.nc.psum_tensor("p_" + name, list(shape), dt))

    def dram_in(self, name, shape, dt=F32):
        return self.nc.dram_tensor(name, list(shape), dt, kind="ExternalInput")

    def dram_out(self, name, shape, dt=F32):
        return self.nc.dram_tensor(name, list(shape), dt, kind="ExternalOutput")

    def close(self):
        self.root.close()


class Ring:
    def __init__(self, bufs):
        self.bufs = bufs
        self.rel = [None] * len(bufs)
        self.i = 0

    def acquire(self, q):
        k = self.i % len(self.bufs)
        self.i += 1
        q.wait(self.rel[k])
        self.rel[k] = None
        return k, self.bufs[k]

    def release(self, k, ev):
        if self.rel[k] is None:
            self.rel[k] = [ev]
        else:
            self.rel[k].append(ev)


def small_loads(cx, pairs):
    ds = cx.dslot("small%d" % cx._nsem)
    ev = None
    for o, i in pairs:
        ev = ds.start(cx.sp, o, i)
    return ev


class Normer:
    def __init__(self, cx, ps_bank, tag, nmax=BLK):
        self.cx = cx
        nc = cx.nc
        self.ones = cx.sb("ones_" + tag, [128, 128], BF16)
        self.sq = Ring([cx.sb("sq%d_%s" % (i, tag), [128, KC, nmax], BF16) for i in range(2)])
        self.rt = Ring([cx.sb("rt%d_%s" % (i, tag), [128, nmax], F32) for i in range(2)])
        self.bank = ps_bank
        self.bank_rel = None
        self.ones_ev = cx.pool.tick(nc.gpsimd.memset(self.ones[:], 1.0))

    def run(self, xs, g, outs, n, x_ev=None, out_wait=None):
        cx = self.cx
        nc = cx.nc
        ks, sq = self.sq.acquire(cx.act)
        cx.act.wait(x_ev)
        ev = None
        for c in range(KC):
            ev = nc.scalar.activation(out=sq[:, c, 0:n], in_=xs(c), func=AF.Square)
        sq_ev = cx.act.tick(ev)
        cx.pe.wait(sq_ev, self.ones_ev, self.bank_rel)
        for c in range(KC):
            mm = nc.tensor.matmul(self.bank[:, 0:n], self.ones[:], sq[:, c, 0:n], start=(c == 0), stop=(c == KC - 1))
        ss_ev = cx.pe.tick(mm)
        self.sq.release(ks, ss_ev)
        kr, rt = self.rt.acquire(cx.act)
        cx.act.wait(ss_ev)
        rt_ev = cx.act.tick(nc.scalar.activation(out=rt[:, 0:n], in_=self.bank[:, 0:n], func=AF.Sqrt,
                                                 bias=self.epsb[:, 0:1], scale=1.0 / D))
        self.bank_rel = rt_ev
        cx.dve.wait(rt_ev, x_ev, out_wait)
        r_ev = cx.dve.tick(nc.vector.reciprocal(out=rt[:, 0:n], in_=rt[:, 0:n]))
        cx.dve.wait(r_ev)
        for c in range(KC):
            o = nc.vector.scalar_tensor_tensor(out=outs(c), in0=xs(c), scalar=g[:, c:c + 1], in1=rt[:, 0:n],
                                               op0=ALU.mult, op1=ALU.mult)
        o_ev = cx.dve.tick(o)
        self.rt.release(kr, o_ev)
        return o_ev


def make_epsb(cx, normer):
    normer.epsb = cx.sb("epsb_%d" % cx._nsem, [128, 1], F32)
    ev = cx.pool.tick(cx.nc.gpsimd.memset(normer.epsb[:], EPS))
    cx.act.wait(ev)


FF_PARTS = [(0, 6), (6, 12), (12, 17), (17, 22)]


def _emit_F(cx, io, final, xT_res=None, hbuf=None):
    nc = cx.nc
    xT_d = io["xT"]
    fl_d = io["fl"]
    g3_d = io["g3"]
    gf_d = io["gf"]
    wup_d = io["wup"]
    bup_d = io["bup"]
    cw_d = io["cw"]
    cb_d = io["cb"]
    wd_d = io["wd"]
    bd_d = io["bd"]
    out_d = io["out"]

    xT = xT_res if xT_res is not None else cx.sb("xT", [128, KC, T], F32)
    xh = cx.sb("xh", [128, KC, 2], F32)
    hT = hbuf if hbuf is not None else cx.sb("hT", [128, KC, T], BF16)
    hTh = cx.sb("hTh", [128, KC, 2], BF16)
    fl = cx.sb("fl", [128, 2], F32)
    g3 = cx.sb("g3", [128, KC], F32)
    gf = cx.sb("gf", [128, KC], F32)
    bup = cx.sb("bup", [128, 2, NFF], F32)
    cw = cx.sb("cw", [128, 3, 2, NFF], F32)
    cb = cx.sb("cb", [128, 2, NFF], F32)
    bd = cx.sb("bd", [128, KC], F32)
    G = cx.sb("G", [128, 6, T], BF16)
    ubufs = [[cx.sb("ubuf%d_%d" % (s, i), [128, T + 2], F32) for i in range(2)] for s in range(2)]
    cA = [Ring([cx.sb("cA%d_%d" % (s, i), [128, BLK], F32) for i in range(4)]) for s in range(2)]
    sg = Ring([cx.sb("sg%d" % i, [128, BLK], F32) for i in range(2)])
    wu = Ring([cx.sb("wu%d" % i, [128, KC, 2, 128], BF16) for i in range(2)])
    wdb = Ring([cx.sb("wdb%d" % i, [128, 6, 128], BF16) for i in range(2)])
    wu_d = [cx.dslot("wu%d" % i, sw=True) for i in range(2)]
    wd_s = [cx.dslot("wd%d" % i, sw=True) for i in range(2)]
    banks = Ring([cx.bank(i) for i in range(6)])
    dbanks = banks
    pnorm = cx.bank(6)
    phalo = cx.bank(7)

    xT_dv = xT_d.ap().rearrange("(c p) t -> p c t", p=128)
    xs = cx.dslot("x")
    x_evs = []
    for b in range(NBLK):
        if xT_res is not None:
            x_evs.append(None)
        else:
            x_evs.append(xs.start(cx.sp, xT[:, :, b * BLK:(b + 1) * BLK], xT_dv[:, :, b * BLK:(b + 1) * BLK]))
    with nc.allow_non_contiguous_dma(reason="halo columns"):
        xhs = cx.dslot("xh")
        xhs.start(cx.sp, xh[:, :, 0:1], io["xhl"].ap())
        xh_ev = xhs.start(cx.sp, xh[:, :, 1:2], io["xhr"].ap())
    c_ev = small_loads(cx, [(fl[:], fl_d.ap()), (g3[:], g3_d.ap()), (gf[:], gf_d.ap()), (bup[:], bup_d.ap()),
                            (cw[:], cw_d.ap()), (cb[:], cb_d.ap()), (bd[:], bd_d.ap())])
    for q in (cx.act, cx.dve, cx.pool):
        q.wait(c_ev)

    nm = Normer(cx, pnorm, "f")
    make_epsb(cx, nm)
    h_evs = []
    for b in range(NBLK):
        sl = slice(b * BLK, (b + 1) * BLK)
        h_evs.append(nm.run(lambda c: xT[:, c, sl], g3, lambda c: hT[:, c, sl], BLK, x_ev=x_evs[b]))
    cx.act.wait(xh_ev)
    cx.dve.wait(xh_ev)
    hh_ev = nm.run(lambda c: xh[:, c, :], g3, lambda c: hTh[:, c, :], 2, x_ev=xh_ev)

    ubuf_rel = [[None, None], [None, None]]
    phalo_rel = None
    G_rel = None
    x_upd = [[None] * NBLK for _ in range(KC)]
    for (k0, k1) in FF_PARTS:
        for j in range(k0, k1):
            kw, wub = wu.acquire(cx.pool)
            w_ev = wu_d[kw].start(cx.pool, wub[:], wup_d.ap()[j])
            cx.pe.wait(w_ev, h_evs, hh_ev)
            conv_out = [None, None]
            for s in range(2):
                ub = ubufs[s][j % 2]
                cx.pe.wait(phalo_rel)
                for c in range(KC):
                    mm = nc.tensor.matmul(phalo[:, 2 * s:2 * s + 2], wub[:, c, s, :], hTh[:, c, :], start=(c == 0), stop=(c == KC - 1))
                ph_ev = cx.pe.tick(mm)
                cx.act.wait(ubuf_rel[s][j % 2])
                cx.dve.wait(ubuf_rel[s][j % 2], ph_ev)
                nc.vector.tensor_scalar(out=ub[:, 0:1], in0=phalo[:, 2 * s:2 * s + 1], scalar1=bup[:, s, j:j + 1],
                                        scalar2=fl[:, 0:1], op0=ALU.add, op1=ALU.mult)
                hv = nc.vector.tensor_scalar(out=ub[:, T + 1:T + 2], in0=phalo[:, 2 * s + 1:2 * s + 2], scalar1=bup[:, s, j:j + 1],
                                             scalar2=fl[:, 1:2], op0=ALU.add, op1=ALU.mult)
                halo_ev = cx.dve.tick(hv)
                phalo_rel = halo_ev
                ev_blocks = []
                for b in range(NBLK):
                    kb, bank = banks.acquire(cx.pe)
                    for c in range(KC):
                        mm = nc.tensor.matmul(bank[:, :], wub[:, c, s, :], hT[:, c, b * BLK:(b + 1) * BLK], start=(c == 0), stop=(c == KC - 1))
                    mm_ev = cx.pe.tick(mm)
                    cx.act.wait(mm_ev)
                    e = cx.act.tick(nc.scalar.activation(out=ub[:, 1 + b * BLK:1 + (b + 1) * BLK], in_=bank[:, :], func=AF.Identity,
                                                         bias=bup[:, s, j:j + 1], scale=1.0))
                    banks.release(kb, e)
                    ev_blocks.append(e)
                if s == 1:
                    wu.release(kw, mm_ev)
                slots = []
                t0s = []
                for b in range(NBLK):
                    ka, ca = cA[s].acquire(cx.act)
                    cx.act.wait(halo_ev, ev_blocks[b])
                    t0s.append(cx.act.tick(nc.scalar.activation(out=ca[:], in_=ub[:, b * BLK:b * BLK + BLK], func=AF.Copy,
                                                                scale=cw[:, 0, s, j:j + 1])))
                    slots.append((ka, ca))
                t1s = []
                for b in range(NBLK):
                    ka, ca = slots[b]
                    cx.dve.wait(t0s[b], halo_ev)
                    t1s.append(cx.dve.tick(nc.vector.scalar_tensor_tensor(out=ca[:], in0=ub[:, 1 + b * BLK:1 + b * BLK + BLK],
                                                                          scalar=cw[:, 1, s, j:j + 1], in1=ca[:], op0=ALU.mult, op1=ALU.add)))
                outs = []
                for b in range(NBLK):
                    ka, ca = slots[b]
                    cx.dve.wait(t1s[b], ev_blocks[min(b + 1, NBLK - 1)])
                    t2 = cx.dve.tick(nc.vector.scalar_tensor_tensor(out=ca[:], in0=ub[:, 2 + b * BLK:2 + b * BLK + BLK],
                                                                    scalar=cw[:, 2, s, j:j + 1], in1=ca[:], op0=ALU.mult, op1=ALU.add))
                    outs.append((ka, ca, t2))
                ubuf_rel[s][j % 2] = outs[-1][2]
                conv_out[s] = outs
            for b in range(NBLK):
                kag, cag, tg = conv_out[0][b]
                kav, cav, tv = conv_out[1][b]
                ks, sgb = sg.acquire(cx.act)
                cx.act.wait(tg)
                s_ev = cx.act.tick(nc.scalar.activation(out=sgb[:], in_=cag[:], func=AF.Silu, bias=cb[:, 0, j:j + 1], scale=1.0))
                cA[0].release(kag, s_ev)
                cx.dve.wait(s_ev, tv, G_rel)
                g_ev = cx.dve.tick(nc.vector.scalar_tensor_tensor(out=G[:, j - k0, b * BLK:(b + 1) * BLK], in0=cav[:], scalar=cb[:, 1, j:j + 1],
                                                                  in1=sgb[:], op0=ALU.add, op1=ALU.mult))
                cA[1].release(kav, g_ev)
                sg.release(ks, g_ev)
            G_ev = g_ev
        nk = k1 - k0
        for dc in range(KC):
            kd, wb = wdb.acquire(cx.pool)
            w_ev = wd_s[kd].start(cx.pool, wb[:, 0:nk, :], wd_d.ap()[dc, :, k0:k1, :])
            cx.pe.wait(w_ev, G_ev)
            for b in range(NBLK):
                kb, bank = dbanks.acquire(cx.pe)
                for k in range(nk):
                    mm = nc.tensor.matmul(bank[:, :], wb[:, k, :], G[:, k, b * BLK:(b + 1) * BLK], start=(k == 0), stop=(k == nk - 1))
                mm_ev = cx.pe.tick(mm)
                cx.dve.wait(mm_ev, x_upd[dc][b])
                xs_ = xT[:, dc, b * BLK:(b + 1) * BLK]
                if k0 == 0:
                    ins = nc.vector.scalar_tensor_tensor(out=xs_, in0=bank[:, :], scalar=bd[:, dc:dc + 1], in1=xs_, op0=ALU.add, op1=ALU.add)
                else:
                    ins = nc.vector.tensor_tensor(out=xs_, in0=bank[:, :], in1=xs_, op=ALU.add)
                e = cx.dve.tick(ins)
                x_upd[dc][b] = e
                dbanks.release(kb, e)
            wdb.release(kd, mm_ev)
        G_rel = mm_ev

    os_ = cx.dslot("o")
    out_dv = out_d.ap().rearrange("(c p) t -> p c t", p=128)
    o_ev = None
    if final:
        for b in range(NBLK):
            sl = slice(b * BLK, (b + 1) * BLK)
            e = nm.run(lambda c: xT[:, c, sl], gf, lambda c: xT[:, c, sl], BLK, x_ev=[x_upd[c][b] for c in range(KC)])
            cx.sp.wait(e)
            o_ev = os_.start(cx.sp, out_dv[:, :, sl], xT[:, :, sl])
    else:
        for b in range(NBLK):
            sl = slice(b * BLK, (b + 1) * BLK)
            cx.sp.wait([x_upd[c][b] for c in range(KC)])
            o_ev = os_.start(cx.sp, out_dv[:, :, sl], xT[:, :, sl])
    cx.sp.wait(o_ev)
    return


def colvec(v):
    v = np.asarray(v, np.float32)
    return np.ascontiguousarray(v.reshape(-1, 128).T)


def wchunks(w):
    K, N = w.shape
    return np.ascontiguousarray(w.reshape(K // 128, 128, N // 128, 128).transpose(2, 1, 0, 3))


def core_flags():
    out = []
    for cid in range(8):
        s = cid % 2
        f = np.zeros((128, 2), np.float32)
        f[:, 0] = 1.0 if s == 1 else 0.0
        f[:, 1] = 1.0 if s == 0 else 0.0
        out.append(f)
    return out


def prep_F(inp, l):
    wup = inp["ffn_w_up"][l]
    wu = wup.reshape(KC, 128, 2, NFF, 128).transpose(3, 1, 0, 2, 4)
    bup = inp["ffn_b_up"][l].reshape(2, NFF, 128).transpose(2, 0, 1)
    cw = inp["ffn_conv_w"][l].reshape(3, 2, NFF, 128).transpose(3, 0, 1, 2)
    cb = inp["ffn_conv_b"][l].reshape(2, NFF, 128).transpose(2, 0, 1)
    wd = inp["ffn_w_down"][l].reshape(NFF, 128, KC, 128).transpose(2, 1, 0, 3)
    return {
        "g3": colvec(inp["norm_ffn_g"][l]), "gf": colvec(inp["final_norm_g"]),
        "wup": np.ascontiguousarray(wu), "bup": np.ascontiguousarray(bup), "cw": np.ascontiguousarray(cw),
        "cb": np.ascontiguousarray(cb), "wd": np.ascontiguousarray(wd), "bd": colvec(inp["ffn_b_down"][l]),
    }


def proj_fm(cx, w_src, nj, kcin, rhs, evac, wring, wslots, banks, rhs_ev, nblk=NBLK, wshape=None):
    nc = cx.nc
    mm_ev = None
    for j in range(nj):
        kw, wb = wring.acquire(cx.pool)
        w_ev = wslots[kw].start(cx.pool, wb[:, 0:kcin, :], w_src(j))
        cx.pe.wait(w_ev, rhs_ev)
        for b in range(nblk):
            kb, bank = banks.acquire(cx.pe)
            for c in range(kcin):
                mm = nc.tensor.matmul(bank[:, :], wb[:, c, :], rhs(c, b), start=(c == 0), stop=(c == kcin - 1))
            mm_ev = cx.pe.tick(mm)
            banks.release(kb, evac(j, b, bank, mm_ev))
        wring.release(kw, mm_ev)
    return mm_ev


MEM = 256


def _emit_M2(cx, io, xT_res=None, store=True, load=True, hbuf=None):
    nc = cx.nc
    xT_d = io["xT"]
    memT_d = io["memT"]
    gm_d = io["gm"]
    g2_d = io["g2"]
    wq_d = io["wq"]
    wk_d = io["wk"]
    wv_d = io["wv"]
    wo_d = io["wo"]
    bo_d = io["bo"]
    out_d = io["out"]

    xT = xT_res if xT_res is not None else cx.sb("xT", [128, KC, T], F32)
    memT = cx.sb("memT", [128, KC, MEM], F32)
    mnT = cx.sb("mnT", [128, KC, MEM], BF16)
    gm = cx.sb("gm", [128, KC], F32)
    g2 = cx.sb("g2", [128, KC], F32)
    bo = cx.sb("bo", [128, KC], F32)
    hT = hbuf if hbuf is not None else cx.sb("hT", [128, KC, T], BF16)
    QT = cx.sb("QT", [128, KC, T], BF16)
    KxT = cx.sb("KxT", [128, KC, MEM], BF16)
    Vx = cx.sb("Vx", [128, 2, D], BF16)
    wv = cx.sb("wv", [128, KC, D], BF16)
    PT = Ring([cx.sb("PT%d" % i, [128, 2, BLK], BF16) for i in range(2)])
    rc = Ring([cx.sb("rc%d" % i, [128, BLK], F32) for i in range(2)])
    wring = Ring([cx.sb("w%d" % i, [128, KC, 128], BF16) for i in range(2)])
    wslots = [cx.dslot("w%d" % i, sw=True) for i in range(2)]
    banks = Ring([cx.bank(i) for i in range(4)])
    pden = Ring([cx.bank(4)])
    pov = Ring([cx.bank(5), cx.bank(6)])
    pnorm = cx.bank(7)

    xT_dv = xT_d.ap().rearrange("(c p) t -> p c t", p=128)
    xs = cx.dslot("x")
    m_ev = cx.dslot("mem").start(cx.sp, memT[:], memT_d.ap().rearrange("(c p) t -> p c t", p=128))
    x_evs = [xs.start(cx.sp, xT[:, :, b * BLK:(b + 1) * BLK], xT_dv[:, :, b * BLK:(b + 1) * BLK]) if load else None for b in range(NBLK)]
    c_ev = small_loads(cx, [(gm[:], gm_d.ap()), (g2[:], g2_d.ap()), (bo[:], bo_d.ap())])
    wv_ev = cx.dslot("wv", sw=True).start(cx.pool, wv[:], wv_d.ap())
    for q in (cx.act, cx.dve):
        q.wait(c_ev)
    nm = Normer(cx, pnorm, "m")
    make_epsb(cx, nm)
    ones = nm.ones

    mn_ev = nm.run(lambda c: memT[:, c, :], gm, lambda c: mnT[:, c, :], MEM, x_ev=m_ev)

    def evac_k(j, b, bank, mm_ev):
        cx.act.wait(mm_ev)
        return cx.act.tick(nc.scalar.copy(out=KxT[:, j, :], in_=bank[:, 0:MEM]))

    nck = None
    for j in range(KC):
        kw, wb = wring.acquire(cx.pool)
        w_ev = wslots[kw].start(cx.pool, wb[:], wk_d.ap()[j])
        cx.pe.wait(w_ev, mn_ev)
        kb, bank = banks.acquire(cx.pe)
        for c in range(KC):
            mm = nc.tensor.matmul(bank[:, 0:MEM], wb[:, c, :], mnT[:, c, :], start=(c == 0), stop=(c == KC - 1))
        mm_ev = cx.pe.tick(mm)
        k_ev = evac_k(j, 0, bank, mm_ev)
        banks.release(kb, k_ev)
        wring.release(kw, mm_ev)
    cx.pe.wait(wv_ev)
    for mt in range(2):
        for nh in range(2):
            kb, bank = banks.acquire(cx.pe)
            for c in range(KC):
                mm = nc.tensor.matmul(bank[:, :], mnT[:, c, mt * 128:(mt + 1) * 128], wv[:, c, nh * 512:(nh + 1) * 512],
                                      start=(c == 0), stop=(c == KC - 1))
            mm_ev = cx.pe.tick(mm)
            cx.act.wait(mm_ev)
            v_ev = cx.act.tick(nc.scalar.copy(out=Vx[:, mt, nh * 512:(nh + 1) * 512], in_=bank[:, :]))
            banks.release(kb, v_ev)

    h_evs = []
    for b in range(NBLK):
        sl = slice(b * BLK, (b + 1) * BLK)
        h_evs.append(nm.run(lambda c: xT[:, c, sl], g2, lambda c: hT[:, c, sl], BLK, x_ev=x_evs[b]))

    def evac_q(j, b, bank, mm_ev):
        cx.act.wait(mm_ev)
        return cx.act.tick(nc.scalar.copy(out=QT[:, j, b * BLK:(b + 1) * BLK], in_=bank[:, :]))

    q_pe = proj_fm(cx, lambda j: wq_d.ap()[j], KC, KC, lambda c, b: hT[:, c, b * BLK:(b + 1) * BLK], evac_q, wring, wslots, banks, h_evs)
    q_ev = (cx.act.sem, cx.act.n, cx.act.key)
    oT = hT
    o_evs = []
    for h in range(4):
        for b in range(NBLK):
            sl = slice(b * BLK, (b + 1) * BLK)
            kp, pt = PT.acquire(cx.act)
            for mt in range(2):
                kb, bank = banks.acquire(cx.pe)
                cx.pe.wait(q_ev, k_ev)
                for hf in range(2):
                    mm = nc.tensor.matmul(bank[:, :], KxT[:, 2 * h + hf, mt * 128:(mt + 1) * 128], QT[:, 2 * h + hf, sl],
                                          start=(hf == 0), stop=(hf == 1))
                mm_ev = cx.pe.tick(mm)
                cx.act.wait(mm_ev)
                p_ev = cx.act.tick(nc.scalar.activation(out=pt[:, mt, :], in_=bank[:, :], func=AF.Exp, scale=1.0 / 16.0))
                banks.release(kb, p_ev)
            cx.pe.wait(p_ev, v_ev)
            kd, dbank = pden.acquire(cx.pe)
            for mt in range(2):
                mm = nc.tensor.matmul(dbank[:, :], ones[:], pt[:, mt, :], start=(mt == 0), stop=(mt == 1))
            d_ev = cx.pe.tick(mm)
            kr, rcb = rc.acquire(cx.dve)
            cx.dve.wait(d_ev)
            r_ev = cx.dve.tick(nc.vector.reciprocal(out=rcb[:], in_=dbank[:, :]))
            pden.release(kd, r_ev)
            for hf in range(2):
                ko, obank = pov.acquire(cx.pe)
                if h == 0 and b == 0 and hf == 0:
                    cx.pe.wait(q_pe)
                for mt in range(2):
                    mm = nc.tensor.matmul(obank[:, :], Vx[:, mt, (2 * h + hf) * 128:(2 * h + hf + 1) * 128], pt[:, mt, :],
                                          start=(mt == 0), stop=(mt == 1))
                o_mm = cx.pe.tick(mm)
                cx.dve.wait(o_mm, r_ev)
                o_ev = cx.dve.tick(nc.vector.tensor_tensor(out=oT[:, 2 * h + hf, sl], in0=obank[:, :], in1=rcb[:], op=ALU.mult))
                pov.release(ko, o_ev)
            PT.release(kp, o_mm)
            rc.release(kr, o_ev)
            o_evs.append(o_ev)

    os_ = cx.dslot("o")
    out_dv = out_d.ap().rearrange("(c p) t -> p c t", p=128)
    st_evs = []

    def evac_o(j, b, bank, mm_ev):
        cx.dve.wait(mm_ev)
        xs_ = xT[:, j, b * BLK:(b + 1) * BLK]
        e = cx.dve.tick(nc.vector.scalar_tensor_tensor(out=xs_, in0=bank[:, :], scalar=bo[:, j:j + 1], in1=xs_, op0=ALU.add, op1=ALU.add))
        if store:
            cx.sp.wait(e)
            st_evs.append(os_.start(cx.sp, out_dv[:, j, b * BLK:(b + 1) * BLK], xs_))
        return e

    proj_fm(cx, lambda j: wo_d.ap()[j], KC, KC, lambda c, b: oT[:, c, b * BLK:(b + 1) * BLK], evac_o, wring, wslots, banks, o_evs)
    if store:
        cx.sp.wait(st_evs[-1])
    return


def prep_M2(inp, l):
    wkv = inp["xattn_w_kv"][l]
    return {
        "gm": colvec(inp["mem_norm_g"]), "g2": colvec(inp["norm_mem_g"][l]),
        "wq": wchunks(inp["xattn_w_q"][l]), "wk": wchunks(wkv[:, :D]),
        "wv": np.ascontiguousarray(wkv[:, D:].reshape(KC, 128, D).transpose(1, 0, 2)),
        "wo": wchunks(inp["xattn_w_o"][l]), "bo": colvec(inp["xattn_b_o"][l]),
    }


AX = mybir.AxisListType
GD = [1, 4, 16]
GOFF = [0, 64, 320]
GTOFF = [0, 1, 5]
NH = 1344
MASKV = -1.0e4


def sts(start, n, step):
    return slice(start, start + (n - 1) * step + 1, step)


def barrier(cx):
    qs = (cx.pe, cx.act, cx.dve, cx.pool)
    evs = [(q.sem, q.n, q.key) for q in qs if q.n > 0]
    devs = [(d.sem, d.n, d.key) for d in cx.scope_slots[-1] + cx.scope_sw[-1] if d.n > 0]
    for q in (cx.pe, cx.act, cx.dve, cx.pool, cx.sp):
        q.wait([e for e in evs if e[2] != q.key] + devs)


@contextlib.contextmanager
def scope(cx):
    old = cx.st
    cx.st = contextlib.ExitStack()
    cx.scope_slots.append([])
    cx.scope_sw.append([])
    try:
        yield
    finally:
        barrier(cx)
        cx.st.close()
        cx.st = old
        cx.free_slots.extend(cx.scope_slots.pop())
        cx.scope_sw.pop()


class View:
    def __init__(self, ap):
        self._ap = ap

    def ap(self):
        return self._ap


def _emit_M1(cx, io, p_only, dbg_stop=None, dbg_att=None, hbuf=None, xres=False):
    nc = cx.nc
    xT_d = io["xT"]
    g1_d = io["g1"]
    win_d = io["win"]
    wvC_d = io["wvC"]
    bin_d = io["bin"]
    bvC_d = io["bvC"]
    if p_only:
        KTb_d = io["KTb"]
        Vb_d = io["Vb"]
        zbb_d = io["zbb"]
    else:
        wvA_d = io["wvA"]
        bvA_d = io["bvA"]
        vgain_d = io["vgain"]
        wsT_d = io["wsT"]
        bsT_d = io["bsT"]
        wpbd_d = io["wpbd"]
        pb_d = io["pb"]
        psc_d = io["psc"]
        icnt_d = io["icnt"]
        wout_d = io["wout"]
        bout_d = io["bout"]
        bAB_d = io["bAB"]
        bAL_d = io["bAL"]
        bBR_d = io["bBR"]
        fl_d = io["fl"]
        out_d = io["out"]
    hT = hbuf if hbuf is not None else cx.sb("hT", [128, KC, T], BF16)
    inner = None
    if xres:
        inner = scope(cx)
        inner.__enter__()
    zbT = cx.sb("zbT", [128, 2, T + 16], F32)
    KT = [cx.sb("KT%d" % g, [128, GD[g], T // GD[g] + 128], BF16) for g in range(3)]
    V = [cx.sb("V%d" % g, [128, 16, 128], BF16) for g in range(3)]
    g1 = cx.sb("g1", [128, KC], F32)
    binc = cx.sb("binc", [128, 17], F32)
    bvC = cx.sb("bvC", [128, 384], F32)
    loads = [(g1[:], g1_d.ap()), (binc[:], bin_d.ap()), (bvC[:], bvC_d.ap().partition_broadcast(128))]
    if not p_only:
        uT = cx.sb("uT", [128, 3, T], BF16)
        vn = cx.sb("vn", [128, 16, 384], BF16)
        QT = [cx.sb("QT%d" % g, [128, GD[g], T // GD[g]], BF16) for g in range(3)]
        bvA = cx.sb("bvA", [128, 384], F32)
        vgain = cx.sb("vgain", [128, 384], F32)
        bsT = cx.sb("bsT", [128, 3, 128], F32)
        pbc = cx.sb("pbc", [128, 2], F32)
        psc = cx.sb("psc", [128, 2], F32)
        icnt = cx.sb("icnt", [128, 2, 16], F32)
        bout = cx.sb("bout", [128, KC], F32)
        flm = cx.sb("flm", [128, 2], F32)
        wsT = cx.sb("wsT", [128, 6, 128], BF16)
        wpbd = cx.sb("wpbd", [128, 2, 128], BF16)
        loads += [(bvA[:], bvA_d.ap().partition_broadcast(128)), (vgain[:], vgain_d.ap().partition_broadcast(128)),
                  (bsT[:], bsT_d.ap()), (pbc[:], pb_d.ap()), (psc[:], psc_d.ap()), (icnt[:], icnt_d.ap()), (bout[:], bout_d.ap()), (flm[:], fl_d.ap())]
    c_ev = small_loads(cx, loads)
    for q in (cx.act, cx.dve):
        q.wait(c_ev)
    banks = Ring([cx.bank(i) for i in range(6)])
    pnorm = cx.bank(6)
    xT_dv = xT_d.ap().rearrange("(c p) t -> p c t", p=128)
    wsl = cx.dslot("wres", sw=True)
    NB2 = 256

    with scope(cx):
        nm = Normer(cx, pnorm, "n1", nmax=NB2)
        make_epsb(cx, nm)
        arena = cx.sb("arena", [128, 4 * KC * NB2], F32)
        xring = Ring([arena[:, i * KC * NB2:(i + 1) * KC * NB2].rearrange("p (c t) -> p c t", c=KC) for i in range(4)])
        xsl = [cx.dslot("xr%d" % i) for i in range(4)]
        wring = Ring([cx.sb("w%d" % i, [128, KC, 128], BF16) for i in range(2)])
        wslots = [cx.dslot("w%d" % i, sw=True) for i in range(2)]
        wvC = cx.sb("wvC", [128, KC, 384], BF16)
        wvC_ev = wsl.start(cx.pool, wvC[:], wvC_d.ap())
        if not p_only:
            wvA = cx.sb("wvA", [128, KC, 384], BF16)
            wsl.start(cx.pool, wvA[:], wvA_d.ap())
            wsl.start(cx.pool, wsT[:], wsT_d.ap())
            wvA_ev = wsl.start(cx.pool, wpbd[:], wpbd_d.ap())
            wvC_ev = wvA_ev
        h_evs = []
        for bb in range(T // NB2):
            kx, xr = xring.acquire(cx.sp)
            x_ev = xsl[kx].start(cx.sp, xr, xT_dv[:, :, bb * NB2:(bb + 1) * NB2])
            h_ev = nm.run(lambda c: xr[:, c, :], g1, lambda c: hT[:, c, bb * NB2:(bb + 1) * NB2], NB2, x_ev=x_ev)
            xring.release(kx, h_ev)
            h_evs.append(h_ev)

        jls = [[11, 12, 13, 6, 7]] if p_only else [[11, 12, 13, 6, 7], [0, 1, 2, 8, 9, 10]]
        cur = {"jl": jls[0]}
        fm_evs = []

        def evac_in(i, b, bank, mm_ev):
            j = cur["jl"][i]
            sl = slice(b * BLK, (b + 1) * BLK)
            cx.act.wait(mm_ev)
            bias = binc[:, j:j + 1]
            if j < 3:
                ins = nc.scalar.activation(out=uT[:, j, sl], in_=bank[:, :], func=AF.Gelu_apprx_tanh, bias=bias, scale=1.0)
            elif j < 8:
                ins = nc.scalar.activation(out=zbT[:, j - 6, 8 + b * BLK:8 + (b + 1) * BLK], in_=bank[:, :], func=AF.Identity, bias=bias, scale=1.0)
            elif j < 11:
                g = j - 8
                mb = BLK // GD[g]
                ins = nc.scalar.activation(out=QT[g][:, :, b * mb:(b + 1) * mb].rearrange("p r m -> p m r"),
                                           in_=bank[:, :].rearrange("p (m r) -> p m r", r=GD[g]), func=AF.Identity, bias=bias, scale=1.0)
            else:
                g = j - 11
                mb = BLK // GD[g]
                ins = nc.scalar.activation(out=KT[g][:, :, 64 + b * mb:64 + (b + 1) * mb].rearrange("p r m -> p m r"),
                                           in_=bank[:, :].rearrange("p (m r) -> p m r", r=GD[g]), func=AF.Identity, bias=bias, scale=1.0)
            e = cx.act.tick(ins)
            fm_evs.append(e)
            return e

        proj_fm(cx, lambda i: win_d.ap()[cur["jl"][i]], len(cur["jl"]), KC, lambda c, b: hT[:, c, b * BLK:(b + 1) * BLK], evac_in,
                wring, wslots, banks, h_evs)

        v_evs = []
        for g in range(3):
            d = GD[g]
            nt = 16 // d
            tiles = [(r, t) for r in range(d) for t in range(nt)]
            for q4 in range(4):
                kb, bank = banks.acquire(cx.pe)
                cx.pe.wait(wvC_ev, h_evs)
                for i in range(4):
                    r, t = tiles[q4 * 4 + i]
                    s0 = r + d * 128 * t
                    for c in range(KC):
                        mm = nc.tensor.matmul(bank[:, i * 128:(i + 1) * 128], hT[:, c, sts(s0, 128, d)], wvC[:, c, g * 128:(g + 1) * 128],
                                              start=(c == 0), stop=(c == KC - 1))
                mm_ev = cx.pe.tick(mm)
                cx.dve.wait(mm_ev)
                e = cx.dve.tick(nc.vector.tensor_tensor(out=V[g][:, q4 * 4:q4 * 4 + 4, :], in0=bank[:, :].rearrange("p (a n) -> p a n", n=128),
                                                        in1=bvC[:, g * 128:(g + 1) * 128].unsqueeze(1).to_broadcast([128, 4, 128]), op=ALU.add))
                banks.release(kb, e)
                v_evs.append(e)

        cc_evs = None
        if "exp" in io:
            ex = io["exp"]
            es = cx.dslot("exp")
            cx.sp.wait(fm_evs, v_evs)
            with nc.allow_non_contiguous_dma(reason="boundary export"):
                for g in range(3):
                    d = GD[g]
                    nt = 16 // d
                    L = T // d
                    es.start(cx.sp, ex["K"](0, g), KT[g][:, :, 64:128])
                    es.start(cx.sp, ex["K"](1, g), KT[g][:, :, L:L + 64])
                    es.start(cx.sp, ex["V"](0, g), V[g][0:64, 0:16:nt, :])
                    es.start(cx.sp, ex["V"](1, g), V[g][64:128, nt - 1:16:nt, :])
                es.start(cx.sp, ex["zb"](0), zbT[:, :, 8:16])
                e_last = es.start(cx.sp, ex["zb"](1), zbT[:, :, T:T + 8])
        if not p_only:
            cur["jl"] = jls[1]
            proj_fm(cx, lambda i: win_d.ap()[cur["jl"][i]], len(cur["jl"]), KC, lambda c, b: hT[:, c, b * BLK:(b + 1) * BLK], evac_in,
                    wring, wslots, banks, h_evs)
        if "exp" in io:
            cc_evs = ex["run"](cx, e_last)
        if not p_only:
            cx.dve.wait(h_evs)
            cx.act.wait(h_evs)
            vtr = Ring([(arena[:, (2 * i) * 1536:(2 * i + 1) * 1536].rearrange("p (a n) -> p a n", a=4),
                         arena[:, (2 * i + 1) * 1536:(2 * i + 2) * 1536].rearrange("p (a n) -> p a n", a=4),
                         cx.sb("v6%d" % i, [128, 24], F32)) for i in range(2)])
            vn_evs = []
            for q4 in range(4):
                kv, (vt, vg, v6) = vtr.acquire(cx.dve)
                for i in range(4):
                    t = q4 * 4 + i
                    kb, bank = banks.acquire(cx.pe)
                    cx.pe.wait(wvA_ev, h_evs)
                    for c in range(KC):
                        mm = nc.tensor.matmul(bank[:, 0:384], hT[:, c, t * 128:(t + 1) * 128], wvA[:, c, :], start=(c == 0), stop=(c == KC - 1))
                    mm_ev = cx.pe.tick(mm)
                    cx.dve.wait(mm_ev)
                    e1 = cx.dve.tick(nc.vector.tensor_tensor(out=vt[:, i, :], in0=bank[:, 0:384], in1=bvA[:], op=ALU.add))
                    banks.release(kb, e1)
                cx.act.wait(e1)
                e2 = cx.act.tick(nc.scalar.activation(out=vg[:], in_=vt[:], func=AF.Gelu_apprx_tanh))
                cx.act.wait(e2)
                e3 = cx.act.tick(nc.scalar.activation(out=vt[:], in_=vg[:], func=AF.Square))
                cx.dve.wait(e3)
                e4 = cx.dve.tick(nc.vector.tensor_reduce(out=v6[:], in_=vt[:].rearrange("p a (h e) -> p (a h) e", e=64), axis=AX.X, op=ALU.add))
                cx.act.wait(e4)
                e5 = cx.act.tick(nc.scalar.activation(out=v6[:], in_=v6[:], func=AF.Sqrt, bias=nm.epsb[:, 0:1], scale=1.0 / 64))
                cx.dve.wait(e5)
                e6 = cx.dve.tick(nc.vector.reciprocal(out=v6[:], in_=v6[:]))
                cx.dve.wait(e6)
                e7 = cx.dve.tick(nc.vector.tensor_tensor(out=vt[:].rearrange("p a (h e) -> p (a h) e", e=64), in0=vg[:].rearrange("p a (h e) -> p (a h) e", e=64),
                                                         in1=v6[:].unsqueeze(2).to_broadcast([128, 24, 64]), op=ALU.mult))
                cx.dve.wait(e7)
                e8 = cx.dve.tick(nc.vector.tensor_tensor(out=vn[:, q4 * 4:q4 * 4 + 4, :], in0=vt[:], in1=vgain[:].unsqueeze(1).to_broadcast([128, 4, 384]), op=ALU.mult))
                vtr.release(kv, e8)
                vn_evs.append(e8)

        if p_only:
            os_ = cx.dslot("o", sw=True)
            cx.pool.wait(fm_evs, v_evs)
            for g in range(3):
                d = GD[g]
                nt = 16 // d
                L = T // d
                os_.start(cx.pool, KTb_d.ap()[:, 0, GOFF[g]:GOFF[g] + 64 * d].rearrange("p (r m) -> p r m", m=64), KT[g][:, :, 64:128])
                os_.start(cx.pool, KTb_d.ap()[:, 1, GOFF[g]:GOFF[g] + 64 * d].rearrange("p (r m) -> p r m", m=64), KT[g][:, :, L:L + 64])
                os_.start(cx.pool, Vb_d.ap()[0, :, GTOFF[g]:GTOFF[g] + d, :], V[g][0:64, 0:16:nt, :])
                os_.start(cx.pool, Vb_d.ap()[1, :, GTOFF[g]:GTOFF[g] + d, :], V[g][64:128, nt - 1:16:nt, :])
            cx.sp.wait(fm_evs)
            o2 = cx.dslot("o2")
            with nc.allow_non_contiguous_dma(reason="small boundary"):
                o2.start(cx.sp, zbb_d.ap()[:, :, 0:8], zbT[:, :, 8:16])
                e_o2 = o2.start(cx.sp, zbb_d.ap()[:, :, 8:16], zbT[:, :, T:T + 8])
            cx.sp.wait(e_o2)
            cx.pool.wait((os_.sem, os_.n, os_.key))

    if p_only:
        return

    if dbg_stop == 1:
        return
    if cc_evs is not None:
        cx.pool.wait(cc_evs)
        cx.sp.wait(cc_evs)
    hs = cx.dslot("halo", sw=True)
    for g in range(3):
        d = GD[g]
        L = T // d
        hs.start(cx.pool, KT[g][:, :, 0:64], io["KThL"].ap()[:, GOFF[g]:GOFF[g] + 64 * d].rearrange("p (r m) -> p r m", m=64))
        hs.start(cx.pool, KT[g][:, :, 64 + L:128 + L], io["KThR"].ap()[:, GOFF[g]:GOFF[g] + 64 * d].rearrange("p (r m) -> p r m", m=64))
    kh_ev = (hs.sem, hs.n, hs.key)
    h2 = cx.dslot("halo2")
    with nc.allow_non_contiguous_dma(reason="small halo"):
        h2.start(cx.sp, zbT[:, :, 0:8], io["zbhL"].ap())
        zh_ev = h2.start(cx.sp, zbT[:, :, T + 8:T + 16], io["zbhR"].ap())
    yT = hT

    with scope(cx):
        pq = cx.dve
        pv = nc.vector
        VhL = cx.sb("VhL", [64, 21, 128], BF16)
        VhR = cx.sb("VhR", [64, 21, 128], BF16)
        hs2 = cx.dslot("vh", sw=True)
        hs2.start(cx.pool, VhL[:], io["VhL"].ap())
        vh_ev = hs2.start(cx.pool, VhR[:], io["VhR"].ap())
        gt = Ring([cx.sb("gt%d" % i, [128, 512], F32) for i in range(2)])
        for hp in range(3):
            for t4 in range(4):
                kb, bank = banks.acquire(cx.pe)
                cx.pe.wait(vn_evs, wvA_ev)
                for i in range(4):
                    t = t4 * 4 + i
                    for hh in range(2):
                        h = 2 * hp + hh
                        mm = nc.tensor.matmul(bank[hh * 64:(hh + 1) * 64, i * 128:(i + 1) * 128], vn[:, t, h * 64:(h + 1) * 64], wsT[:, h, :],
                                              start=True, stop=True)
                mm_ev = cx.pe.tick(mm)
                kg, gtb = gt.acquire(cx.dve)
                cx.dve.wait(mm_ev)
                e1 = cx.dve.tick(nc.vector.tensor_tensor(out=gtb[:].rearrange("p (a n) -> p a n", n=128), in0=bank[:, :].rearrange("p (a n) -> p a n", n=128),
                                                         in1=bsT[:, hp, :].unsqueeze(1).to_broadcast([128, 4, 128]), op=ALU.add))
                banks.release(kb, e1)
                cx.dve.wait(e1, fm_evs)
                e2 = cx.dve.tick(nc.vector.tensor_tensor(out=yT[:, hp, t4 * 512:(t4 + 1) * 512], in0=gtb[:], in1=uT[:, hp, t4 * 512:(t4 + 1) * 512], op=ALU.mult))
                gt.release(kg, e2)

        if dbg_stop == 2:
            barrier(cx)
            return
        pa = cx.sb("pa", [128, T + 16], F32)
        pbuf = cx.sb("pbuf", [128, T + 16], F32)
        pooled = cx.sb("pooled", [128, 2, T], BF16)
        W = T + 16
        pool_ops = []

        def dv(thunk):
            pool_ops.append(thunk)

        def run_pool_ops(n):
            for _ in range(n):
                if not pool_ops:
                    return
                if not pool_started:
                    pool_started.append(1)
                    pq.wait(zh_ev, fm_evs, c_ev)
                e = pq.tick(pool_ops.pop(0)())
                pq.wait(e)

        pool_started = []
        dv(lambda: pv.tensor_scalar(out=zbT[:, :, 0:8], in0=zbT[:, :, 0:8], scalar1=flm[:, 0:1], scalar2=None, op0=ALU.mult))
        dv(lambda: pv.tensor_scalar(out=zbT[:, :, T + 8:T + 16], in0=zbT[:, :, T + 8:T + 16], scalar1=flm[:, 1:2], scalar2=None, op0=ALU.mult))

        def pool_out(src, p0, ch, w):
            ps_ = slice(p0, p0 + 64)
            o = 8 - w // 2
            dv(lambda: pv.tensor_scalar(out=pa[ps_, 0:T], in0=src[ps_, o:o + T], scalar1=1.0 / w, scalar2=None, op0=ALU.mult))
            dv(lambda: pv.tensor_tensor(out=pooled[ps_, ch, :], in0=pa[ps_, 0:T], in1=zbT[ps_, ch, 8:8 + T], op=ALU.subtract))
            for (c0, k0) in ((0, 0), (T - 8, 8)):
                dv(lambda c0=c0, k0=k0: pv.tensor_tensor(out=pa[ps_, 0:8], in0=src[ps_, o + c0:o + c0 + 8], in1=icnt[ps_, ch, k0:k0 + 8], op=ALU.mult))
                dv(lambda c0=c0, k0=k0: pv.tensor_tensor(out=pooled[ps_, ch, c0:c0 + 8], in0=pa[ps_, 0:8], in1=zbT[ps_, ch, 8 + c0:16 + c0], op=ALU.subtract))

        dv(lambda: pv.tensor_tensor(out=pbuf[:, 0:W - 1], in0=zbT[:, 0, 0:W - 1], in1=zbT[:, 0, 1:W], op=ALU.add))
        dv(lambda: pv.tensor_tensor(out=pa[64:128, 8:W - 3], in0=pbuf[64:128, 8:W - 3], in1=pbuf[64:128, 10:W - 1], op=ALU.add))
        pool_out(pbuf, 0, 0, 2)
        dv(lambda: pv.tensor_tensor(out=pa[64:128, 0:W - 3], in0=pbuf[64:128, 0:W - 3], in1=pbuf[64:128, 2:W - 1], op=ALU.add))
        dv(lambda: pv.tensor_copy(out=pbuf[64:128, 0:W - 3], in_=pa[64:128, 0:W - 3]))
        pool_out(pbuf, 64, 0, 4)
        dv(lambda: pv.tensor_tensor(out=pa[:, 0:W - 1], in0=zbT[:, 1, 0:W - 1], in1=zbT[:, 1, 1:W], op=ALU.add))
        dv(lambda: pv.tensor_tensor(out=pbuf[:, 0:W - 3], in0=pa[:, 0:W - 3], in1=pa[:, 2:W - 1], op=ALU.add))
        dv(lambda: pv.tensor_tensor(out=pa[:, 0:W - 7], in0=pbuf[:, 0:W - 7], in1=pbuf[:, 4:W - 3], op=ALU.add))
        dv(lambda: pv.tensor_tensor(out=pbuf[64:128, 0:W - 15], in0=pa[64:128, 0:W - 15], in1=pa[64:128, 8:W - 7], op=ALU.add))
        dv(lambda: pv.tensor_copy(out=pbuf[0:64, 0:W - 7], in_=pa[0:64, 0:W - 7]))
        pool_out(pbuf, 0, 1, 8)
        pe_pool = pool_out(pbuf, 64, 1, 16)
        acc = cx.sb("acc", [128, 2, T], F32)
        EAB = cx.sb("EAB", [128, 6, 2, 128], F32)
        EAL = cx.sb("EAL", [64, 6, 64], F32)
        EBR = cx.sb("EBR", [64, 6, 64], F32)
        ones = cx.sb("ones_a", [128, 64], BF16)
        pex = Ring([cx.sb("pex%d" % i, [128, 128], F32) for i in range(4)])
        PTr = Ring([cx.sb("PT%d" % i, [128, 2, 128], BF16) for i in range(4)])
        b_ev = small_loads(cx, [(EAB[:], bAB_d.ap()), (EAL[:], bAL_d.ap()), (EBR[:], bBR_d.ap())])
        cx.act.wait(b_ev)
        nc.scalar.activation(out=EAB[:], in_=EAB[:], func=AF.Exp)
        nc.scalar.activation(out=EAL[:], in_=EAL[:], func=AF.Exp)
        eb_ev = cx.act.tick(nc.scalar.activation(out=EBR[:], in_=EBR[:], func=AF.Exp))
        on_ev = cx.dve.tick(nc.vector.memset(ones[:], 1.0))
        cx.dve.wait(eb_ev)
        cx.pe.wait(on_ev, vh_ev, kh_ev, fm_evs, v_evs)
        sbanks = [Ring([banks.bufs[2], banks.bufs[3], banks.bufs[4]]), Ring([banks.bufs[5], cx.bank(6), cx.bank(7)])]
        obank = Ring([banks.bufs[0], banks.bufs[1]])
        acc_last = None
        da = dbg_att or {}
        for g in da.get("groups", range(3)):
            d = GD[g]
            nt = 16 // d
            L = T // d
            for r in range(d):
                for j in range(nt + 1):
                    if j == 0:
                        nq, m0, qq0 = 64, 0, 64
                    elif j == nt:
                        nq, m0, qq0 = 64, L - 64, 0
                    else:
                        nq, m0, qq0 = 128, 128 * j - 64, 0
                    q0 = r + d * m0
                    tl = []
                    for ti in range(2):
                        t = j - 1 + ti
                        if t < 0:
                            tl.append((64, 0, lambda hh: EAL[:, 2 * g + hh, :],
                                       lambda hh: VhL[:, GTOFF[g] + r, hh * 64:(hh + 1) * 64]))
                        elif t >= nt:
                            tl.append((64, 64 + L, lambda hh: EBR[:, 2 * g + hh, :],
                                       lambda hh: VhR[:, GTOFF[g] + r, hh * 64:(hh + 1) * 64]))
                        else:
                            tl.append((128, 64 + 128 * t, (lambda hh, ti=ti: EAB[:, 2 * g + hh, ti, qq0:qq0 + nq]),
                                       (lambda hh, t=t: V[g][:, r * nt + t, hh * 64:(hh + 1) * 64])))
                    ko, ob = obank.acquire(cx.pe)
                    pts = []
                    for hh in range(2):
                        hp_ = slice(hh * 64, (hh + 1) * 64)
                        kp, pt = PTr.acquire(cx.dve)
                        for ti, (nk, kc0, ebf, vf) in enumerate(tl):
                            ks, sb_ = sbanks[hh].acquire(cx.pe)
                            mm_ev = cx.pe.tick(nc.tensor.matmul(sb_[0:nk, 0:nq], KT[g][hp_, r, kc0:kc0 + nk],
                                                                QT[g][hp_, r, m0:m0 + nq], start=True, stop=True))
                            kx, px = pex.acquire(cx.act)
                            cx.act.wait(mm_ev)
                            x_ev = cx.act.tick(nc.scalar.activation(out=px[0:nk, 0:nq], in_=sb_[0:nk, 0:nq], func=AF.Exp, scale=0.125))
                            sbanks[hh].release(ks, x_ev)
                            cx.dve.wait(x_ev)
                            p_ev = cx.dve.tick(nc.vector.tensor_tensor(out=pt[0:nk, ti, 0:nq], in0=px[0:nk, 0:nq], in1=ebf(hh), op=ALU.mult))
                            pex.release(kx, p_ev)
                        pts.append((kp, pt, p_ev))
                    if da.get("nopv"):
                        for hh in range(2):
                            PTr.release(pts[hh][0], pts[hh][2])
                        obank.release(ko, pts[1][2])
                        acc_last = pts[1][2]
                        continue
                    for hh in range(2):
                        kp, pt, p_ev = pts[hh]
                        cx.pe.wait(p_ev)
                        for ti, (nk, kc0, ebf, vf) in enumerate(tl):
                            mm = nc.tensor.matmul(ob[hh * 64:(hh + 1) * 64, 0:nq], vf(hh), pt[0:nk, ti, 0:nq], start=(ti == 0), stop=(ti == 1))
                    for hh in range(2):
                        kp, pt, p_ev = pts[hh]
                        for ti, (nk, kc0, ebf, vf) in enumerate(tl):
                            mm = nc.tensor.matmul(ob[hh * 64:(hh + 1) * 64, 128:128 + nq], ones[0:nk, :], pt[0:nk, ti, 0:nq], start=(ti == 0), stop=(ti == 1))
                        o_mm = cx.pe.tick(mm)
                        PTr.release(kp, o_mm)
                    cx.dve.wait(o_mm, acc_last)
                    src = ob[:, 0:256].rearrange("p (a n) -> p a n", n=128)[:, :, 0:nq]
                    dst = acc[:, :, sts(q0, nq, d)]
                    if g == 0:
                        a_ev = cx.dve.tick(nc.vector.tensor_copy(out=dst, in_=src))
                    else:
                        a_ev = cx.dve.tick(nc.vector.tensor_tensor(out=dst, in0=src, in1=dst, op=ALU.add))
                    acc_last = a_ev
                    obank.release(ko, a_ev)
                    run_pool_ops(1)
        cx.dve.wait(acc_last)
        e = cx.dve.tick(nc.vector.reciprocal(out=acc[:, 1, :], in_=acc[:, 1, :]))
        cx.dve.wait(e)
        yc_ev = cx.dve.tick(nc.vector.tensor_tensor(out=yT[:, 5, :], in0=acc[:, 0, :], in1=acc[:, 1, :], op=ALU.mult))
        run_pool_ops(len(pool_ops))
        pool_ev = (pq.sem, pq.n, pq.key)
        cx.pe.wait(acc_last)
        for ch in range(2):
            for b in range(NBLK):
                kb, bank = banks.acquire(cx.pe)
                cx.pe.wait(pool_ev, wvA_ev)
                mm_ev = cx.pe.tick(nc.tensor.matmul(bank[:, :], wpbd[:, ch, :], pooled[:, ch, b * BLK:(b + 1) * BLK], start=True, stop=True))
                cx.dve.wait(mm_ev)
                e = cx.dve.tick(nc.vector.tensor_scalar(out=yT[:, 3 + ch, b * BLK:(b + 1) * BLK], in0=bank[:, :], scalar1=pbc[:, ch:ch + 1],
                                                        scalar2=psc[:, ch:ch + 1], op0=ALU.add, op1=ALU.mult))
                banks.release(kb, e)

    xT_res = None
    if inner is not None:
        inner.__exit__(None, None, None)
        xT_res = cx.sb("xT_res", [128, KC, T], F32)
    with scope(cx):
        if xT_res is not None:
            bout = cx.sb("bout3", [128, KC], F32)
            b3_ev = small_loads(cx, [(bout[:], bout_d.ap())])
            cx.dve.wait(b3_ev)
        wring = Ring([cx.sb("wo%d" % i, [128, 6, 128], BF16) for i in range(2)])
        wslots = [cx.dslot("wo%d" % i, sw=True) for i in range(2)]
        xr = Ring([cx.sb("xo%d" % i, [128, BLK], F32) for i in range(6)])
        xsl = [cx.dslot("xo%d" % i) for i in range(6)]
        os_ = cx.dslot("o")
        out_dv = out_d.ap().rearrange("(c p) t -> p c t", p=128)
        st = []

        def evac_o(j, b, bank, mm_ev):
            sl = slice(b * BLK, (b + 1) * BLK)
            kx, xb = xr.acquire(cx.act)
            x_ev = xsl[kx].start(cx.act, xb[:], xT_dv[:, j, sl])
            cx.dve.wait(mm_ev, x_ev)
            if xT_res is not None:
                e = cx.dve.tick(nc.vector.scalar_tensor_tensor(out=xT_res[:, j, sl], in0=bank[:, :], scalar=bout[:, j:j + 1], in1=xb[:], op0=ALU.add, op1=ALU.add))
                xr.release(kx, e)
                return e
            e = cx.dve.tick(nc.vector.scalar_tensor_tensor(out=xb[:], in0=bank[:, :], scalar=bout[:, j:j + 1], in1=xb[:], op0=ALU.add, op1=ALU.add))
            cx.sp.wait(e)
            o_ev = os_.start(cx.sp, out_dv[:, j, sl], xb[:])
            xr.release(kx, o_ev)
            st.append(o_ev)
            return e

        proj_fm(cx, lambda j: wout_d.ap()[j], KC, 6, lambda c, b: yT[:, c, b * BLK:(b + 1) * BLK], evac_o, wring, wslots, banks, None)
        if st:
            cx.sp.wait(st[-1])
    return xT_res


def _t5_bucket(rel):
    nb = 16
    ret = (rel > 0).astype(np.int32) * nb
    n = np.abs(rel)
    max_exact = nb // 2
    large = max_exact + (np.log(np.maximum(n, 1) / max_exact) / math.log(1024 / max_exact) * (nb - max_exact)).astype(np.int32)
    large = np.minimum(large, nb - 1)
    return ret + np.where(n < max_exact, n, large)


def attn_bias(rel_table):
    kk = np.arange(128)[:, None]
    qq = np.arange(128)[None, :]
    bAB = np.full((128, 6, 2, 128), MASKV, np.float32)
    for g, d in enumerate(GD):
        for ti, delta in enumerate((kk - qq - 64, kk - qq + 64)):
            bk = _t5_bucket(delta * d)
            ok = np.abs(delta) <= 64
            for hh in range(2):
                vals = rel_table[bk, 2 * g + hh]
                bAB[:, 2 * g + hh, ti, :] = np.where(ok, vals, MASKV)
    bAL = np.ascontiguousarray(bAB[64:128, :, 0, 64:128])
    bBR = np.ascontiguousarray(bAB[0:64, :, 1, 0:64])
    return bAB, bAL, bBR


def pool_icnt(s):
    out = np.zeros((128, 2, 16), np.float32)
    a = s * T
    for ch in range(2):
        for gh in range(2):
            w = (2, 4, 8, 16)[2 * ch + gh]
            for k in range(16):
                pos = a + (k if k < 8 else T - 16 + k)
                lo = min(max(pos - w // 2, 0), SEQ)
                hi = min(max(pos + w // 2, 0), SEQ)
                out[gh * 64:(gh + 1) * 64, ch, k] = 1.0 / (hi - lo)
    return out


def prep_M1(inp, l):
    w_in = inp["w_in"][l]
    b_in = inp["b_in"][l]

    def tm(w):
        return np.ascontiguousarray(w.reshape(KC, 128, -1).transpose(1, 0, 2))
    ws = inp["gmlp_w_s"][l]
    bs = inp["gmlp_b_s"][l]
    bsT = np.zeros((128, 3, 128), np.float32)
    for hp in range(3):
        for hh in range(2):
            bsT[hh * 64:(hh + 1) * 64, hp, :] = bs[2 * hp + hh][None, :]
    wpbd = np.zeros((128, 2, 128), np.float32)
    for ch in range(2):
        for gh in range(2):
            wpbd[gh * 64:(gh + 1) * 64, ch, gh * 64:(gh + 1) * 64] = inp["pool_w"][l][2 * ch + gh]
    bAB, bAL, bBR = attn_bias(inp["rel_table"])
    shared = {
        "g1": colvec(inp["norm_mix_g"][l]), "win": wchunks(w_in), "wvC": tm(w_in[:, 1792:2176]), "bin": colvec(b_in),
        "bvC": np.ascontiguousarray(b_in[1792:2176]),
    }
    extra = {
        "wvA": tm(w_in[:, 384:768]), "bvA": np.ascontiguousarray(b_in[384:768]),
        "vgain": np.ascontiguousarray(inp["gmlp_v_g"][l].reshape(384)),
        "wsT": np.ascontiguousarray(ws.transpose(2, 0, 1)), "bsT": bsT, "wpbd": wpbd,
        "pb": colvec(inp["pool_b"][l].reshape(256)), "psc": colvec(inp["pool_scale"][l]),
        "wout": wchunks(inp["w_out"][l]), "bout": colvec(inp["b_out"][l]), "bAB": bAB,
    }
    percore = []
    for cid in range(8):
        s = cid % 2
        percore.append({
            "icnt": pool_icnt(s),
            "bAL": bAL if s == 1 else np.full_like(bAL, MASKV),
            "bBR": bBR if s == 0 else np.full_like(bBR, MASKV),
        })
    return shared, extra, percore


NCORES = 8
F_KEYS = ["g3", "gf", "wup", "bup", "cw", "cb", "wd", "bd"]
M2_KEYS = ["gm", "g2", "wq", "wk", "wv", "wo", "bo"]
M1_KEYS = ["g1", "win", "wvC", "bin", "bvC", "wvA", "bvA", "vgain", "wsT", "bsT", "wpbd", "pb", "psc", "wout", "bout"]
CORE_KEYS = ["icnt", "bAL", "bBR", "fl"]
PAIRS = [[0, 1], [2, 3], [4, 5], [6, 7]]
KVW = 2 * NH + 21 * 128


def emit_F(cx, io, final):
    with scope(cx):
        _emit_F(cx, io, final)


def emit_M2(cx, io):
    with scope(cx):
        _emit_M2(cx, io)


def emit_M1(cx, io, p_only):
    with scope(cx):
        _emit_M1(cx, io, p_only)


def run_cc(cx, pairs, after_ev):
    nc = cx.nc
    cx.pool.wait(after_ev)
    evs = []
    for (snd, rcv) in pairs:
        sem = cx.sem("cc")
        nc.gpsimd.collective_compute("AllGather", ALU.bypass, replica_groups=PAIRS, ins=[snd.ap().opt()],
                                     outs=[rcv.ap().opt()]).then_inc(sem)
        evs.append((sem, 1, new_key()))
    return evs


def build_fused(shapes):
    nc = bass.Bass("TRN2", target_bir_lowering=False)
    cx = Cx(nc)
    dr = {name: cx.dram_in(name, list(shp)) for name, shp in shapes.items()}
    out_d = cx.dram_out("out", [D, T])

    def scratch(name, shape):
        return nc.dram_tensor(name, list(shape), F32)
    XA = scratch("XA", [D, T])
    XB = scratch("XB", [D, T])
    X1 = scratch("X1", [D, T])
    nkv = 128 * KVW // 2 // 16
    for l in range(2):
        KVs = scratch("KVs%d" % l, [16, nkv])
        KVr = scratch("KVr%d" % l, [32, nkv])
        ZBs = scratch("ZBs%d" % l, [16, 256])
        ZBr = scratch("ZBr%d" % l, [32, 256])
        XHs = scratch("XHs%d" % l, [16, 128])
        XHr = scratch("XHr%d" % l, [32, 128])
        KVs_bf = KVs.ap().bitcast(BF16).rearrange("a (b c) -> (a b) c", b=8)
        KVr_bf = KVr.ap().bitcast(BF16).rearrange("(r a) (b c) -> r (a b) c", r=2, b=8)
        ZBs_v = ZBs.ap().rearrange("a (b c) -> (a b) c", b=8).rearrange("p (h k) -> p h k", k=16)
        ZBr_v = ZBr.ap().rearrange("(r a) (b c) -> r (a b) c", r=2, b=8).rearrange("r p (h k) -> r p h k", k=16)
        XHs_v = XHs.ap().rearrange("a (b c) -> (a b) c", b=8).rearrange("p (c k) -> p c k", k=2)
        XHr_v = XHr.ap().rearrange("(r a) (b c) -> r (a b) c", r=2, b=8).rearrange("r p (c k) -> r p c k", k=2)
        Xin = dr["x"] if l == 0 else X1
        Xout = out_d if l == 1 else X1

        def wts(keys):
            return {k: dr["%s_%d" % (k, l)] for k in keys}

        def expK(side, g, KVs_bf=KVs_bf):
            return KVs_bf[:, side * NH + GOFF[g]:side * NH + GOFF[g] + 64 * GD[g]].rearrange("p (r m) -> p r m", m=64)

        def expV(side, g, KVs_bf=KVs_bf):
            c0 = 2 * NH + GTOFF[g] * 128
            return KVs_bf[side * 64:(side + 1) * 64, c0:c0 + GD[g] * 128].rearrange("p (t n) -> p t n", n=128)

        def expZ(side, ZBs_v=ZBs_v):
            return ZBs_v[:, :, side * 8:(side + 1) * 8]

        io = wts(M1_KEYS)
        io.update({k: dr[k] for k in CORE_KEYS})
        io.update({"xT": Xin, "out": XA, "bAB": dr["bAB"],
                   "exp": {"K": expK, "V": expV, "zb": expZ,
                           "run": (lambda cx_, ev, a=KVs, b=KVr, c=ZBs, d=ZBr: run_cc(cx_, [(a, b), (c, d)], ev))},
                   "KThL": View(KVr_bf[0, :, NH:2 * NH]), "KThR": View(KVr_bf[1, :, 0:NH]),
                   "VhL": View(KVr_bf[0, 64:128, 2 * NH:KVW].rearrange("p (t n) -> p t n", n=128)),
                   "VhR": View(KVr_bf[1, 0:64, 2 * NH:KVW].rearrange("p (t n) -> p t n", n=128)),
                   "zbhL": View(ZBr_v[0, :, :, 8:16]), "zbhR": View(ZBr_v[1, :, :, 0:8])})
        io1 = io
        with scope(cx):
            hbuf = cx.sb("hbuf", [128, KC, T], BF16)
            xT_res = _emit_M1(cx, io1, False, hbuf=hbuf, xres=True)
            io = wts(M2_KEYS)
            io.update({"xT": XA, "out": XB, "memT": dr["memT"]})
            with scope(cx):
                _emit_M2(cx, io, xT_res=xT_res, store=False, load=False, hbuf=hbuf)
            with scope(cx):
                ds = cx.dslot("xh", sw=True)
                with nc.allow_non_contiguous_dma(reason="boundary columns"):
                    ds.start(cx.pool, XHs_v[:, :, 0:1], xT_res[:, :, 0:1])
                    e = ds.start(cx.pool, XHs_v[:, :, 1:2], xT_res[:, :, T - 1:T])
                cc = run_cc(cx, [(XHs, XHr)], e)
                for q in (cx.pool, cx.sp):
                    q.wait(cc)
            io = wts(F_KEYS)
            io.update({"xT": XB, "out": Xout, "fl": dr["fl"], "xhl": View(XHr_v[0, :, :, 1:2]), "xhr": View(XHr_v[1, :, :, 0:1])})
            with scope(cx):
                _emit_F(cx, io, l == 1, xT_res=xT_res, hbuf=hbuf)
    print('semaphores used:', cx._nsem)
    cx.close()
    return nc


_FUSED = {}


def kernel(**inp):
    inp = {k: np.asarray(v) for k, v in inp.items()}
    x = inp["x"].astype(np.float32, copy=False)
    shared = {}
    for l in range(2):
        sh, ex, pc = prep_M1(inp, l)
        d = dict(sh)
        d.update(ex)
        d.update(prep_M2(inp, l))
        d.update(prep_F(inp, l))
        bAB = d.pop("bAB")
        for k, v in d.items():
            shared["%s_%d" % (k, l)] = np.ascontiguousarray(v, dtype=np.float32)
    shared["bAB"] = bAB
    fls = core_flags()
    maps = []
    for c in range(NCORES):
        m = dict(shared)
        for k in ("icnt", "bAL", "bBR"):
            m[k] = pc[c][k]
        m["fl"] = fls[c]
        m["x"] = np.ascontiguousarray(x[c // 2, (c % 2) * T:(c % 2 + 1) * T, :].T)
        m["memT"] = np.ascontiguousarray(inp["mem"][c // 2].T)
        maps.append(m)
    if "nc" not in _FUSED:
        _FUSED["nc"] = build_fused({k: v.shape for k, v in maps[0].items()})
    res = run_bass_kernel_spmd(_FUSED["nc"], maps, core_ids=list(range(NCORES)))
    out = np.empty((BATCH, SEQ, D), np.float32)
    for c in range(NCORES):
        out[c // 2, (c % 2) * T:(c % 2 + 1) * T, :] = res.results[c]["out"].T
    return out
```
